# Optimizing a Trainium2 kernel written in Bass

```python
import math
import jax, jax.numpy as jnp
from jax import lax
import numpy as np

D_MODEL = 4096
BATCH = 2
SEQ = 8192
DEPTH = 1

HEAD_DIM = 128
N_DIFF_HEADS = D_MODEL // 512
ATTN_QK = N_DIFF_HEADS * 2 * HEAD_DIM
ATTN_V = N_DIFF_HEADS * 2 * HEAD_DIM
ROT_DIM = HEAD_DIM // 4
ROPE_THETA = 500000.0
Q_BLOCK = 128
FOURIER_WIDTH = D_MODEL // 2
FOURIER_GROUP = 256
N_FOURIER_GROUPS = FOURIER_WIDTH // FOURIER_GROUP
IN_COLS = 2 * ATTN_QK + ATTN_V + FOURIER_WIDTH
D_FF = ((8 * D_MODEL // 3 + 255) // 256) * 256
CONV_WIDTH = 3
ALPHA = (2.0 * DEPTH) ** 0.25
BETA = (8.0 * DEPTH) ** -0.25
LN_EPS = 1e-5

kernel_name = 'hybrid_diffattn_fnet_convffn_deepnorm'


def layer_norm(x, g, b):
    xf = x.astype(jnp.float32)
    mu = jnp.mean(xf, axis=-1, keepdims=True)
    var = jnp.mean(jnp.square(xf - mu), axis=-1, keepdims=True)
    return ((xf - mu) * lax.rsqrt(var + LN_EPS) * g.astype(jnp.float32) + b.astype(jnp.float32)).astype(x.dtype)


def rms_norm(x, g):
    xf = x.astype(jnp.float32)
    ms = jnp.mean(jnp.square(xf), axis=-1, keepdims=True)
    return (xf * lax.rsqrt(ms + LN_EPS) * g.astype(jnp.float32)).astype(x.dtype)


def partial_rotary(t):
    s = t.shape[1]
    inv_freq = ROPE_THETA ** (-jnp.arange(0, ROT_DIM, 2, dtype=jnp.float32) / ROT_DIM)
    ang = jnp.arange(s, dtype=jnp.float32)[:, None] * inv_freq[None, :]
    cos = jnp.cos(ang).astype(t.dtype)[None, :, None, None, :]
    sin = jnp.sin(ang).astype(t.dtype)[None, :, None, None, :]
    half = ROT_DIM // 2
    x1 = t[..., :half]
    x2 = t[..., half:ROT_DIM]
    return jnp.concatenate([x1 * cos - x2 * sin, x2 * cos + x1 * sin, t[..., ROT_DIM:]], axis=-1)


def diff_attention(u_q, u_k, u_v, lq1, lk1, lq2, lk2, subln_g, lambda_init):
    b, s, _ = u_q.shape
    q = partial_rotary(u_q.reshape(b, s, N_DIFF_HEADS, 2, HEAD_DIM))
    k = partial_rotary(u_k.reshape(b, s, N_DIFF_HEADS, 2, HEAD_DIM))
    q = q.transpose(0, 2, 3, 1, 4) * (HEAD_DIM ** -0.5)
    k = k.transpose(0, 2, 3, 1, 4)
    v = u_v.reshape(b, s, N_DIFF_HEADS, 2 * HEAD_DIM).transpose(0, 2, 1, 3)
    lam = (jnp.exp(jnp.sum(lq1.astype(jnp.float32) * lk1.astype(jnp.float32)))
           - jnp.exp(jnp.sum(lq2.astype(jnp.float32) * lk2.astype(jnp.float32)))
           + lambda_init)

    def query_block(i):
        qb = lax.dynamic_slice_in_dim(q, i * Q_BLOCK, Q_BLOCK, axis=3)
        sc = jnp.einsum('bhcqd,bhckd->bhcqk', qb, k, preferred_element_type=jnp.float32)
        p = jax.nn.softmax(sc, axis=-1)
        a = (p[:, :, 0] - lam * p[:, :, 1]).astype(v.dtype)
        return jnp.einsum('bhqk,bhkv->bhqv', a, v)

    o = lax.map(query_block, jnp.arange(s // Q_BLOCK))
    o = o.transpose(1, 0, 3, 2, 4).reshape(b, s, N_DIFF_HEADS, 2 * HEAD_DIM)
    o = rms_norm(o, subln_g) * (1.0 - lambda_init)
    return o.reshape(b, s, ATTN_V)


def fourier_mix(u_f):
    b, s, _ = u_f.shape
    ug = u_f.astype(jnp.float32).reshape(b, s, N_FOURIER_GROUPS, FOURIER_GROUP)
    y = jnp.fft.fft2(ug, axes=(1, 3), norm='ortho').real
    return y.reshape(b, s, FOURIER_WIDTH).astype(u_f.dtype)


def hybrid_mixer(h, w_in, lq1, lk1, lq2, lk2, subln_g, w_attn_o, w_fourier, w_gate, b_gate, w_mix_out, lambda_init):
    u = h @ w_in
    u_q, u_k, u_v, u_f = jnp.split(u, [ATTN_QK, 2 * ATTN_QK, 2 * ATTN_QK + ATTN_V], axis=-1)
    y_attn = diff_attention(u_q, u_k, u_v, lq1, lk1, lq2, lk2, subln_g, lambda_init) @ w_attn_o
    y_four = fourier_mix(u_f) @ w_fourier
    g = jax.nn.sigmoid(h @ w_gate + b_gate)
    g_attn, g_four = jnp.split(g, 2, axis=-1)
    return (g_attn * y_attn + g_four * y_four) @ w_mix_out


def conv_ffn(h, w_up, conv_w, conv_b, w_down):
    a = h @ w_up
    ap = jnp.pad(a, ((0, 0), (1, 1), (0, 0)))
    c = ap[:, :-2] * conv_w[0] + ap[:, 1:-1] * conv_w[1] + ap[:, 2:] * conv_w[2] + conv_b
    gate, val = jnp.split(c, 2, axis=-1)
    return (jax.nn.silu(gate) * val) @ w_down


def setup_inputs(seed: int = 0) -> dict:
    key = jax.random.key(seed)
    ks = iter(jax.random.split(key, 32))
    L, D = DEPTH, D_MODEL

    def nrm(shape, scale):
        return jax.random.normal(next(ks), shape, jnp.float32) * scale

    x = nrm((BATCH, SEQ, D), 1.0)
    ln_emb_g = 1.0 + nrm((D,), 0.01)
    ln_emb_b = nrm((D,), 0.01)
    w_in = jnp.concatenate([
        nrm((L, D, ATTN_QK), D ** -0.5),
        nrm((L, D, ATTN_QK), D ** -0.5),
        nrm((L, D, ATTN_V), D ** -0.5 * BETA),
        nrm((L, D, FOURIER_WIDTH), D ** -0.5),
    ], axis=-1)
    lambda_q1 = nrm((L, HEAD_DIM), 0.1)
    lambda_k1 = nrm((L, HEAD_DIM), 0.1)
    lambda_q2 = nrm((L, HEAD_DIM), 0.1)
    lambda_k2 = nrm((L, HEAD_DIM), 0.1)
    subln_g = 1.0 + nrm((L, 2 * HEAD_DIM), 0.01)
    w_attn_o = nrm((L, ATTN_V, D), ATTN_V ** -0.5 * BETA)
    w_fourier = nrm((L, FOURIER_WIDTH, D), FOURIER_WIDTH ** -0.5 * BETA)
    w_gate = nrm((L, D, 2 * D), D ** -0.5)
    b_gate = nrm((L, 2 * D), 0.01)
    w_mix_out = nrm((L, D, D), D ** -0.5 * BETA)
    ln1_g = 1.0 + nrm((L, D), 0.01)
    ln1_b = nrm((L, D), 0.01)
    w_up = nrm((L, D, 2 * D_FF), D ** -0.5 * BETA)
    conv_w = nrm((L, CONV_WIDTH, 2 * D_FF), CONV_WIDTH ** -0.5)
    conv_b = nrm((L, 2 * D_FF), 0.01)
    w_down = nrm((L, D_FF, D), D_FF ** -0.5 * BETA)
    ln2_g = 1.0 + nrm((L, D), 0.01)
    ln2_b = nrm((L, D), 0.01)
    return {'x': x, 'ln_emb_g': ln_emb_g, 'ln_emb_b': ln_emb_b, 'w_in': w_in,
            'lambda_q1': lambda_q1, 'lambda_k1': lambda_k1, 'lambda_q2': lambda_q2, 'lambda_k2': lambda_k2,
            'subln_g': subln_g, 'w_attn_o': w_attn_o, 'w_fourier': w_fourier,
            'w_gate': w_gate, 'b_gate': b_gate, 'w_mix_out': w_mix_out,
            'ln1_g': ln1_g, 'ln1_b': ln1_b, 'w_up': w_up, 'conv_w': conv_w, 'conv_b': conv_b,
            'w_down': w_down, 'ln2_g': ln2_g, 'ln2_b': ln2_b}


def reference(x, ln_emb_g, ln_emb_b, w_in, lambda_q1, lambda_k1, lambda_q2, lambda_k2, subln_g,
              w_attn_o, w_fourier, w_gate, b_gate, w_mix_out, ln1_g, ln1_b,
              w_up, conv_w, conv_b, w_down, ln2_g, ln2_b):
    h = layer_norm(x, ln_emb_g, ln_emb_b)
    for l in range(DEPTH):
        lambda_init = 0.8 - 0.6 * math.exp(-0.3 * l)
        m = hybrid_mixer(h, w_in[l], lambda_q1[l], lambda_k1[l], lambda_q2[l], lambda_k2[l], subln_g[l],
                         w_attn_o[l], w_fourier[l], w_gate[l], b_gate[l], w_mix_out[l], lambda_init)
        h = layer_norm(ALPHA * h + m, ln1_g[l], ln1_b[l])
        f = conv_ffn(h, w_up[l], conv_w[l], conv_b[l], w_down[l])
        h = layer_norm(ALPHA * h + f, ln2_g[l], ln2_b[l])
    return h
```

```python
import math
import os
from contextlib import ExitStack, contextmanager

import numpy as np
import ml_dtypes

import concourse.bass as bass
import concourse.mybir as mybir
from concourse.bass_utils import run_bass_kernel_spmd

F32 = mybir.dt.float32
BF16 = mybir.dt.bfloat16
ALU = mybir.AluOpType
AF = mybir.ActivationFunctionType
AX = mybir.AxisListType

LN_EPS = 1e-5
ROPE_THETA = 500000.0
N_CORES = 8


class Cfg:
    def __init__(self, D=4096, S=8192, DFF=11008, upto=99, debug=()):
        self.D, self.S, self.DFF = D, S, DFF
        self.KC = D // 128
        self.NH = D // 512
        self.AQ = self.NH * 256
        self.TL = S // 4
        self.NLOC = self.TL + 2
        self.NTOK = S + self.NLOC
        self.FC = DFF // 128
        self.alpha = 2.0 ** 0.25
        self.lambda_init = 0.8 - 0.6 * math.exp(0.0)
        self.upto = upto
        self.debug = tuple(debug)


class Sem:
    def __init__(self, h, inc):
        self.h, self.inc, self.count = h, inc, 0


class Tok:
    __slots__ = ("sem", "val")

    def __init__(self, sem, val):
        self.sem, self.val = sem, val


ENGS = ("pe", "act", "dve", "pool", "sp")


def _flat(deps):
    out = []
    for d in deps:
        if d is None:
            continue
        if isinstance(d, (list, tuple)):
            out.extend(_flat(d))
        else:
            out.append(d)
    return out


class KB:
    def __init__(self, nc, es):
        self.nc, self.es = nc, es
        self.q = {e: [] for e in ENGS}
        self.waited = {e: {} for e in ENGS}
        self.esem = {}
        self.nsem = 0
        self.dsems = []
        self.bar = []
        self.last = {e: None for e in ENGS}
        self._new_esems()

    def _sem(self, name, inc):
        self.nsem += 1
        h = self.es.enter_context(self.nc.semaphore(f"{name}_{self.nsem}"))
        return Sem(h, inc)

    def _new_esems(self):
        for e in ("pe", "act", "dve", "pool"):
            self.esem[e] = self._sem("e" + e, 1)

    def dsem(self, name="d", barrier=True):
        s = self._sem(name, 16)
        if barrier:
            self.dsems.append(s)
        return s

    def _waits(self, eng, deps):
        waits = []
        w = self.waited[eng]
        best = {}
        for t in _flat(list(deps) + self.bar):
            if id(t.sem) not in best or best[id(t.sem)].val < t.val:
                best[id(t.sem)] = t
        for t in best.values():
            if w.get(id(t.sem), 0) >= t.val:
                continue
            w[id(t.sem)] = t.val
            waits.append(t)
        return waits

    def op(self, eng, fn, deps=(), sig=True):
        waits = self._waits(eng, deps)
        tok = None
        if sig:
            s = self.esem[eng]
            s.count += 1
            tok = Tok(s, s.count)
            self.last[eng] = tok
        self.q[eng].append((waits, fn, tok))
        return tok

    def dma(self, q, out, in_, deps=(), sem=None, slow=False):
        waits = self._waits(q, deps)
        sem.count += 16
        tok = Tok(sem, sem.count)
        if slow:
            fn = lambda e, o=out, i=in_: e.dma_start(out=o, in_=i, allow_slow_non_contiguous=True)
        else:
            fn = lambda e, o=out, i=in_: e.dma_start(out=o, in_=i)
        self.q[q].append((waits, fn, tok))
        return tok

    def wait_only(self, eng, deps):
        waits = self._waits(eng, deps)
        self.q[eng].append((waits, None, None))

    def barrier(self):
        toks = [t for t in self.last.values() if t is not None]
        toks += [Tok(s, s.count) for s in self.dsems if s.count > 0]
        self.bar = toks

    def flush(self):
        nc = self.nc

        def emit(eng, e):
            for waits, fn, tok in self.q[eng]:
                for w in waits:
                    e.wait_ge(w.sem.h, w.val)
                if fn is None:
                    continue
                ins = fn(e)
                if tok is not None:
                    ins.then_inc(tok.sem.h, tok.sem.inc)
            self.q[eng] = []

        with nc.Block() as blk:
            @blk.tensor
            def _(e):
                emit("pe", e)

            @blk.scalar
            def _(e):
                emit("act", e)

            @blk.vector
            def _(e):
                emit("dve", e)

            @blk.gpsimd
            def _(e):
                emit("pool", e)

            @blk.sync
            def _(e):
                emit("sp", e)

    @contextmanager
    def phase(self, name):
        ph = Phase(self, name)
        with ExitStack() as pes:
            ph.es = pes
            yield ph
            self.barrier()
            self.flush()


class Phase:
    def __init__(self, k, name):
        self.k, self.name, self.es, self.n = k, name, None, 0

    def sbuf(self, name, shape, dt):
        self.n += 1
        return self.es.enter_context(self.k.nc.sbuf_tensor(f"{self.name}_{name}_{self.n}", list(shape), dt))

    def psum(self, name, shape, dt):
        self.n += 1
        return self.es.enter_context(self.k.nc.psum_tensor(f"{self.name}_{name}_{self.n}", list(shape), dt))


class Ring:
    def __init__(self, k, bufs, name="r"):
        self.k, self.bufs = k, bufs
        self.n = len(bufs)
        self.sems = [k.dsem(name) for _ in bufs]
        self.free = [[] for _ in bufs]
        self.i = 0

    def take(self):
        s = self.i % self.n
        self.i += 1
        return s


def transpose_tile(k, cfg, hb, n, deps, ident, pst, pst_free, stg, col0, evac_eng_cycle):
    KC = cfg.KC
    toks = []
    ngrp = (KC + 7) // 8
    for g in range(ngrp):
        kcs = list(range(g * 8, min(KC, g * 8 + 8)))
        b = pst["i"] % len(pst["t"])
        pst["i"] += 1
        ps = pst["t"][b]
        tp = None
        for j, kc in enumerate(kcs):
            last = j == len(kcs) - 1
            tp = k.op("pe", lambda e, j=j, kc=kc, ps=ps: e.transpose(ps[:, j, :n], hb[:n, kc * 128:(kc + 1) * 128], ident[:n, :n]),
                      deps=(list(deps) + [pst_free[b]]) if j == 0 else (), sig=last)
        eng = evac_eng_cycle[g % len(evac_eng_cycle)]
        nk = len(kcs)
        if eng == "act":
            te = k.op("act", lambda e, ps=ps, k0=kcs[0], nk=nk: e.activation(stg[:, k0:k0 + nk, col0:col0 + n], ps[:, 0:nk, :n], AF.Copy),
                      deps=[tp])
        else:
            te = k.op("dve", lambda e, ps=ps, k0=kcs[0], nk=nk: e.tensor_copy(stg[:, k0:k0 + nk, col0:col0 + n], ps[:, 0:nk, :n]),
                      deps=[tp])
        pst_free[b] = te
        toks.append(te)
    return toks


def phase_ln(k, cfg, T, name, src, ntok, g_in, b_in, res=None, out_f=None, out_T=None, mask_in=None, group_break=None):
    D, KC = cfg.D, cfg.KC
    nch = D // 512
    with k.phase(name) as ph:
        g_rep = ph.sbuf("g", [128, D], F32)
        b_rep = ph.sbuf("b", [128, D], F32)
        xts = [ph.sbuf("xt", [128, D], F32) for _ in range(3 if res is None else 2)]
        stats = ph.sbuf("st", [128, 6 * nch], F32)
        mv = ph.sbuf("mv", [128, 2], F32)
        rstd = ph.sbuf("rs", [128, 1], F32)
        nmr = ph.sbuf("nm", [128, 1], F32)
        eps_t = ph.sbuf("eps", [128, 1], F32)
        t_eps = k.op("dve", lambda e: e.memset(eps_t[:], LN_EPS))
        cs = k.dsem("c")
        tc = [k.dma("sp", g_rep[:], g_in, sem=cs), k.dma("sp", b_rep[:], b_in, sem=cs)]
        if res is not None:
            rts = [ph.sbuf("rt", [128, D], F32) for _ in range(2)]
            rr = Ring(k, rts, "r")
        if out_T is not None:
            ident = ph.sbuf("id", [128, 128], BF16)
            tc.append(k.dma("sp", ident[:], T["ident"], sem=cs))
            hbs = [ph.sbuf("hb", [128, D], BF16) for _ in range(2)]
            hb_free = [[], []]
            nstg = 1 if res is not None else 2
            stgs = [ph.sbuf("stg", [128, KC, 512], BF16) for _ in range(nstg)]
            stg_free = [[] for _ in range(nstg)]
            sts = [k.dsem("st") for _ in range(nstg)]
            pst = {"t": [ph.psum("pt", [128, 8, 128], BF16) for _ in range(4)], "i": 0}
            pst_free = [None] * 4
            oT_v = out_T.rearrange("(kc p) s -> p kc s", p=128)
        if mask_in is not None:
            mask = ph.sbuf("mask", [128, 1], F32)
            tc.append(k.dma("sp", mask[:], mask_in, sem=cs))
        xr = Ring(k, xts, "x")
        fsems = [k.dsem("f") for _ in range(len(xts))]
        tiles = [(i * 128, 128) for i in range(ntok // 128)]
        if ntok % 128:
            tiles.append((ntok - ntok % 128, ntok % 128))
        gi = gcol = g0 = 0
        gtoks = []
        loads = {}

        def load(i):
            t0, n = tiles[i]
            s_ = xr.take()
            tl = [k.dma("sp", xts[s_][:n, :], src[t0:t0 + n, :], deps=xr.free[s_], sem=xr.sems[s_])]
            rs_ = None
            if res is not None:
                rs_ = rr.take()
                tl.append(k.dma("sp", rts[rs_][:n, :], res[t0:t0 + n, :], deps=rr.free[rs_], sem=rr.sems[rs_]))
            loads[i] = (s_, rs_, tl)

        R = len(xts)
        for i0 in range(min(R, len(tiles))):
            load(i0)
        sm2 = [(stats, mv, rstd, nmr),
               (ph.sbuf("st2", [128, 6 * nch], F32), ph.sbuf("mv2", [128, 2], F32), ph.sbuf("rs2", [128, 1], F32), ph.sbuf("nm2", [128, 1], F32))]
        stA = {}

        def stage_a(i):
            t0, n = tiles[i]
            s_, rs_, tl = loads.pop(i)
            xt = xts[s_]
            stats_, mv_, rstd_, nmr_ = sm2[i % 2]
            deps = [tl, t_eps, tc]
            if res is not None:
                rt = rts[rs_]
                t = k.op("dve", lambda e, xt=xt, rt=rt, n=n: e.scalar_tensor_tensor(xt[:n, :], rt[:n, :], cfg.alpha, xt[:n, :], ALU.mult, ALU.add),
                         deps=deps)
                rr.free[rs_] = [t]
                deps = [t]
            t = None
            for c in range(nch):
                t = k.op("dve", lambda e, c=c, xt=xt, n=n: e.bn_stats(stats_[:n, c * 6:(c + 1) * 6], xt[:n, c * 512:(c + 1) * 512]),
                         deps=deps if c == 0 else ())
            t = k.op("dve", lambda e, n=n: e.bn_aggr(mv_[:n, :], stats_[:n, :]), deps=[t])
            t = k.op("act", lambda e, n=n: e.activation(rstd_[:n, :], mv_[:n, 1:2], AF.Sqrt, bias=eps_t[:n, 0:1], scale=1.0), deps=[t])
            t = k.op("dve", lambda e, n=n: e.reciprocal(rstd_[:n, :], rstd_[:n, :]), deps=[t])
            t = k.op("dve", lambda e, n=n: e.tensor_scalar(nmr_[:n, :], mv_[:n, 0:1], rstd_[:n, 0:1], -1.0, ALU.mult, ALU.mult), deps=[t])
            t = k.op("act", lambda e, xt=xt, n=n: e.activation(xt[:n, :], xt[:n, :], AF.Identity, bias=nmr_[:n, 0:1], scale=rstd_[:n, 0:1]),
                     deps=[t])
            t = k.op("pool", lambda e, xt=xt, n=n: e.tensor_tensor(xt[:n, :], xt[:n, :], g_rep[:n, :], ALU.mult), deps=[t])
            stA[i] = (s_, xt, t)

        def stage_b(i):
            nonlocal gi, gcol, g0, gtoks
            t0, n = tiles[i]
            s_, xt, t = stA.pop(i)
            want_f = out_f is not None and t0 >= out_f[1]
            done = []
            if want_f or out_T is None:
                t_y = k.op("dve", lambda e, xt=xt, n=n: e.tensor_tensor(xt[:n, :], xt[:n, :], b_rep[:n, :], ALU.add), deps=[t])
                done.append(t_y)
            if want_f:
                r0 = t0 - out_f[1] + out_f[2]
                done.append(k.dma("pool", out_f[0][r0:r0 + n, :], xt[:n, :], deps=[t_y], sem=fsems[s_]))
            if out_T is not None:
                hb = hbs[i % 2]
                if want_f:
                    t_b = k.op("act", lambda e, xt=xt, hb=hb, n=n: e.activation(hb[:n, :], xt[:n, :], AF.Copy), deps=[t_y, hb_free[i % 2]])
                else:
                    t_b = k.op("dve", lambda e, xt=xt, hb=hb, n=n: e.tensor_tensor(hb[:n, :], xt[:n, :], b_rep[:n, :], ALU.add),
                               deps=[t, hb_free[i % 2]])
                if mask_in is not None and n < 128:
                    t_b = k.op("dve", lambda e, hb=hb, n=n: e.tensor_scalar(hb[:n, :], hb[:n, :], mask[:n, 0:1], None, ALU.mult), deps=[t_b])
                done.append(t_b)
                stg = stgs[gi % nstg]
                if gcol == 0:
                    g0 = t0
                    gtoks = []
                tt = transpose_tile(k, cfg, hb, n, [t_b] + (stg_free[gi % nstg] if gcol == 0 else []), ident, pst, pst_free,
                                    stg, gcol, ["act", "dve"])
                hb_free[i % 2] = tt
                gtoks += tt
                gcol += n
                nxt = tiles[i + 1] if i + 1 < len(tiles) else None
                if gcol == 512 or nxt is None or nxt[0] == group_break or nxt[1] != 128:
                    stg_free[gi % nstg] = [k.dma("pool", oT_v[:, :, g0:g0 + gcol], stg[:, :, 0:gcol], deps=gtoks, sem=sts[gi % nstg])]
                    gi += 1
                    gcol = 0
            xr.free[s_] = done

        stage_a(0)
        for i in range(len(tiles)):
            if i + 1 < len(tiles):
                stage_a(i + 1)
            stage_b(i)
            if i + R < len(tiles):
                load(i + R)


def mm_group(k, ps_ap, lhs_fn, rhs_fn, nk, deps):
    t = None
    for kc in range(nk):
        t = k.op("pe", lambda e, kc=kc: e.matmul(ps_ap, lhs_fn(kc), rhs_fn(kc), start=(kc == 0), stop=(kc == nk - 1)),
                 deps=deps if kc == 0 else (), sig=(kc == nk - 1))
    return t


def phase_proj(k, cfg, T, name, col0, ntok, TBN, jobs, rope_cos=None, rope_sin=None):
    D, KC, AQ = cfg.D, cfg.KC, cfg.AQ
    CW = min(512, AQ)
    NJ = CW // 128
    has_f = any(j["kind"] == "fmf" for j in jobs)
    has_rope = any(j["kind"] == "rope" for j in jobs)
    tiles = [(i * 512, min(512, ntok - i * 512)) for i in range((ntok + 511) // 512)]
    blocks = [tiles[i:i + TBN] for i in range(0, len(tiles), TBN)]
    BW = TBN * 512
    with k.phase(name) as ph:
        hTb = ph.sbuf("hTb", [128, KC, BW], BF16)
        wbufs = [ph.sbuf("w", [128, KC, CW], BF16) for _ in range(2)]
        wr = Ring(k, wbufs, "w")
        ksts = [ph.sbuf("kst", [128, 512], BF16) for _ in range(2)]
        kst_free = [[], []]
        kst_sems = [k.dsem("ks") for _ in range(2)]
        kst_i = 0
        banks = [ph.psum("ps", [128, 512], F32) for _ in range(6)]
        bank_free = [[] for _ in banks]
        bi = 0
        if has_rope:
            perm = ph.sbuf("perm", [128, 128], BF16)
            cs = k.dsem("c")
            t_perm = k.dma("sp", perm[:], T["perm32"], sem=cs)
            cosb = ph.sbuf("cos", [128, BW], F32)
            sinb = ph.sbuf("sin", [128, BW], F32)
            rsem = k.dsem("rt")
            t1f = [ph.sbuf("t1f", [128, 512], F32) for _ in range(2)]
            t2f = [ph.sbuf("t2f", [128, 512], F32) for _ in range(2)]
            tf_free = [[], []]
            rbanks = [ph.psum("pr", [128, 512], F32) for _ in range(2)]
            rbank_free = [[], []]
            ri = 0
        if has_f:
            ufT = ph.sbuf("ufT", [128, AQ // 128, BW], BF16)
            csg = ph.sbuf("csg", [128, 2, 512], BF16)
            cs2 = k.dsem("c2")
            t_csg = k.dma("sp", csg[:], T["csg"].rearrange("(cc p) n -> p cc n", p=128), sem=cs2)
        hsem = k.dsem("h")
        hT_v = T["hT"].rearrange("(kc p) s -> p kc s", p=128)
        blk_reads = []
        pending_tails = []
        tail_reads = []

        def run_tails():
            while pending_tails:
                pending_tails.pop(0)()

        def issue_block_loads(blk):
            b0 = blk[0][0]
            bn = sum(n for _, n in blk)
            t_h = k.dma("sp", hTb[:, :, 0:bn], hT_v[:, :, col0 + b0:col0 + b0 + bn], deps=blk_reads, sem=hsem)
            t_rt = []
            if has_rope:
                t_rt = [k.dma("sp", cosb[0:32, 0:bn], rope_cos[:, b0:b0 + bn], deps=blk_reads, sem=rsem),
                        k.dma("sp", sinb[0:32, 0:bn], rope_sin[:, b0:b0 + bn], deps=blk_reads, sem=rsem)]
            return t_h, t_rt

        pre = None
        for bidx, blk in enumerate(blocks):
            run_tails()
            blk_reads += tail_reads
            del tail_reads[:]
            b0 = blk[0][0]
            bn = sum(n for _, n in blk)
            if pre is None:
                t_h, t_rt = issue_block_loads(blk)
            else:
                t_h, t_rt = pre
                pre = None
            blk_reads = []
            uf_writes = []
            for job in jobs:
                w_v = job["w"].rearrange("(kc p) n -> p kc n", p=128)
                for cb in range(job["nblk"]):
                    c0 = job["c0"] + cb * CW
                    ws = wr.take()
                    wt = wbufs[ws]
                    t_w = k.dma("pool", wt[:], w_v[:, :, c0:c0 + CW], deps=wr.free[ws], sem=wr.sems[ws])
                    wreads = []
                    for (tt0, n) in blk:
                        lo = tt0 - b0
                        if job["kind"] in ("rope", "fmf"):
                            for j in range(NJ):
                                b = bi % len(banks); bi += 1
                                ps = banks[b]
                                tg = mm_group(k, ps[:, :n], lambda kc, j=j, wt=wt: wt[:, kc, j * 128:(j + 1) * 128],
                                              lambda kc, lo=lo, n=n: hTb[:, kc, lo:lo + n], KC, [t_w, t_h, bank_free[b]])
                                wreads.append(tg)
                                run_tails()
                                chunk = (c0 - job["c0"]) // 128 + j
                                if job["kind"] == "fmf":
                                    te = k.op("act", lambda e, ps=ps, chunk=chunk, lo=lo, n=n: e.activation(
                                        ufT[:, chunk, lo:lo + n], ps[:, :n], AF.Copy), deps=[tg, t_rt])
                                    bank_free[b] = [te]
                                    uf_writes.append(te)
                                else:
                                    si = kst_i % 2; kst_i += 1
                                    kst = ksts[si]
                                    ta = k.op("act", lambda e, ps=ps, kst=kst, n=n: e.activation(kst[:, :n], ps[:, :n], AF.Copy),
                                              deps=[tg, kst_free[si]])
                                    rb = ri % 2; ri += 1
                                    pr = rbanks[rb]
                                    f1, f2 = t1f[rb], t2f[rb]
                                    td1 = k.op("dve", lambda e, ps=ps, f1=f1, lo=lo, n=n: e.tensor_tensor(
                                        f1[0:32, :n], ps[0:32, :n], cosb[0:32, lo:lo + n], ALU.mult), deps=[tg, t_rt, tf_free[rb], ta])
                                    bank_free[b] = [ta, td1]
                                    blk_reads.append(td1)

                                    def tail(pr=pr, kst=kst, n=n, lo=lo, f1=f1, f2=f2, rb=rb, si=si, ta=ta, td1=td1, chunk=chunk, tt0=tt0, job=job):
                                        tr = k.op("pe", lambda e: e.matmul(pr[:, :n], perm[:, :], kst[:, :n], start=True, stop=True),
                                                  deps=[ta, t_perm, rbank_free[rb]])
                                        td2 = k.op("dve", lambda e: e.tensor_tensor(f2[0:32, :n], pr[0:32, :n], sinb[0:32, lo:lo + n], ALU.mult),
                                                   deps=[tr, t_rt])
                                        td3 = k.op("dve", lambda e: e.tensor_tensor(kst[0:32, :n], f1[0:32, :n], f2[0:32, :n], ALU.add),
                                                   deps=[td1, td2, ta])
                                        rbank_free[rb] = [td2]
                                        tf_free[rb] = [td3]
                                        tail_reads.append(td2)
                                        kst_free[si] = [k.dma("sp", job["out"][chunk, :, tt0:tt0 + n], kst[:, :n], deps=[td3], sem=kst_sems[si])]

                                    pending_tails.append(tail)
                        else:
                            nsub = (n + 127) // 128
                            for sub in range(nsub):
                                m = min(128, n - sub * 128)
                                b = bi % len(banks); bi += 1
                                ps = banks[b]
                                tg = mm_group(k, ps[:m, :CW], lambda kc, lo=lo, sub=sub, m=m: hTb[:, kc, lo + sub * 128:lo + sub * 128 + m],
                                              lambda kc, wt=wt: wt[:, kc, :], KC, [t_w, t_h, bank_free[b]])
                                wreads.append(tg)
                                run_tails()
                                si = kst_i % 2; kst_i += 1
                                kst = ksts[si]
                                ta = k.op("act", lambda e, ps=ps, kst=kst, m=m: e.activation(kst[:m, :CW], ps[:m, :CW], AF.Copy),
                                          deps=[tg, kst_free[si]])
                                bank_free[b] = [ta]
                                r0 = tt0 + sub * 128
                                tst = k.dma("sp", job["out"][r0:r0 + m, c0 - job["c0"]:c0 - job["c0"] + CW], kst[:m, :CW],
                                            deps=[ta], sem=kst_sems[si])
                                kst_free[si] = [tst]
                    wr.free[ws] = wreads
                    blk_reads += wreads
            run_tails()
            blk_reads += tail_reads
            del tail_reads[:]
            if has_f and bidx + 1 < len(blocks):
                pre = issue_block_loads(blocks[bidx + 1])
                blk_reads = []
            if has_f:
                NG = AQ // 256
                for (tt0, n) in blk:
                    lo = tt0 - b0
                    for sub in range((n + 127) // 128):
                        m = min(128, n - sub * 128)
                        for g in range(NG):
                            b = bi % len(banks); bi += 1
                            ps = banks[b]
                            tg = mm_group(k, ps[:m, :], lambda cc, g=g, lo=lo, sub=sub, m=m: ufT[:, 2 * g + cc, lo + sub * 128:lo + sub * 128 + m],
                                          lambda cc: csg[:, cc, :], 2, [uf_writes, t_csg, bank_free[b]])
                            blk_reads.append(tg)
                            si = kst_i % 2; kst_i += 1
                            kst = ksts[si]
                            ta = k.op("act", lambda e, ps=ps, kst=kst, m=m: e.activation(kst[:m, :], ps[:m, :], AF.Copy, scale=1.0 / 16.0),
                                      deps=[tg, kst_free[si]])
                            bank_free[b] = [ta]
                            sc = (tt0 + sub * 128) // 128
                            kv = kst[:m, :].rearrange("p (ri h c) -> p ri h c", ri=2, h=2)
                            tst = [k.dma("sp", T["z"][g, h, 0:m, sc, :].rearrange("p (ri c) -> p ri c", ri=2), kv[:, :, h, :],
                                         deps=[ta], sem=kst_sems[si]) for h in range(2)]
                            kst_free[si] = tst


def issue_pack(k, cfg, T, PK):
    D = cfg.D

    def pack(sem, w_ap, pk_ap, CWB):
        w_v = w_ap.rearrange("(kc p) n -> p kc n", p=128)
        ncol = w_ap.shape[1]
        for b in range(ncol // CWB):
            k.dma("pool", pk_ap[b], w_v[:, :, b * CWB:(b + 1) * CWB], sem=sem)

    sa = k.dsem("pkA", barrier=False)
    pack(sa, T["w_attn_o"], T["wao_pk"], 256)
    pack(sa, T["w_fourier"], T["wf_pk"], 256)
    pack(sa, T["w_gate"], T["wg_pk"], 256)
    PK["A"] = Tok(sa, sa.count)
    sb = k.dsem("pkB", barrier=False)
    pack(sb, T["w_mix"], T["wmix_pk"], min(512, D))
    PK["B"] = Tok(sb, sb.count)


def phase_attn(k, cfg, T, PK):
    S, NH, NLOC, TL = cfg.S, cfg.NH, cfg.NLOC, cfg.TL
    SC = S // 128
    NKB = S // 512
    NG8 = SC // 8
    qtiles = [(i * 128, 128) for i in range(TL // 128)] + [(TL, NLOC - TL)]
    sm_scale = 128.0 ** -0.5
    with k.phase("p3") as ph:
        KT = ph.sbuf("KT", [128, 2, S], BF16)
        V = ph.sbuf("V", [128, SC, 256], BF16)
        QT = ph.sbuf("QT", [128, 2, NLOC], BF16)
        E1 = [ph.sbuf("e1", [128, S], BF16) for _ in range(2)]
        E2 = [ph.sbuf("e2", [128, S], BF16) for _ in range(2)]
        aT = [ph.sbuf("aT", [128, SC, 128], BF16) for _ in range(2)]
        rs = [ph.sbuf("rs", [128, 2, NKB], F32) for _ in range(2)]
        lsum = [ph.sbuf("l", [128, 2], F32) for _ in range(2)]
        rl = [ph.sbuf("rl", [128, 2], F32) for _ in range(2)]
        coef = [ph.sbuf("cf", [128, 1], F32) for _ in range(2)]
        o_sb = [ph.sbuf("o", [128, 256], F32) for _ in range(2)]
        junk = ph.sbuf("junk", [128, 256], F32)
        ss = [ph.sbuf("ss", [128, 1], F32) for _ in range(2)]
        rstd = [ph.sbuf("rstd", [128, 1], F32) for _ in range(2)]
        onb = [ph.sbuf("onb", [128, 256], BF16) for _ in range(2)]
        onst = [ph.sbuf("onst", [128, 2, 128], BF16) for _ in range(2)]
        ident = ph.sbuf("id", [128, 128], BF16)
        g_rep = ph.sbuf("g", [128, 256], F32)
        lamr = ph.sbuf("lamr", [128, 4, 128], F32)
        lprod = ph.sbuf("lprod", [128, 2, 128], F32)
        lsm = ph.sbuf("lsm", [128, 2], F32)
        lex = ph.sbuf("lex", [128, 2], F32)
        neg_lam = ph.sbuf("nlam", [128, 1], F32)
        eps_t = ph.sbuf("eps", [128, 1], F32)
        sbanks = [ph.psum("s", [128, 512], F32) for _ in range(4)]
        sbank_free = [[] for _ in sbanks]
        tbanks = [ph.psum("t", [128, 8, 128], BF16) for _ in range(2)]
        tbank_free = [[], []]
        po = ph.psum("po", [128, 256], F32)
        po_free = []
        pon = ph.psum("pon", [128, 2, 128], BF16)
        pon_free = []
        cs = k.dsem("c3")
        tc = [k.dma("sp", ident[:], T["ident"], sem=cs), k.dma("sp", g_rep[:], T["subg_rep"], sem=cs),
              k.dma("sp", lamr[:], T["lam_rep"], sem=cs)]
        t_eps = k.op("dve", lambda e: e.memset(eps_t[:], LN_EPS))
        t = k.op("dve", lambda e: e.tensor_tensor(lprod[:, 0, :], lamr[:, 0, :], lamr[:, 1, :], ALU.mult), deps=tc)
        t = k.op("dve", lambda e: e.tensor_tensor(lprod[:, 1, :], lamr[:, 2, :], lamr[:, 3, :], ALU.mult), deps=[t])
        t = k.op("dve", lambda e: e.tensor_reduce(lsm[:, :], lprod[:, :, :], AX.X, ALU.add), deps=[t])
        t = k.op("act", lambda e: e.activation(lex[:, :], lsm[:, :], AF.Exp), deps=[t])
        t = k.op("dve", lambda e: e.tensor_tensor(neg_lam[:, :], lex[:, 1:2], lex[:, 0:1], ALU.subtract), deps=[t])
        t_lam = k.op("dve", lambda e: e.tensor_scalar(neg_lam[:, :], neg_lam[:, :], -cfg.lambda_init, None, ALU.add), deps=[t])
        issue_pack(k, cfg, T, PK)
        hsem = k.dsem("kv")
        osems = [k.dsem("on") for _ in range(2)]
        kT_d, qT_d = T["kT"], T["qT"]
        v_v = T["v"].rearrange("(sc p) c -> p sc c", p=128)
        onT_v = T["onT"].rearrange("(c p) s -> p c s", p=128)
        head_reads = []
        E_free = [[], []]
        aT_free = [[], []]
        small_free = [[], []]
        rs_free = [[], []]
        onst_free = [[], []]
        st = {"bi": 0, "ti": 0, "po_free": [], "pon_free": []}
        nsteps = 2 * NKB

        for hd in range(NH):
            tl = [k.dma("sp", KT[:, 0, :], kT_d[2 * hd, :, :], deps=head_reads, sem=hsem),
                  k.dma("sp", KT[:, 1, :], kT_d[2 * hd + 1, :, :], deps=head_reads, sem=hsem),
                  k.dma("sp", QT[:, 0, :], qT_d[2 * hd, :, :], deps=head_reads, sem=hsem),
                  k.dma("sp", QT[:, 1, :], qT_d[2 * hd + 1, :, :], deps=head_reads, sem=hsem),
                  k.dma("sp", V[:, :, :], v_v[:, :, hd * 256:(hd + 1) * 256], deps=head_reads, sem=hsem)]
            head_reads = []
            acts = {}

            def qk_steps(ti_, lo, hi, tl=tl, acts=acts):
                q0, nq = qtiles[ti_]
                sl = ti_ % 2
                for stp in range(lo, hi):
                    c, kb = stp // NKB, stp % NKB
                    ec = E1[sl] if c == 0 else E2[sl]
                    b = st["bi"] % 4
                    st["bi"] += 1
                    ps = sbanks[b]
                    tm = k.op("pe", lambda e, ps=ps, c=c, kb=kb, q0=q0, nq=nq: e.matmul(
                        ps[:nq, :], QT[:, c, q0:q0 + nq], KT[:, c, kb * 512:(kb + 1) * 512], start=True, stop=True),
                        deps=[tl, sbank_free[b]])
                    ta = k.op("act", lambda e, ps=ps, ec=ec, kb=kb, c=c, nq=nq, sl=sl: e.activation(
                        ec[:nq, kb * 512:(kb + 1) * 512], ps[:nq, :], AF.Exp, scale=sm_scale,
                        accum_out=rs[sl][:nq, c, kb:kb + 1]), deps=[tm, E_free[sl], rs_free[sl]])
                    sbank_free[b] = [ta]
                    acts.setdefault(ti_, []).append(ta)
                    head_reads.append(tm)

            combs = {}

            def combine(tj, acts=acts, combs=combs):
                q0, nq = qtiles[tj]
                sl = tj % 2
                e1, e2 = E1[sl], E2[sl]
                t = k.op("dve", lambda e: e.tensor_reduce(lsum[sl][:nq, :], rs[sl][:nq, :, :], AX.X, ALU.add),
                         deps=[acts[tj], small_free[sl]])
                rs_free[sl] = [t]
                t = k.op("dve", lambda e: e.reciprocal(rl[sl][:nq, :], lsum[sl][:nq, :]), deps=[t])
                t = k.op("dve", lambda e: e.tensor_tensor(coef[sl][:nq, :], lsum[sl][:nq, 0:1], rl[sl][:nq, 1:2], ALU.mult), deps=[t])
                t = k.op("dve", lambda e: e.tensor_tensor(coef[sl][:nq, :], coef[sl][:nq, :], neg_lam[:nq, :], ALU.mult), deps=[t, t_lam])
                combs[tj] = k.op("dve", lambda e: e.scalar_tensor_tensor(
                    e1[:nq, :], e2[:nq, :], coef[sl][:nq, 0:1], e1[:nq, :], ALU.mult, ALU.add), deps=[t])

            nqt = len(qtiles)
            half = nsteps // 2
            qk_steps(0, 0, nsteps)
            if nqt > 1:
                qk_steps(1, 0, half)
            for ti_ in range(nqt):
                q0, nq = qtiles[ti_]
                sl = ti_ % 2
                e1, e2 = E1[sl], E2[sl]
                nxt = ti_ + 1 < nqt
                nxt2 = ti_ + 2 < nqt
                if ti_ == 0:
                    combine(0)
                t_comb = combs[ti_]
                evs = []
                tp = None
                for g in range(NG8):
                    if nxt:
                        lo = half + (nsteps - half) * g // NG8
                        hi = half + (nsteps - half) * (g + 1) // NG8
                        qk_steps(ti_ + 1, lo, hi)
                    tb = st["ti"] % 2
                    st["ti"] += 1
                    pt = tbanks[tb]
                    for j in range(8):
                        sc = g * 8 + j
                        tp = k.op("pe", lambda e, pt=pt, j=j, sc=sc, e1=e1, nq=nq: e.transpose(
                            pt[:, j, :nq], e1[:nq, sc * 128:(sc + 1) * 128], ident[:nq, :nq]),
                            deps=[t_comb, tbank_free[tb], aT_free[sl]] if j == 0 else (), sig=(j == 7))
                    te = k.op("dve", lambda e, pt=pt, g=g, sl=sl, nq=nq: e.tensor_copy(aT[sl][:, g * 8:(g + 1) * 8, :nq], pt[:, :, :nq]),
                              deps=[tp])
                    tbank_free[tb] = [te]
                    evs.append(te)
                E_free[sl] = [tp]
                tpv = None
                for stp in range(half):
                    if nxt2:
                        qk_steps(ti_ + 2, stp, stp + 1)
                    for sc in range(SC * stp // half, SC * (stp + 1) // half):
                        tpv = k.op("pe", lambda e, sc=sc, sl=sl, nq=nq: e.matmul(po[:nq, :], aT[sl][:, sc, :nq], V[:, sc, :],
                                                                               start=(sc == 0), stop=(sc == SC - 1)),
                                   deps=[evs, st["po_free"]] if sc == 0 else (), sig=(sc == SC - 1))
                aT_free[sl] = [tpv]
                head_reads.append(tpv)
                if nxt:
                    combine(ti_ + 1)
                t_o = k.op("dve", lambda e, sl=sl, nq=nq: e.tensor_scalar(o_sb[sl][:nq, :], po[:nq, :], rl[sl][:nq, 0:1], None, ALU.mult),
                           deps=[tpv])
                st["po_free"] = [t_o]
                t = k.op("dve", lambda e, sl=sl, nq=nq: e.scalar_tensor_tensor(junk[:nq, :], o_sb[sl][:nq, :], 1.0, o_sb[sl][:nq, :],
                                                                               ALU.mult, ALU.mult, accum_out=ss[sl][:nq, 0:1]), deps=[t_o])
                t = k.op("act", lambda e, sl=sl, nq=nq: e.activation(rstd[sl][:nq, :], ss[sl][:nq, :], AF.Ln, bias=eps_t[:nq, 0:1],
                                                                     scale=1.0 / 256.0), deps=[t, t_eps])
                t = k.op("act", lambda e, sl=sl, nq=nq: e.activation(rstd[sl][:nq, :], rstd[sl][:nq, :], AF.Exp, scale=-0.5), deps=[t])
                t = k.op("dve", lambda e, sl=sl, nq=nq: e.tensor_scalar(rstd[sl][:nq, :], rstd[sl][:nq, :], 1.0 - cfg.lambda_init, None, ALU.mult),
                         deps=[t])
                t_on = k.op("dve", lambda e, sl=sl, nq=nq: e.scalar_tensor_tensor(
                    onb[sl][:nq, :], o_sb[sl][:nq, :], rstd[sl][:nq, 0:1], g_rep[:nq, :], ALU.mult, ALU.mult), deps=[t, tc])
                tp2 = None
                for h in range(2):
                    tp2 = k.op("pe", lambda e, h=h, sl=sl, nq=nq: e.transpose(pon[:, h, :nq], onb[sl][:nq, h * 128:(h + 1) * 128], ident[:nq, :nq]),
                               deps=[t_on, st["pon_free"]] if h == 0 else (), sig=(h == 1))
                te = k.op("dve", lambda e, sl=sl, nq=nq: e.tensor_copy(onst[sl][:, :, :nq], pon[:, :, :nq]), deps=[tp2, onst_free[sl]])
                st["pon_free"] = [te]
                small_free[sl] = [te]
                onst_free[sl] = [k.dma("sp", onT_v[:, 2 * hd:2 * hd + 2, q0:q0 + nq], onst[sl][:, :, :nq], deps=[te], sem=osems[sl])]


def phase_fourier(k, cfg, T):
    S, NH, NLOC, TL = cfg.S, cfg.NH, cfg.NLOC, cfg.TL
    SC = S // 128
    W = 256
    stiles = [(i * W, W) for i in range(TL // W)] + [(TL, NLOC - TL)]
    with k.phase("p3b") as ph:
        CTs = [ph.sbuf("CT", [128, SC, W], BF16) for _ in range(2)]
        STs = [ph.sbuf("ST", [128, SC, W], BF16) for _ in range(2)]
        zb = [ph.sbuf("zs", [128, SC, 256], BF16) for _ in range(2)]
        zr = Ring(k, zb, "z")
        ysts = [ph.sbuf("yst", [128, W], BF16) for _ in range(2)]
        yst_free = [[], []]
        ysems = [k.dsem("y") for _ in range(2)]
        banks = [ph.psum("ps", [128, 512], F32) for _ in range(4)]
        bank_free = [[] for _ in banks]
        tsems = [k.dsem("t") for _ in range(2)]
        tab_free = [[], []]
        bi = 0
        yi = 0
        yfT_v = T["yfT"].rearrange("(c p) s -> p c s", p=128)

        def load_tabs(sti):
            sl = sti % 2
            return [k.dma("sp", CTs[sl][:, :, :], T["dftc"][sti], deps=tab_free[sl], sem=tsems[sl]),
                    k.dma("sp", STs[sl][:, :, :], T["dfts"][sti], deps=tab_free[sl], sem=tsems[sl])]

        tabs = {0: load_tabs(0)}
        for sti, (w0, n) in enumerate(stiles):
            if sti + 1 < len(stiles):
                tabs[sti + 1] = load_tabs(sti + 1)
            tt = tabs.pop(sti)
            CT, ST = CTs[sti % 2], STs[sti % 2]
            tab_reads = []
            for g in range(NH):
                for h in range(2):
                    zs = zr.take()
                    Z = zb[zs]
                    tz = k.dma("sp", Z[:], T["z"][g, h], deps=zr.free[zs], sem=zr.sems[zs])
                    b = bi % 4
                    bi += 1
                    ps = banks[b]
                    t = None
                    for sc in range(SC):
                        k.op("pe", lambda e, ps=ps, Z=Z, sc=sc, n=n, CT=CT: e.matmul(ps[:, :n], Z[:, sc, 0:128], CT[:, sc, :n], start=(sc == 0), stop=False),
                             deps=[tz, tt, bank_free[b]] if sc == 0 else (), sig=False)
                        t = k.op("pe", lambda e, ps=ps, Z=Z, sc=sc, n=n, ST=ST: e.matmul(ps[:, :n], Z[:, sc, 128:256], ST[:, sc, :n], start=False,
                                                                                 stop=(sc == SC - 1)), sig=(sc == SC - 1))
                    zr.free[zs] = [t]
                    tab_reads.append(t)
                    ys = yi % 2
                    yi += 1
                    ta = k.op("act", lambda e, ps=ps, ys=ys, n=n: e.activation(ysts[ys][:, :n], ps[:, :n], AF.Copy, scale=float(S) ** -0.5),
                              deps=[t, yst_free[ys]])
                    bank_free[b] = [ta]
                    yst_free[ys] = [k.dma("pool", yfT_v[:, 2 * g + h, w0:w0 + n], ysts[ys][:, :n], deps=[ta], sem=ysems[ys])]
            tab_free[sti % 2] = tab_reads


def phase_gate(k, cfg, T, PK):
    D, KC, AQ, NLOC, S = cfg.D, cfg.KC, cfg.AQ, cfg.NLOC, cfg.S
    AK = AQ // 128
    DBW = 256
    NDB = D // DBW
    tiles = [(i * 512, min(512, NLOC - i * 512)) for i in range((NLOC + 511) // 512)]
    TW = 512
    if len(tiles) >= 2 and tiles[-1][1] <= 8:
        tl_ = tiles.pop()
        tiles[-1] = (tiles[-1][0], tiles[-1][1] + tl_[1])
        TW = 512 + tl_[1]
    with k.phase("p4a") as ph:
        hTt = ph.sbuf("hTt", [128, KC, TW], BF16)
        onTt = ph.sbuf("onTt", [128, AK, TW], BF16)
        yfTt = ph.sbuf("yfTt", [128, AK, TW], BF16)
        wbufs = [ph.sbuf("w", [128, KC, DBW], BF16) for _ in range(6)]
        wr = Ring(k, wbufs, "w")
        bg = ph.sbuf("bg", [128, 2 * D // 128], F32)
        sg = [[ph.sbuf("sg", [128, 512], F32) for _ in range(2)] for _ in range(2)]
        tt12 = [[ph.sbuf("t12", [128, 512], F32) for _ in range(2)] for _ in range(2)]
        zst = [ph.sbuf("zst", [128, 512], BF16) for _ in range(2)]
        banks = [ph.psum("ps", [128, 512], F32) for _ in range(8)]
        bank_free = [[] for _ in banks]
        cs = k.dsem("c")
        t_bg = k.dma("sp", bg[:], T["bgate_pm"], sem=cs)
        isem = k.dsem("i")
        zsems = [k.dsem("z") for _ in range(2)]
        slot_free = [[], []]
        hT_v = T["hT"].rearrange("(kc p) s -> p kc s", p=128)
        on_v = T["onT"].rearrange("(kc p) s -> p kc s", p=128)
        yf_v = T["yfT"].rearrange("(kc p) s -> p kc s", p=128)
        zT_v = T["zT"].rearrange("(kc p) s -> p kc s", p=128)
        wao_v = T["w_attn_o"].rearrange("(kc p) n -> p kc n", p=128)
        wf_v = T["w_fourier"].rearrange("(kc p) n -> p kc n", p=128)
        wg_v = T["w_gate"].rearrange("(kc p) n -> p kc n", p=128)
        tile_reads = []
        di = 0
        for (tt0_, n_) in tiles:
            tin = [k.dma("sp", hTt[:, :, :n_], hT_v[:, :, S + tt0_:S + tt0_ + n_], deps=tile_reads, sem=isem),
                   k.dma("sp", onTt[:, :, :n_], on_v[:, :, tt0_:tt0_ + n_], deps=tile_reads, sem=isem),
                   k.dma("sp", yfTt[:, :, :n_], yf_v[:, :, tt0_:tt0_ + n_], deps=tile_reads, sem=isem)]
            tile_reads = []
            subs = [(o, min(512, n_ - o)) for o in range(0, n_, 512)]
            for db in range(NDB):
                c0 = db * DBW
                sA, sG1, sG2 = wr.take(), wr.take(), wr.take()
                wA, wG1, wG2 = wbufs[sA], wbufs[sG1], wbufs[sG2]
                tA = [k.dma("pool", wA[:, 0:AK, :], T["wao_pk"][db], deps=[wr.free[sA], PK["A"]], sem=wr.sems[sA]),
                      k.dma("pool", wA[:, AK:2 * AK, :], T["wf_pk"][db], deps=[wr.free[sA], PK["A"]], sem=wr.sems[sA])]
                tG1 = k.dma("pool", wG1[:], T["wg_pk"][db], deps=[wr.free[sG1], PK["A"]], sem=wr.sems[sG1])
                tG2 = k.dma("pool", wG2[:], T["wg_pk"][NDB + db], deps=[wr.free[sG2], PK["A"]], sem=wr.sems[sG2])
                rdA, rdG1, rdG2 = [], [], []
                for j, (off, n) in [(j, sb) for j in range(DBW // 128) for sb in subs]:
                    tt0 = tt0_ + off
                    dc = db * (DBW // 128) + j
                    sl = di % 2
                    di += 1
                    b0 = 4 * sl
                    pA, pF, pGa, pGf = banks[b0], banks[b0 + 1], banks[b0 + 2], banks[b0 + 3]
                    js = slice(j * 128, (j + 1) * 128)
                    gA = mm_group(k, pA[:, :n], lambda kc, wA=wA, js=js: wA[:, kc, js], lambda kc, n=n, off=off: onTt[:, kc, off:off + n], AK,
                                  [tA, tin, bank_free[b0]])
                    gF = mm_group(k, pF[:, :n], lambda kc, wA=wA, js=js: wA[:, AK + kc, js], lambda kc, n=n, off=off: yfTt[:, kc, off:off + n], AK,
                                  [tA, tin, bank_free[b0 + 1]])
                    gGa = mm_group(k, pGa[:, :n], lambda kc, wG1=wG1, js=js: wG1[:, kc, js], lambda kc, n=n, off=off: hTt[:, kc, off:off + n], KC,
                                   [tG1, tin, bank_free[b0 + 2]])
                    gGf = mm_group(k, pGf[:, :n], lambda kc, wG2=wG2, js=js: wG2[:, kc, js], lambda kc, n=n, off=off: hTt[:, kc, off:off + n], KC,
                                   [tG2, tin, bank_free[b0 + 3]])
                    rdA += [gA, gF]; rdG1.append(gGa); rdG2.append(gGf)
                    tile_reads += [gF, gGf]
                    sga, sgf = sg[sl]
                    t1, t2 = tt12[sl]
                    a1 = k.op("act", lambda e, pGa=pGa, sga=sga, dc=dc, n=n: e.activation(sga[:, :n], pGa[:, :n], AF.Sigmoid,
                                                                                         bias=bg[:, dc:dc + 1], scale=1.0),
                              deps=[gGa, t_bg, slot_free[sl]])
                    a2 = k.op("act", lambda e, pGf=pGf, sgf=sgf, dc=dc, n=n: e.activation(sgf[:, :n], pGf[:, :n], AF.Sigmoid,
                                                                                         bias=bg[:, D // 128 + dc:D // 128 + dc + 1], scale=1.0),
                              deps=[gGf])
                    d1 = k.op("dve", lambda e, pA=pA, sga=sga, t1=t1, n=n: e.tensor_tensor(t1[:, :n], pA[:, :n], sga[:, :n], ALU.mult),
                              deps=[gA, a1])
                    d2 = k.op("dve", lambda e, pF=pF, sgf=sgf, t2=t2, n=n: e.tensor_tensor(t2[:, :n], pF[:, :n], sgf[:, :n], ALU.mult),
                              deps=[gF, a2])
                    d3 = k.op("dve", lambda e, sl=sl, t1=t1, t2=t2, n=n: e.tensor_tensor(zst[sl][:, :n], t1[:, :n], t2[:, :n], ALU.add),
                              deps=[d1, d2])
                    bank_free[b0] = [d1]; bank_free[b0 + 1] = [d2]; bank_free[b0 + 2] = [a1]; bank_free[b0 + 3] = [a2]
                    tz = k.dma("sp", zT_v[:, dc, tt0:tt0 + n], zst[sl][:, :n], deps=[d3], sem=zsems[sl])
                    slot_free[sl] = [tz]
                wr.free[sA] = rdA; wr.free[sG1] = rdG1; wr.free[sG2] = rdG2


def phase_tm_matmul(k, cfg, T, name, aT_dram, nk, ntok, w_pk, out_dram, TB=512, nslots=3, CWB=256, wdep=None):
    D = cfg.D
    NCB = D // CWB
    blocks = [(i * TB, min(TB, ntok - i * TB)) for i in range((ntok + TB - 1) // TB)]
    TBW = TB
    if len(blocks) >= 2 and blocks[-1][1] <= 8:
        tl_ = blocks.pop()
        blocks[-1] = (blocks[-1][0], blocks[-1][1] + tl_[1])
        TBW = TB + tl_[1]
    with k.phase(name) as ph:
        abufs = [ph.sbuf("a", [128, nk, TBW], BF16) for _ in range(2 if nk * TB * 2 <= 40 * 1024 else 1)]
        ar = Ring(k, abufs, "a")
        wbufs = [ph.sbuf("w", [128, nk, CWB], BF16) for _ in range(nslots)]
        wr = Ring(k, wbufs, "w")
        msts = [ph.sbuf("mst", [128, CWB], F32) for _ in range(2)]
        mst_free = [[], []]
        msems = [k.dsem("m") for _ in range(2)]
        banks = [ph.psum("ps", [128, 512], F32) for _ in range(8)]
        bank_free = [[] for _ in banks]
        a_v = aT_dram.rearrange("(kc p) s -> p kc s", p=128)
        bi = mi = 0
        for (tb0, bn) in blocks:
            as_ = ar.take()
            At = abufs[as_]
            ta = k.dma("sp", At[:, :, :bn], a_v[:, :, tb0:tb0 + bn], deps=ar.free[as_], sem=ar.sems[as_])
            areads = []
            for cb in range(NCB):
                ws = wr.take()
                Wt = wbufs[ws]
                tw = k.dma("pool", Wt[:], w_pk[cb], deps=[wr.free[ws], wdep], sem=wr.sems[ws])
                wreads = []
                for sub in range((bn + 127) // 128):
                    m = min(128, bn - sub * 128)
                    b = bi % 8
                    bi += 1
                    ps = banks[b]
                    tg = mm_group(k, ps[:m, :CWB], lambda kc, At=At, sub=sub, m=m: At[:, kc, sub * 128:sub * 128 + m],
                                  lambda kc, Wt=Wt: Wt[:, kc, :], nk, [ta, tw, bank_free[b]])
                    wreads.append(tg)
                    ms = mi % 2
                    mi += 1
                    te = k.op("act", lambda e, ps=ps, ms=ms, m=m: e.activation(msts[ms][:m, :], ps[:m, :CWB], AF.Copy), deps=[tg, mst_free[ms]])
                    bank_free[b] = [te]
                    r0 = tb0 + sub * 128
                    mst_free[ms] = [k.dma("sp", out_dram[r0:r0 + m, cb * CWB:(cb + 1) * CWB], msts[ms][:m, :], deps=[te], sem=msems[ms])]
                wr.free[ws] = wreads
                areads += wreads
            ar.free[as_] = areads


def phase_ffn_up(k, cfg, T, PK):
    D, KC, TL, FC, DFF = cfg.D, cfg.KC, cfg.TL, cfg.FC, cfg.DFF
    NHALF = 2
    HL = TL // NHALF
    L = HL + 2
    atiles = [(i * 512, min(512, L - i * 512)) for i in range((L + 511) // 512)]
    with k.phase("p5") as ph:
        h1Th = ph.sbuf("h1Th", [128, KC, L], BF16)
        wbufs = [ph.sbuf("w", [128, KC, 256], BF16) for _ in range(4)]
        wr = Ring(k, wbufs, "w")
        cw = ph.sbuf("cw", [128, 3, 2 * FC], F32)
        cb = ph.sbuf("cb", [128, 2 * FC], F32)
        cbuf = [[ph.sbuf("c", [128, L], F32) for _ in range(2)] for _ in range(2)]
        sgt = [ph.sbuf("sgt", [128, HL], F32) for _ in range(2)]
        pst = [ph.sbuf("pst", [128, HL], BF16) for _ in range(2)]
        banks = [ph.psum("ps", [128, 512], F32) for _ in range(8)]
        bank_free = [[] for _ in banks]
        cs = k.dsem("c")
        tcw = [k.dma("sp", cw[:], T["convw_pm"], sem=cs), k.dma("sp", cb[:], T["convb_pm"], sem=cs)]
        hsem = k.dsem("h")
        psems = [k.dsem("p") for _ in range(2)]
        slot_free = [[], []]
        h1T_v = T["h1T"].rearrange("(kc p) s -> p kc s", p=128)
        w_v = T["w_up"].rearrange("(kc p) n -> p kc n", p=128)
        pT_v = T["pT"].rearrange("(c p) s -> p c s", p=128)
        half_reads = []
        bi = 0
        ji = 0
        sc_ = k.dsem("pkC", barrier=False)
        wd_v = T["w_down"].rearrange("(kc p) n -> p kc n", p=128)
        wd_blocks = list(range(D // 256))
        for hf in range(NHALF):
            if hf == 0:
                pieces = [(0, TL, 1), (1, 0, HL + 1)]
            else:
                pieces = [(0, HL * hf - 1, HL + 1), (HL + 1, TL + 1, 1)]
            if NHALF > 2:
                raise NotImplementedError
            th = [k.dma("sp", h1Th[:, :, d0:d0 + n], h1T_v[:, :, s0:s0 + n], deps=half_reads, sem=hsem, slow=(n == 1))
                  for (d0, s0, n) in pieces]
            half_reads = []
            for jp in range(FC // 2):
                sg_, sv_ = wr.take(), wr.take()
                Wg, Wv = wbufs[sg_], wbufs[sv_]
                tg_ = k.dma("pool", Wg[:], w_v[:, :, jp * 256:(jp + 1) * 256], deps=wr.free[sg_], sem=wr.sems[sg_])
                tv_ = k.dma("pool", Wv[:], w_v[:, :, DFF + jp * 256:DFF + (jp + 1) * 256], deps=wr.free[sv_], sem=wr.sems[sv_])
                if wd_blocks and (hf * (FC // 2) + jp) % 5 == 4:
                    wb = wd_blocks.pop(0)
                    k.dma("pool", T["wdn_pk"][wb], wd_v[:, :, wb * 256:(wb + 1) * 256], sem=sc_)
                rdg, rdv = [], []
                for j in range(2):
                    jj = jp * 2 + j
                    sl = ji % 2
                    ji += 1
                    js = slice(j * 128, (j + 1) * 128)
                    lastc = []
                    for ci, (Wt, tw, rd, cidx) in enumerate(((Wg, tg_, rdg, jj), (Wv, tv_, rdv, FC + jj))):
                        cbf = cbuf[sl][ci]
                        tl_ = []
                        for (a0, n) in atiles:
                            b = bi % 8
                            bi += 1
                            ps = banks[b]
                            tg = mm_group(k, ps[:, :n], lambda kc, Wt=Wt, js=js: Wt[:, kc, js], lambda kc, a0=a0, n=n: h1Th[:, kc, a0:a0 + n], KC,
                                          [tw, th, bank_free[b]])
                            rd.append(tg)
                            tl_.append((a0, n, b, ps, tg))
                        half_reads.append(tl_[-1][4])
                        inits = []
                        for (a0, n, b, ps, tg) in tl_:
                            lo1, hi1 = max(a0, 1), min(a0 + n, L - 1)
                            if hi1 > lo1:
                                inits.append(k.op("act", lambda e, ps=ps, cbf=cbf, a0=a0, lo1=lo1, hi1=hi1, cidx=cidx: e.activation(
                                    cbf[:, lo1:hi1], ps[:, lo1 - a0:hi1 - a0], AF.Identity, bias=cb[:, cidx:cidx + 1], scale=cw[:, 1, cidx:cidx + 1]),
                                    deps=[tg, tcw, slot_free[sl]]))
                            else:
                                inits.append(None)
                        for ti_, (a0, n, b, ps, tg) in enumerate(tl_):
                            last = inits[ti_]
                            e0 = min(a0 + n, L - 2)
                            if e0 > a0:
                                last = k.op("dve", lambda e, ps=ps, cbf=cbf, a0=a0, e0=e0, cidx=cidx: e.scalar_tensor_tensor(
                                    cbf[:, a0 + 1:e0 + 1], ps[:, 0:e0 - a0], cw[:, 0, cidx:cidx + 1], cbf[:, a0 + 1:e0 + 1], ALU.mult, ALU.add),
                                    deps=[tg, inits, last])
                            s0 = max(a0, 2)
                            if a0 + n > s0:
                                last = k.op("dve", lambda e, ps=ps, cbf=cbf, a0=a0, n=n, s0=s0, cidx=cidx: e.scalar_tensor_tensor(
                                    cbf[:, s0 - 1:a0 + n - 1], ps[:, s0 - a0:n], cw[:, 2, cidx:cidx + 1], cbf[:, s0 - 1:a0 + n - 1], ALU.mult, ALU.add),
                                    deps=[tg, inits, last])
                            bank_free[b] = [last, inits[ti_]]
                            lastc.append(last)
                    cg, cv = cbuf[sl]
                    t_s = k.op("act", lambda e, cg=cg, sl=sl: e.activation(sgt[sl][:, :], cg[:, 1:L - 1], AF.Silu), deps=[lastc])
                    t_p = k.op("dve", lambda e, cv=cv, sl=sl: e.tensor_tensor(pst[sl][:, :], sgt[sl][:, :], cv[:, 1:L - 1], ALU.mult), deps=[t_s, lastc])
                    slot_free[sl] = [k.dma("sp", pT_v[:, jj, hf * HL:(hf + 1) * HL], pst[sl][:, :], deps=[t_p], sem=psems[sl]), t_p]
                wr.free[sg_] = rdg
                wr.free[sv_] = rdv
        for wb in wd_blocks:
            k.dma("pool", T["wdn_pk"][wb], wd_v[:, :, wb * 256:(wb + 1) * 256], sem=sc_)
        PK["C"] = Tok(sc_, sc_.count)


def build(cfg):
    nc = bass.Bass("TRN2", target_bir_lowering=False)
    D, S, AQ, NLOC, NTOK, DFF, TL = cfg.D, cfg.S, cfg.AQ, cfg.NLOC, cfg.NTOK, cfg.DFF, cfg.TL
    T = {}

    def inp(name, shape, dt=F32):
        T[name] = nc.dram_tensor(name, list(shape), dt, kind="ExternalInput").ap()

    def scr(name, shape, dt):
        kind = "ExternalOutput" if name in cfg.debug else "Internal"
        T[name] = nc.dram_tensor(name, list(shape), dt, kind=kind).ap()

    inp("xcat", [NTOK, D])
    inp("lng_rep", [128, D]); inp("lnb_rep", [128, D])
    inp("ident", [128, 128], BF16)
    scr("hT", [D, NTOK], BF16)
    scr("hloc", [NLOC, D], F32)
    T["out"] = nc.dram_tensor("out", [TL, D], F32, kind="ExternalOutput").ap()

    NH = cfg.NH
    inp("w_in", [D, 4 * AQ])
    inp("perm32", [128, 128], BF16)
    inp("ropek_cos", [32, S]); inp("ropek_sin", [32, S])
    inp("ropeq_cos", [32, NLOC]); inp("ropeq_sin", [32, NLOC])
    inp("csg", [256, 512], BF16)
    scr("kT", [2 * NH, 128, S], BF16)
    scr("v", [S, AQ], BF16)
    scr("z", [NH, 2, 128, S // 128, 256], BF16)
    scr("qT", [2 * NH, 128, NLOC], BF16)
    CW = min(512, AQ)
    inp("subg_rep", [128, 256]); inp("lam_rep", [128, 4, 128])
    scr("onT", [AQ, NLOC], BF16)
    NST = TL // 256 + 1
    inp("dftc", [NST, 128, S // 128, 256], BF16); inp("dfts", [NST, 128, S // 128, 256], BF16)
    scr("yfT", [AQ, NLOC], BF16)
    inp("w_attn_o", [AQ, D]); inp("w_fourier", [AQ, D]); inp("w_gate", [D, 2 * D]); inp("w_mix", [D, D])
    inp("bgate_pm", [128, 2 * D // 128])
    inp("ln1g_rep", [128, D]); inp("ln1b_rep", [128, D]); inp("hmask", [128, 1])
    scr("zT", [D, NLOC], BF16)
    scr("msc", [NLOC, D], F32)
    scr("h1", [NLOC, D], F32)
    scr("h1T", [D, NLOC], BF16)
    inp("w_up", [D, 2 * DFF]); inp("w_down", [DFF, D])
    inp("convw_pm", [128, 3, 2 * DFF // 128]); inp("convb_pm", [128, 2 * DFF // 128])
    inp("ln2g_rep", [128, D]); inp("ln2b_rep", [128, D])
    scr("pT", [DFF, TL], BF16)
    scr("wao_pk", [D // 256, 128, AQ // 128, 256], BF16); scr("wf_pk", [D // 256, 128, AQ // 128, 256], BF16)
    scr("wg_pk", [2 * D // 256, 128, D // 128, 256], BF16)
    scr("wmix_pk", [D // min(512, D), 128, D // 128, min(512, D)], BF16)
    scr("wdn_pk", [D // 256, 128, DFF // 128, 256], BF16)
    scr("fsc", [TL, D], F32)

    with ExitStack() as es:
        k = KB(nc, es)
        phase_ln(k, cfg, T, "p0", T["xcat"], NTOK, T["lng_rep"], T["lnb_rep"], out_f=(T["hloc"], S, 0), out_T=T["hT"], group_break=S)
        if cfg.upto >= 1:
            jobs = [dict(kind="rope", w=T["w_in"], c0=AQ, nblk=AQ // CW, out=T["kT"]),
                    dict(kind="tm", w=T["w_in"], c0=2 * AQ, nblk=AQ // CW, out=T["v"]),
                    dict(kind="fmf", w=T["w_in"], c0=3 * AQ, nblk=AQ // CW, out=None)]
            sel = os.environ.get("P1SEL")
            if sel:
                jobs = [j for j in jobs if j["kind"] in sel.split(",")]
            phase_proj(k, cfg, T, "p1", 0, S, 2, jobs, T["ropek_cos"], T["ropek_sin"])
        if cfg.upto >= 2:
            jobs = [dict(kind="rope", w=T["w_in"], c0=0, nblk=AQ // CW, out=T["qT"])]
            phase_proj(k, cfg, T, "p2", S, NLOC, 2, jobs, T["ropeq_cos"], T["ropeq_sin"])
        PK = {}
        if cfg.upto >= 3:
            phase_attn(k, cfg, T, PK)
        if cfg.upto >= 4:
            phase_fourier(k, cfg, T)
        if cfg.upto >= 5:
            phase_gate(k, cfg, T, PK)
        if cfg.upto >= 6:
            phase_tm_matmul(k, cfg, T, "p4b", T["zT"], cfg.KC, NLOC, T["wmix_pk"], T["msc"], nslots=3, CWB=min(512, D), wdep=PK["B"])
            phase_ln(k, cfg, T, "p4c", T["msc"], NLOC, T["ln1g_rep"], T["ln1b_rep"], res=T["hloc"], out_f=(T["h1"], 0, 0),
                     out_T=T["h1T"], mask_in=T["hmask"])
        if cfg.upto >= 7:
            phase_ffn_up(k, cfg, T, PK)
        if cfg.upto >= 8:
            phase_tm_matmul(k, cfg, T, "p6", T["pT"], cfg.FC, TL, T["wdn_pk"], T["fsc"], TB=512, nslots=2, wdep=PK["C"])
        if cfg.upto >= 9:
            phase_ln(k, cfg, T, "p7", T["fsc"], TL, T["ln2g_rep"], T["ln2b_rep"], res=T["h1"], out_f=(T["out"], 0, 0))
        with k.phase("fin"):
            k.wait_only("sp", [])
    return nc


def host_inputs(cfg, inputs):
    D, S, TL, NLOC = cfg.D, cfg.S, cfg.TL, cfg.NLOC
    x = np.asarray(inputs["x"], dtype=np.float32)
    maps = []
    common = {
        "lng_rep": np.ascontiguousarray(np.broadcast_to(np.asarray(inputs["ln_emb_g"], np.float32)[None, :], (128, D))),
        "lnb_rep": np.ascontiguousarray(np.broadcast_to(np.asarray(inputs["ln_emb_b"], np.float32)[None, :], (128, D))),
        "ident": np.eye(128, dtype=np.float32).astype(ml_dtypes.bfloat16),
    }
    bf = ml_dtypes.bfloat16
    f32c = lambda a: np.ascontiguousarray(np.asarray(a, np.float32))
    common["w_in"] = f32c(inputs["w_in"][0])
    perm = np.zeros((128, 128), np.float32)
    for j in range(16):
        perm[j + 16, j] = 1.0
        perm[j, j + 16] = 1.0
    common["perm32"] = perm.astype(bf)
    inv_freq = (np.float32(ROPE_THETA) ** (-np.arange(0, 32, 2, dtype=np.float32) / np.float32(32))).astype(np.float32)

    def rope_tabs(pos):
        ang = (pos.astype(np.float32)[None, :] * inv_freq[:, None]).astype(np.float32).astype(np.float64)
        cos = np.concatenate([np.cos(ang), np.cos(ang)], 0)
        sin = np.concatenate([-np.sin(ang), np.sin(ang)], 0)
        return f32c(cos), f32c(sin)

    common["ropek_cos"], common["ropek_sin"] = rope_tabs(np.arange(S))
    cg = np.arange(256)[:, None] * np.arange(256)[None, :] % 256
    ang = 2.0 * np.pi * cg / 256.0
    common["csg"] = np.concatenate([np.cos(ang), -np.sin(ang)], 1).astype(np.float32).astype(bf)
    rep = lambda v: np.ascontiguousarray(np.broadcast_to(np.asarray(v, np.float32)[None], (128,) + np.asarray(v).shape))
    common["subg_rep"] = rep(inputs["subln_g"][0])
    common["lam_rep"] = rep(np.stack([inputs["lambda_q1"][0], inputs["lambda_k1"][0], inputs["lambda_q2"][0], inputs["lambda_k2"][0]], 0))
    pm = lambda v: np.ascontiguousarray(np.asarray(v, np.float32).reshape(-1, 128).T)
    common["w_attn_o"] = f32c(inputs["w_attn_o"][0]); common["w_fourier"] = f32c(inputs["w_fourier"][0])
    common["w_gate"] = f32c(inputs["w_gate"][0]); common["w_mix"] = f32c(inputs["w_mix_out"][0])
    common["bgate_pm"] = pm(inputs["b_gate"][0])
    common["ln1g_rep"] = rep(inputs["ln1_g"][0]); common["ln1b_rep"] = rep(inputs["ln1_b"][0])
    common["ln2g_rep"] = rep(inputs["ln2_g"][0]); common["ln2b_rep"] = rep(inputs["ln2_b"][0])
    common["w_up"] = f32c(inputs["w_up"][0]); common["w_down"] = f32c(inputs["w_down"][0])
    cwv = np.asarray(inputs["conv_w"][0], np.float32)
    common["convw_pm"] = np.ascontiguousarray(cwv.reshape(3, -1, 128).transpose(2, 0, 1))
    common["convb_pm"] = pm(inputs["conv_b"][0])
    for c in range(N_CORES):
        b, qi = c // 4, c % 4
        t0 = qi * TL
        iL = t0 - 1 if t0 - 1 >= 0 else t0
        iR = t0 + TL if t0 + TL < S else t0 + TL - 1
        xcat = np.concatenate([x[b], x[b, t0:t0 + TL], x[b, iL:iL + 1], x[b, iR:iR + 1]], axis=0)
        m = dict(common)
        m["xcat"] = np.ascontiguousarray(xcat)
        lpos = np.concatenate([np.arange(t0, t0 + TL), [iL, iR]])
        m["ropeq_cos"], m["ropeq_sin"] = rope_tabs(lpos)
        hm = np.zeros((128, 1), np.float32)
        hm[0, 0] = 1.0 if t0 - 1 >= 0 else 0.0
        hm[1, 0] = 1.0 if t0 + TL < S else 0.0
        m["hmask"] = hm
        NST = TL // 256 + 1
        lp = np.zeros(NST * 256, np.int64)
        lp[:NLOC] = lpos
        prod = (np.arange(S, dtype=np.int64)[:, None] * lp[None, :]) % S
        ang = prod.astype(np.float64) * (2.0 * np.pi / S)
        lay = lambda a: np.ascontiguousarray(a.astype(np.float32).astype(bf).reshape(S // 128, 128, NST, 256).transpose(2, 1, 0, 3))
        m["dftc"] = lay(np.cos(ang))
        m["dfts"] = lay(np.sin(ang))
        maps.append(m)
    return maps


def kernel(**inputs):
    cfg = Cfg()
    nc = build(cfg)
    maps = host_inputs(cfg, inputs)
    res = run_bass_kernel_spmd(nc, maps, core_ids=list(range(N_CORES)))
    out = np.empty((2, cfg.S, cfg.D), np.float32)
    for c in range(N_CORES):
        b, qi = c // 4, c % 4
        out[b, qi * cfg.TL:(qi + 1) * cfg.TL] = res.results[c]["out"]
    return out
```

```python
import math
import os
from contextlib import ExitStack, contextmanager

import numpy as np
import ml_dtypes

import concourse.bass as bass
import concourse.mybir as mybir
from concourse.bass_utils import run_bass_kernel_spmd

F32 = mybir.dt.float32
BF16 = mybir.dt.bfloat16
ALU = mybir.AluOpType
AF = mybir.ActivationFunctionType
AX = mybir.AxisListType

LN_EPS = 1e-5
ROPE_THETA = 500000.0
N_CORES = 8


class Cfg:
    def __init__(self, D=4096, S=8192, DFF=11008, upto=99, debug=()):
        self.D, self.S, self.DFF = D, S, DFF
        self.KC = D // 128
        self.NH = D // 512
        self.AQ = self.NH * 256
        self.TL = S // 4
        self.NLOC = self.TL + 2
        self.NTOK = S + self.NLOC
        self.FC = DFF // 128
        self.alpha = 2.0 ** 0.25
        self.lambda_init = 0.8 - 0.6 * math.exp(0.0)
        self.upto = upto
        self.debug = tuple(debug)


class Sem:
    def __init__(self, h, inc):
        self.h, self.inc, self.count = h, inc, 0


class Tok:
    __slots__ = ("sem", "val")

    def __init__(self, sem, val):
        self.sem, self.val = sem, val


ENGS = ("pe", "act", "dve", "pool", "sp")


def _flat(deps):
    out = []
    for d in deps:
        if d is None:
            continue
        if isinstance(d, (list, tuple)):
            out.extend(_flat(d))
        else:
            out.append(d)
    return out


class KB:
    def __init__(self, nc, es):
        self.nc, self.es = nc, es
        self.q = {e: [] for e in ENGS}
        self.waited = {e: {} for e in ENGS}
        self.esem = {}
        self.nsem = 0
        self.dsems = []
        self.bar = []
        self.last = {e: None for e in ENGS}
        self._new_esems()

    def _sem(self, name, inc):
        self.nsem += 1
        h = self.es.enter_context(self.nc.semaphore(f"{name}_{self.nsem}"))
        return Sem(h, inc)

    def _new_esems(self):
        for e in ("pe", "act", "dve", "pool"):
            self.esem[e] = self._sem("e" + e, 1)

    def dsem(self, name="d", barrier=True):
        s = self._sem(name, 16)
        if barrier:
            self.dsems.append(s)
        return s

    def _waits(self, eng, deps):
        waits = []
        w = self.waited[eng]
        best = {}
        for t in _flat(list(deps) + self.bar):
            if id(t.sem) not in best or best[id(t.sem)].val < t.val:
                best[id(t.sem)] = t
        for t in best.values():
            if w.get(id(t.sem), 0) >= t.val:
                continue
            w[id(t.sem)] = t.val
            waits.append(t)
        return waits

    def op(self, eng, fn, deps=(), sig=True):
        waits = self._waits(eng, deps)
        tok = None
        if sig:
            s = self.esem[eng]
            s.count += 1
            tok = Tok(s, s.count)
            self.last[eng] = tok
        self.q[eng].append((waits, fn, tok))
        return tok

    def dma(self, q, out, in_, deps=(), sem=None, slow=False):
        waits = self._waits(q, deps)
        sem.count += 16
        tok = Tok(sem, sem.count)
        if slow:
            fn = lambda e, o=out, i=in_: e.dma_start(out=o, in_=i, allow_slow_non_contiguous=True)
        else:
            fn = lambda e, o=out, i=in_: e.dma_start(out=o, in_=i)
        self.q[q].append((waits, fn, tok))
        return tok

    def wait_only(self, eng, deps):
        waits = self._waits(eng, deps)
        self.q[eng].append((waits, None, None))

    def barrier(self):
        toks = [t for t in self.last.values() if t is not None]
        toks += [Tok(s, s.count) for s in self.dsems if s.count > 0]
        self.bar = toks

    def flush(self):
        nc = self.nc

        def emit(eng, e):
            for waits, fn, tok in self.q[eng]:
                for w in waits:
                    e.wait_ge(w.sem.h, w.val)
                if fn is None:
                    continue
                ins = fn(e)
                if tok is not None:
                    ins.then_inc(tok.sem.h, tok.sem.inc)
            self.q[eng] = []

        with nc.Block() as blk:
            @blk.tensor
            def _(e):
                emit("pe", e)

            @blk.scalar
            def _(e):
                emit("act", e)

            @blk.vector
            def _(e):
                emit("dve", e)

            @blk.gpsimd
            def _(e):
                emit("pool", e)

            @blk.sync
            def _(e):
                emit("sp", e)

    @contextmanager
    def phase(self, name):
        ph = Phase(self, name)
        with ExitStack() as pes:
            ph.es = pes
            yield ph
            self.barrier()
            self.flush()


class Phase:
    def __init__(self, k, name):
        self.k, self.name, self.es, self.n = k, name, None, 0

    def sbuf(self, name, shape, dt):
        self.n += 1
        return self.es.enter_context(self.k.nc.sbuf_tensor(f"{self.name}_{name}_{self.n}", list(shape), dt))

    def psum(self, name, shape, dt):
        self.n += 1
        return self.es.enter_context(self.k.nc.psum_tensor(f"{self.name}_{name}_{self.n}", list(shape), dt))


class Ring:
    def __init__(self, k, bufs, name="r"):
        self.k, self.bufs = k, bufs
        self.n = len(bufs)
        self.sems = [k.dsem(name) for _ in bufs]
        self.free = [[] for _ in bufs]
        self.i = 0

    def take(self):
        s = self.i % self.n
        self.i += 1
        return s


def transpose_tile(k, cfg, hb, n, deps, ident, pst, pst_free, stg, col0, evac_eng_cycle):
    KC = cfg.KC
    toks = []
    ngrp = (KC + 7) // 8
    for g in range(ngrp):
        kcs = list(range(g * 8, min(KC, g * 8 + 8)))
        b = pst["i"] % len(pst["t"])
        pst["i"] += 1
        ps = pst["t"][b]
        tp = None
        for j, kc in enumerate(kcs):
            last = j == len(kcs) - 1
            tp = k.op("pe", lambda e, j=j, kc=kc, ps=ps: e.transpose(ps[:, j, :n], hb[:n, kc * 128:(kc + 1) * 128], ident[:n, :n]),
                      deps=(list(deps) + [pst_free[b]]) if j == 0 else (), sig=last)
        eng = evac_eng_cycle[g % len(evac_eng_cycle)]
        nk = len(kcs)
        if eng == "act":
            te = k.op("act", lambda e, ps=ps, k0=kcs[0], nk=nk: e.activation(stg[:, k0:k0 + nk, col0:col0 + n], ps[:, 0:nk, :n], AF.Copy),
                      deps=[tp])
        else:
            te = k.op("dve", lambda e, ps=ps, k0=kcs[0], nk=nk: e.tensor_copy(stg[:, k0:k0 + nk, col0:col0 + n], ps[:, 0:nk, :n]),
                      deps=[tp])
        pst_free[b] = te
        toks.append(te)
    return toks


def phase_ln(k, cfg, T, name, src, ntok, g_in, b_in, res=None, out_f=None, out_T=None, mask_in=None, group_break=None):
    D, KC = cfg.D, cfg.KC
    nch = D // 512
    with k.phase(name) as ph:
        g_rep = ph.sbuf("g", [128, D], F32)
        b_rep = ph.sbuf("b", [128, D], F32)
        xts = [ph.sbuf("xt", [128, D], F32) for _ in range(3)]
        stats = ph.sbuf("st", [128, 6 * nch], F32)
        mv = ph.sbuf("mv", [128, 2], F32)
        rstd = ph.sbuf("rs", [128, 1], F32)
        nmr = ph.sbuf("nm", [128, 1], F32)
        eps_t = ph.sbuf("eps", [128, 1], F32)
        t_eps = k.op("dve", lambda e: e.memset(eps_t[:], LN_EPS))
        cs = k.dsem("c")
        tc = [k.dma("sp", g_rep[:], g_in, sem=cs), k.dma("sp", b_rep[:], b_in, sem=cs)]
        if res is not None:
            rts = [ph.sbuf("rt", [128, D], F32) for _ in range(3)]
            rr = Ring(k, rts, "r")
        if out_T is not None:
            ident = ph.sbuf("id", [128, 128], BF16)
            tc.append(k.dma("sp", ident[:], T["ident"], sem=cs))
            hbs = [ph.sbuf("hb", [128, D], BF16) for _ in range(2)]
            hb_free = [[], []]
            nstg = 1 if res is not None else 2
            stgs = [ph.sbuf("stg", [128, KC, 512], BF16) for _ in range(nstg)]
            stg_free = [[] for _ in range(nstg)]
            sts = [k.dsem("st") for _ in range(nstg)]
            pst = {"t": [ph.psum("pt", [128, 8, 128], BF16) for _ in range(4)], "i": 0}
            pst_free = [None] * 4
            oT_v = out_T.rearrange("(kc p) s -> p kc s", p=128)
        if mask_in is not None:
            mask = ph.sbuf("mask", [128, 1], F32)
            tc.append(k.dma("sp", mask[:], mask_in, sem=cs))
        xr = Ring(k, xts, "x")
        fsems = [k.dsem("f") for _ in range(len(xts))]
        tiles = [(i * 128, 128) for i in range(ntok // 128)]
        if ntok % 128:
            tiles.append((ntok - ntok % 128, ntok % 128))
        gi = gcol = g0 = 0
        gtoks = []
        loads = {}

        def load(i):
            t0, n = tiles[i]
            s_ = xr.take()
            tl = [k.dma("sp", xts[s_][:n, :], src[t0:t0 + n, :], deps=xr.free[s_], sem=xr.sems[s_])]
            rs_ = None
            if res is not None:
                rs_ = rr.take()
                tl.append(k.dma("sp", rts[rs_][:n, :], res[t0:t0 + n, :], deps=rr.free[rs_], sem=rr.sems[rs_]))
            loads[i] = (s_, rs_, tl)

        R = len(xts)
        for i0 in range(min(R, len(tiles))):
            load(i0)
        sm2 = [(stats, mv, rstd, nmr),
               (ph.sbuf("st2", [128, 6 * nch], F32), ph.sbuf("mv2", [128, 2], F32), ph.sbuf("rs2", [128, 1], F32), ph.sbuf("nm2", [128, 1], F32))]
        stA = {}

        def stage_a(i):
            t0, n = tiles[i]
            s_, rs_, tl = loads.pop(i)
            xt = xts[s_]
            stats_, mv_, rstd_, nmr_ = sm2[i % 2]
            deps = [tl, t_eps, tc]
            if res is not None:
                rt = rts[rs_]
                t = k.op("dve", lambda e, xt=xt, rt=rt, n=n: e.scalar_tensor_tensor(xt[:n, :], rt[:n, :], cfg.alpha, xt[:n, :], ALU.mult, ALU.add),
                         deps=deps)
                rr.free[rs_] = [t]
                deps = [t]
            t = None
            for c in range(nch):
                t = k.op("dve", lambda e, c=c, xt=xt, n=n: e.bn_stats(stats_[:n, c * 6:(c + 1) * 6], xt[:n, c * 512:(c + 1) * 512]),
                         deps=deps if c == 0 else ())
            t = k.op("dve", lambda e, n=n: e.bn_aggr(mv_[:n, :], stats_[:n, :]), deps=[t])
            t = k.op("act", lambda e, n=n: e.activation(rstd_[:n, :], mv_[:n, 1:2], AF.Sqrt, bias=eps_t[:n, 0:1], scale=1.0), deps=[t])
            t = k.op("dve", lambda e, n=n: e.reciprocal(rstd_[:n, :], rstd_[:n, :]), deps=[t])
            t = k.op("dve", lambda e, n=n: e.tensor_scalar(nmr_[:n, :], mv_[:n, 0:1], rstd_[:n, 0:1], -1.0, ALU.mult, ALU.mult), deps=[t])
            t = k.op("act", lambda e, xt=xt, n=n: e.activation(xt[:n, :], xt[:n, :], AF.Identity, bias=nmr_[:n, 0:1], scale=rstd_[:n, 0:1]),
                     deps=[t])
            t = k.op("pool", lambda e, xt=xt, n=n: e.tensor_tensor(xt[:n, :], xt[:n, :], g_rep[:n, :], ALU.mult), deps=[t])
            stA[i] = (s_, xt, t)

        def stage_b(i):
            nonlocal gi, gcol, g0, gtoks
            t0, n = tiles[i]
            s_, xt, t = stA.pop(i)
            want_f = out_f is not None and t0 >= out_f[1]
            done = []
            if want_f or out_T is None:
                t_y = k.op("dve", lambda e, xt=xt, n=n: e.tensor_tensor(xt[:n, :], xt[:n, :], b_rep[:n, :], ALU.add), deps=[t])
                done.append(t_y)
            if want_f:
                r0 = t0 - out_f[1] + out_f[2]
                done.append(k.dma("pool", out_f[0][r0:r0 + n, :], xt[:n, :], deps=[t_y], sem=fsems[s_]))
            if out_T is not None:
                hb = hbs[i % 2]
                if want_f:
                    t_b = k.op("act", lambda e, xt=xt, hb=hb, n=n: e.activation(hb[:n, :], xt[:n, :], AF.Copy), deps=[t_y, hb_free[i % 2]])
                else:
                    t_b = k.op("dve", lambda e, xt=xt, hb=hb, n=n: e.tensor_tensor(hb[:n, :], xt[:n, :], b_rep[:n, :], ALU.add),
                               deps=[t, hb_free[i % 2]])
                if mask_in is not None and n < 128:
                    t_b = k.op("dve", lambda e, hb=hb, n=n: e.tensor_scalar(hb[:n, :], hb[:n, :], mask[:n, 0:1], None, ALU.mult), deps=[t_b])
                done.append(t_b)
                stg = stgs[gi % nstg]
                if gcol == 0:
                    g0 = t0
                    gtoks = []
                tt = transpose_tile(k, cfg, hb, n, [t_b] + (stg_free[gi % nstg] if gcol == 0 else []), ident, pst, pst_free,
                                    stg, gcol, ["act", "dve"])
                hb_free[i % 2] = tt
                gtoks += tt
                gcol += n
                nxt = tiles[i + 1] if i + 1 < len(tiles) else None
                if gcol == 512 or nxt is None or nxt[0] == group_break or nxt[1] != 128:
                    stg_free[gi % nstg] = [k.dma("pool", oT_v[:, :, g0:g0 + gcol], stg[:, :, 0:gcol], deps=gtoks, sem=sts[gi % nstg])]
                    gi += 1
                    gcol = 0
            xr.free[s_] = done

        stage_a(0)
        for i in range(len(tiles)):
            if i + 1 < len(tiles):
                stage_a(i + 1)
            stage_b(i)
            if i + R < len(tiles):
                load(i + R)


def mm_group(k, ps_ap, lhs_fn, rhs_fn, nk, deps):
    t = None
    for kc in range(nk):
        t = k.op("pe", lambda e, kc=kc: e.matmul(ps_ap, lhs_fn(kc), rhs_fn(kc), start=(kc == 0), stop=(kc == nk - 1)),
                 deps=deps if kc == 0 else (), sig=(kc == nk - 1))
    return t


def phase_proj(k, cfg, T, name, col0, ntok, TBN, jobs, rope_cos=None, rope_sin=None):
    D, KC, AQ = cfg.D, cfg.KC, cfg.AQ
    CW = min(512, AQ)
    NJ = CW // 128
    has_f = any(j["kind"] == "fmf" for j in jobs)
    has_rope = any(j["kind"] == "rope" for j in jobs)
    tiles = [(i * 512, min(512, ntok - i * 512)) for i in range((ntok + 511) // 512)]
    blocks = [tiles[i:i + TBN] for i in range(0, len(tiles), TBN)]
    BW = TBN * 512
    with k.phase(name) as ph:
        hTb = ph.sbuf("hTb", [128, KC, BW], BF16)
        wbufs = [ph.sbuf("w", [128, KC, CW], BF16) for _ in range(2)]
        wr = Ring(k, wbufs, "w")
        ksts = [ph.sbuf("kst", [128, 512], BF16) for _ in range(2)]
        kst_free = [[], []]
        kst_sems = [k.dsem("ks") for _ in range(2)]
        kst_i = 0
        banks = [ph.psum("ps", [128, 512], F32) for _ in range(6)]
        bank_free = [[] for _ in banks]
        bi = 0
        if has_rope:
            perm = ph.sbuf("perm", [128, 128], BF16)
            cs = k.dsem("c")
            t_perm = k.dma("sp", perm[:], T["perm32"], sem=cs)
            cosb = ph.sbuf("cos", [128, BW], F32)
            sinb = ph.sbuf("sin", [128, BW], F32)
            rsem = k.dsem("rt")
            t1f = [ph.sbuf("t1f", [128, 512], F32) for _ in range(2)]
            t2f = [ph.sbuf("t2f", [128, 512], F32) for _ in range(2)]
            tf_free = [[], []]
            rbanks = [ph.psum("pr", [128, 512], F32) for _ in range(2)]
            rbank_free = [[], []]
            ri = 0
        if has_f:
            ufT = ph.sbuf("ufT", [128, AQ // 128, BW], BF16)
            csg = ph.sbuf("csg", [128, 2, 512], BF16)
            cs2 = k.dsem("c2")
            t_csg = k.dma("sp", csg[:], T["csg"].rearrange("(cc p) n -> p cc n", p=128), sem=cs2)
        hsem = k.dsem("h")
        hT_v = T["hT"].rearrange("(kc p) s -> p kc s", p=128)
        blk_reads = []
        pending_tails = []
        tail_reads = []

        def run_tails():
            while pending_tails:
                pending_tails.pop(0)()

        def issue_block_loads(blk):
            b0 = blk[0][0]
            bn = sum(n for _, n in blk)
            t_h = k.dma("sp", hTb[:, :, 0:bn], hT_v[:, :, col0 + b0:col0 + b0 + bn], deps=blk_reads, sem=hsem)
            t_rt = []
            if has_rope:
                t_rt = [k.dma("sp", cosb[0:32, 0:bn], rope_cos[:, b0:b0 + bn], deps=blk_reads, sem=rsem),
                        k.dma("sp", sinb[0:32, 0:bn], rope_sin[:, b0:b0 + bn], deps=blk_reads, sem=rsem)]
            return t_h, t_rt

        pre = None
        for bidx, blk in enumerate(blocks):
            run_tails()
            blk_reads += tail_reads
            del tail_reads[:]
            b0 = blk[0][0]
            bn = sum(n for _, n in blk)
            if pre is None:
                t_h, t_rt = issue_block_loads(blk)
            else:
                t_h, t_rt = pre
                pre = None
            blk_reads = []
            uf_writes = []
            for job in jobs:
                w_v = job["w"].rearrange("(kc p) n -> p kc n", p=128)
                for cb in range(job["nblk"]):
                    c0 = job["c0"] + cb * CW
                    ws = wr.take()
                    wt = wbufs[ws]
                    t_w = k.dma("pool", wt[:], w_v[:, :, c0:c0 + CW], deps=wr.free[ws], sem=wr.sems[ws])
                    wreads = []
                    for (tt0, n) in blk:
                        lo = tt0 - b0
                        if job["kind"] in ("rope", "fmf"):
                            for j in range(NJ):
                                b = bi % len(banks); bi += 1
                                ps = banks[b]
                                tg = mm_group(k, ps[:, :n], lambda kc, j=j, wt=wt: wt[:, kc, j * 128:(j + 1) * 128],
                                              lambda kc, lo=lo, n=n: hTb[:, kc, lo:lo + n], KC, [t_w, t_h, bank_free[b]])
                                wreads.append(tg)
                                run_tails()
                                chunk = (c0 - job["c0"]) // 128 + j
                                if job["kind"] == "fmf":
                                    te = k.op("act", lambda e, ps=ps, chunk=chunk, lo=lo, n=n: e.activation(
                                        ufT[:, chunk, lo:lo + n], ps[:, :n], AF.Copy), deps=[tg, t_rt])
                                    bank_free[b] = [te]
                                    uf_writes.append(te)
                                else:
                                    si = kst_i % 2; kst_i += 1
                                    kst = ksts[si]
                                    ta = k.op("act", lambda e, ps=ps, kst=kst, n=n: e.activation(kst[:, :n], ps[:, :n], AF.Copy),
                                              deps=[tg, kst_free[si]])
                                    rb = ri % 2; ri += 1
                                    pr = rbanks[rb]
                                    f1, f2 = t1f[rb], t2f[rb]
                                    td1 = k.op("dve", lambda e, ps=ps, f1=f1, lo=lo, n=n: e.tensor_tensor(
                                        f1[0:32, :n], ps[0:32, :n], cosb[0:32, lo:lo + n], ALU.mult), deps=[tg, t_rt, tf_free[rb], ta])
                                    bank_free[b] = [ta, td1]
                                    blk_reads.append(td1)

                                    def tail(pr=pr, kst=kst, n=n, lo=lo, f1=f1, f2=f2, rb=rb, si=si, ta=ta, td1=td1, chunk=chunk, tt0=tt0, job=job):
                                        tr = k.op("pe", lambda e: e.matmul(pr[:, :n], perm[:, :], kst[:, :n], start=True, stop=True),
                                                  deps=[ta, t_perm, rbank_free[rb]])
                                        td2 = k.op("dve", lambda e: e.tensor_tensor(f2[0:32, :n], pr[0:32, :n], sinb[0:32, lo:lo + n], ALU.mult),
                                                   deps=[tr, t_rt])
                                        td3 = k.op("dve", lambda e: e.tensor_tensor(kst[0:32, :n], f1[0:32, :n], f2[0:32, :n], ALU.add),
                                                   deps=[td1, td2, ta])
                                        rbank_free[rb] = [td2]
                                        tf_free[rb] = [td3]
                                        tail_reads.append(td2)
                                        kst_free[si] = [k.dma("sp", job["out"][chunk, :, tt0:tt0 + n], kst[:, :n], deps=[td3], sem=kst_sems[si])]

                                    pending_tails.append(tail)
                        else:
                            nsub = (n + 127) // 128
                            for sub in range(nsub):
                                m = min(128, n - sub * 128)
                                b = bi % len(banks); bi += 1
                                ps = banks[b]
                                tg = mm_group(k, ps[:m, :CW], lambda kc, lo=lo, sub=sub, m=m: hTb[:, kc, lo + sub * 128:lo + sub * 128 + m],
                                              lambda kc, wt=wt: wt[:, kc, :], KC, [t_w, t_h, bank_free[b]])
                                wreads.append(tg)
                                run_tails()
                                si = kst_i % 2; kst_i += 1
                                kst = ksts[si]
                                ta = k.op("act", lambda e, ps=ps, kst=kst, m=m: e.activation(kst[:m, :CW], ps[:m, :CW], AF.Copy),
                                          deps=[tg, kst_free[si]])
                                bank_free[b] = [ta]
                                r0 = tt0 + sub * 128
                                tst = k.dma("sp", job["out"][r0:r0 + m, c0 - job["c0"]:c0 - job["c0"] + CW], kst[:m, :CW],
                                            deps=[ta], sem=kst_sems[si])
                                kst_free[si] = [tst]
                    wr.free[ws] = wreads
                    blk_reads += wreads
            run_tails()
            blk_reads += tail_reads
            del tail_reads[:]
            if has_f and bidx + 1 < len(blocks):
                pre = issue_block_loads(blocks[bidx + 1])
                blk_reads = []
            if has_f:
                NG = AQ // 256
                for (tt0, n) in blk:
                    lo = tt0 - b0
                    for sub in range((n + 127) // 128):
                        m = min(128, n - sub * 128)
                        for g in range(NG):
                            b = bi % len(banks); bi += 1
                            ps = banks[b]
                            tg = mm_group(k, ps[:m, :], lambda cc, g=g, lo=lo, sub=sub, m=m: ufT[:, 2 * g + cc, lo + sub * 128:lo + sub * 128 + m],
                                          lambda cc: csg[:, cc, :], 2, [uf_writes, t_csg, bank_free[b]])
                            blk_reads.append(tg)
                            si = kst_i % 2; kst_i += 1
                            kst = ksts[si]
                            ta = k.op("act", lambda e, ps=ps, kst=kst, m=m: e.activation(kst[:m, :], ps[:m, :], AF.Copy, scale=1.0 / 16.0),
                                      deps=[tg, kst_free[si]])
                            bank_free[b] = [ta]
                            sc = (tt0 + sub * 128) // 128
                            kv = kst[:m, :].rearrange("p (ri h c) -> p ri h c", ri=2, h=2)
                            tst = [k.dma("sp", T["z"][g, h, 0:m, sc, :].rearrange("p (ri c) -> p ri c", ri=2), kv[:, :, h, :],
                                         deps=[ta], sem=kst_sems[si]) for h in range(2)]
                            kst_free[si] = tst


def issue_pack(k, cfg, T, PK):
    D = cfg.D

    def pack(sem, w_ap, pk_ap, CWB):
        w_v = w_ap.rearrange("(kc p) n -> p kc n", p=128)
        ncol = w_ap.shape[1]
        for b in range(ncol // CWB):
            k.dma("pool", pk_ap[b], w_v[:, :, b * CWB:(b + 1) * CWB], sem=sem)

    sa = k.dsem("pkA", barrier=False)
    pack(sa, T["w_attn_o"], T["wao_pk"], 256)
    pack(sa, T["w_fourier"], T["wf_pk"], 256)
    pack(sa, T["w_gate"], T["wg_pk"], 256)
    PK["A"] = Tok(sa, sa.count)
    sb = k.dsem("pkB", barrier=False)
    pack(sb, T["w_mix"], T["wmix_pk"], min(512, D))
    PK["B"] = Tok(sb, sb.count)


def phase_attn(k, cfg, T, PK):
    S, NH, NLOC, TL = cfg.S, cfg.NH, cfg.NLOC, cfg.TL
    SC = S // 128
    NKB = S // 512
    NG8 = SC // 8
    qtiles = [(i * 128, 128) for i in range(TL // 128)] + [(TL, NLOC - TL)]
    sm_scale = 128.0 ** -0.5
    with k.phase("p3") as ph:
        KT = ph.sbuf("KT", [128, 2, S], BF16)
        V = ph.sbuf("V", [128, SC, 256], BF16)
        QT = ph.sbuf("QT", [128, 2, NLOC], BF16)
        E1 = [ph.sbuf("e1", [128, S], BF16) for _ in range(2)]
        E2 = [ph.sbuf("e2", [128, S], BF16) for _ in range(2)]
        aT = [ph.sbuf("aT", [128, SC, 128], BF16) for _ in range(2)]
        rs = [ph.sbuf("rs", [128, 2, NKB], F32) for _ in range(2)]
        lsum = [ph.sbuf("l", [128, 2], F32) for _ in range(2)]
        rl = [ph.sbuf("rl", [128, 2], F32) for _ in range(2)]
        coef = [ph.sbuf("cf", [128, 1], F32) for _ in range(2)]
        o_sb = [ph.sbuf("o", [128, 256], F32) for _ in range(2)]
        junk = ph.sbuf("junk", [128, 256], F32)
        ss = [ph.sbuf("ss", [128, 1], F32) for _ in range(2)]
        rstd = [ph.sbuf("rstd", [128, 1], F32) for _ in range(2)]
        onb = [ph.sbuf("onb", [128, 256], BF16) for _ in range(2)]
        onst = [ph.sbuf("onst", [128, 2, 128], BF16) for _ in range(2)]
        ident = ph.sbuf("id", [128, 128], BF16)
        g_rep = ph.sbuf("g", [128, 256], F32)
        lamr = ph.sbuf("lamr", [128, 4, 128], F32)
        lprod = ph.sbuf("lprod", [128, 2, 128], F32)
        lsm = ph.sbuf("lsm", [128, 2], F32)
        lex = ph.sbuf("lex", [128, 2], F32)
        neg_lam = ph.sbuf("nlam", [128, 1], F32)
        eps_t = ph.sbuf("eps", [128, 1], F32)
        sbanks = [ph.psum("s", [128, 512], F32) for _ in range(4)]
        sbank_free = [[] for _ in sbanks]
        tbanks = [ph.psum("t", [128, 8, 128], BF16) for _ in range(2)]
        tbank_free = [[], []]
        po = ph.psum("po", [128, 256], F32)
        po_free = []
        pon = ph.psum("pon", [128, 2, 128], BF16)
        pon_free = []
        cs = k.dsem("c3")
        tc = [k.dma("sp", ident[:], T["ident"], sem=cs), k.dma("sp", g_rep[:], T["subg_rep"], sem=cs),
              k.dma("sp", lamr[:], T["lam_rep"], sem=cs)]
        t_eps = k.op("dve", lambda e: e.memset(eps_t[:], LN_EPS))
        t = k.op("dve", lambda e: e.tensor_tensor(lprod[:, 0, :], lamr[:, 0, :], lamr[:, 1, :], ALU.mult), deps=tc)
        t = k.op("dve", lambda e: e.tensor_tensor(lprod[:, 1, :], lamr[:, 2, :], lamr[:, 3, :], ALU.mult), deps=[t])
        t = k.op("dve", lambda e: e.tensor_reduce(lsm[:, :], lprod[:, :, :], AX.X, ALU.add), deps=[t])
        t = k.op("act", lambda e: e.activation(lex[:, :], lsm[:, :], AF.Exp), deps=[t])
        t = k.op("dve", lambda e: e.tensor_tensor(neg_lam[:, :], lex[:, 1:2], lex[:, 0:1], ALU.subtract), deps=[t])
        t_lam = k.op("dve", lambda e: e.tensor_scalar(neg_lam[:, :], neg_lam[:, :], -cfg.lambda_init, None, ALU.add), deps=[t])
        issue_pack(k, cfg, T, PK)
        hsem = k.dsem("kv")
        osems = [k.dsem("on") for _ in range(2)]
        kT_d, qT_d = T["kT"], T["qT"]
        v_v = T["v"].rearrange("(sc p) c -> p sc c", p=128)
        onT_v = T["onT"].rearrange("(c p) s -> p c s", p=128)
        head_reads = []
        E_free = [[], []]
        aT_free = [[], []]
        small_free = [[], []]
        rs_free = [[], []]
        onst_free = [[], []]
        st = {"bi": 0, "ti": 0, "po_free": [], "pon_free": []}
        nsteps = 2 * NKB

        for hd in range(NH):
            tl = [k.dma("sp", KT[:, 0, :], kT_d[2 * hd, :, :], deps=head_reads, sem=hsem),
                  k.dma("sp", KT[:, 1, :], kT_d[2 * hd + 1, :, :], deps=head_reads, sem=hsem),
                  k.dma("sp", QT[:, 0, :], qT_d[2 * hd, :, :], deps=head_reads, sem=hsem),
                  k.dma("sp", QT[:, 1, :], qT_d[2 * hd + 1, :, :], deps=head_reads, sem=hsem),
                  k.dma("sp", V[:, :, :], v_v[:, :, hd * 256:(hd + 1) * 256], deps=head_reads, sem=hsem)]
            head_reads = []
            acts = {}

            def qk_steps(ti_, lo, hi, tl=tl, acts=acts):
                q0, nq = qtiles[ti_]
                sl = ti_ % 2
                for stp in range(lo, hi):
                    c, kb = stp // NKB, stp % NKB
                    ec = E1[sl] if c == 0 else E2[sl]
                    b = st["bi"] % 4
                    st["bi"] += 1
                    ps = sbanks[b]
                    tm = k.op("pe", lambda e, ps=ps, c=c, kb=kb, q0=q0, nq=nq: e.matmul(
                        ps[:nq, :], QT[:, c, q0:q0 + nq], KT[:, c, kb * 512:(kb + 1) * 512], start=True, stop=True),
                        deps=[tl, sbank_free[b]])
                    ta = k.op("act", lambda e, ps=ps, ec=ec, kb=kb, c=c, nq=nq, sl=sl: e.activation(
                        ec[:nq, kb * 512:(kb + 1) * 512], ps[:nq, :], AF.Exp, scale=sm_scale,
                        accum_out=rs[sl][:nq, c, kb:kb + 1]), deps=[tm, E_free[sl], rs_free[sl]])
                    sbank_free[b] = [ta]
                    acts.setdefault(ti_, []).append(ta)
                    head_reads.append(tm)

            combs = {}

            def combine(tj, acts=acts, combs=combs):
                q0, nq = qtiles[tj]
                sl = tj % 2
                e1, e2 = E1[sl], E2[sl]
                t = k.op("dve", lambda e: e.tensor_reduce(lsum[sl][:nq, :], rs[sl][:nq, :, :], AX.X, ALU.add),
                         deps=[acts[tj], small_free[sl]])
                rs_free[sl] = [t]
                t = k.op("dve", lambda e: e.reciprocal(rl[sl][:nq, :], lsum[sl][:nq, :]), deps=[t])
                t = k.op("dve", lambda e: e.tensor_tensor(coef[sl][:nq, :], lsum[sl][:nq, 0:1], rl[sl][:nq, 1:2], ALU.mult), deps=[t])
                t = k.op("dve", lambda e: e.tensor_tensor(coef[sl][:nq, :], coef[sl][:nq, :], neg_lam[:nq, :], ALU.mult), deps=[t, t_lam])
                combs[tj] = k.op("dve", lambda e: e.scalar_tensor_tensor(
                    e1[:nq, :], e2[:nq, :], coef[sl][:nq, 0:1], e1[:nq, :], ALU.mult, ALU.add), deps=[t])

            nqt = len(qtiles)
            half = nsteps // 2
            qk_steps(0, 0, nsteps)
            if nqt > 1:
                qk_steps(1, 0, half)
            for ti_ in range(nqt):
                q0, nq = qtiles[ti_]
                sl = ti_ % 2
                e1, e2 = E1[sl], E2[sl]
                nxt = ti_ + 1 < nqt
                nxt2 = ti_ + 2 < nqt
                if ti_ == 0:
                    combine(0)
                t_comb = combs[ti_]
                evs = []
                tp = None
                for g in range(NG8):
                    if nxt:
                        lo = half + (nsteps - half) * g // NG8
                        hi = half + (nsteps - half) * (g + 1) // NG8
                        qk_steps(ti_ + 1, lo, hi)
                    tb = st["ti"] % 2
                    st["ti"] += 1
                    pt = tbanks[tb]
                    for j in range(8):
                        sc = g * 8 + j
                        tp = k.op("pe", lambda e, pt=pt, j=j, sc=sc, e1=e1, nq=nq: e.transpose(
                            pt[:, j, :nq], e1[:nq, sc * 128:(sc + 1) * 128], ident[:nq, :nq]),
                            deps=[t_comb, tbank_free[tb], aT_free[sl]] if j == 0 else (), sig=(j == 7))
                    te = k.op("dve", lambda e, pt=pt, g=g, sl=sl, nq=nq: e.tensor_copy(aT[sl][:, g * 8:(g + 1) * 8, :nq], pt[:, :, :nq]),
                              deps=[tp])
                    tbank_free[tb] = [te]
                    evs.append(te)
                E_free[sl] = [tp]
                tpv = None
                for stp in range(half):
                    if nxt2:
                        qk_steps(ti_ + 2, stp, stp + 1)
                    for sc in range(SC * stp // half, SC * (stp + 1) // half):
                        tpv = k.op("pe", lambda e, sc=sc, sl=sl, nq=nq: e.matmul(po[:nq, :], aT[sl][:, sc, :nq], V[:, sc, :],
                                                                               start=(sc == 0), stop=(sc == SC - 1)),
                                   deps=[evs, st["po_free"]] if sc == 0 else (), sig=(sc == SC - 1))
                aT_free[sl] = [tpv]
                head_reads.append(tpv)
                if nxt:
                    combine(ti_ + 1)
                t_o = k.op("dve", lambda e, sl=sl, nq=nq: e.tensor_scalar(o_sb[sl][:nq, :], po[:nq, :], rl[sl][:nq, 0:1], None, ALU.mult),
                           deps=[tpv])
                st["po_free"] = [t_o]
                t = k.op("dve", lambda e, sl=sl, nq=nq: e.scalar_tensor_tensor(junk[:nq, :], o_sb[sl][:nq, :], 1.0, o_sb[sl][:nq, :],
                                                                               ALU.mult, ALU.mult, accum_out=ss[sl][:nq, 0:1]), deps=[t_o])
                t = k.op("act", lambda e, sl=sl, nq=nq: e.activation(rstd[sl][:nq, :], ss[sl][:nq, :], AF.Ln, bias=eps_t[:nq, 0:1],
                                                                     scale=1.0 / 256.0), deps=[t, t_eps])
                t = k.op("act", lambda e, sl=sl, nq=nq: e.activation(rstd[sl][:nq, :], rstd[sl][:nq, :], AF.Exp, scale=-0.5), deps=[t])
                t = k.op("dve", lambda e, sl=sl, nq=nq: e.tensor_scalar(rstd[sl][:nq, :], rstd[sl][:nq, :], 1.0 - cfg.lambda_init, None, ALU.mult),
                         deps=[t])
                t_on = k.op("dve", lambda e, sl=sl, nq=nq: e.scalar_tensor_tensor(
                    onb[sl][:nq, :], o_sb[sl][:nq, :], rstd[sl][:nq, 0:1], g_rep[:nq, :], ALU.mult, ALU.mult), deps=[t, tc])
                tp2 = None
                for h in range(2):
                    tp2 = k.op("pe", lambda e, h=h, sl=sl, nq=nq: e.transpose(pon[:, h, :nq], onb[sl][:nq, h * 128:(h + 1) * 128], ident[:nq, :nq]),
                               deps=[t_on, st["pon_free"]] if h == 0 else (), sig=(h == 1))
                te = k.op("dve", lambda e, sl=sl, nq=nq: e.tensor_copy(onst[sl][:, :, :nq], pon[:, :, :nq]), deps=[tp2, onst_free[sl]])
                st["pon_free"] = [te]
                small_free[sl] = [te]
                onst_free[sl] = [k.dma("sp", onT_v[:, 2 * hd:2 * hd + 2, q0:q0 + nq], onst[sl][:, :, :nq], deps=[te], sem=osems[sl])]


def phase_fourier(k, cfg, T):
    S, NH, NLOC, TL = cfg.S, cfg.NH, cfg.NLOC, cfg.TL
    SC = S // 128
    W = 256
    stiles = [(i * W, W) for i in range(TL // W)] + [(TL, NLOC - TL)]
    with k.phase("p3b") as ph:
        CTs = [ph.sbuf("CT", [128, SC, W], BF16) for _ in range(2)]
        STs = [ph.sbuf("ST", [128, SC, W], BF16) for _ in range(2)]
        zb = [ph.sbuf("zs", [128, SC, 256], BF16) for _ in range(2)]
        zr = Ring(k, zb, "z")
        ysts = [ph.sbuf("yst", [128, W], BF16) for _ in range(2)]
        yst_free = [[], []]
        ysems = [k.dsem("y") for _ in range(2)]
        banks = [ph.psum("ps", [128, 512], F32) for _ in range(4)]
        bank_free = [[] for _ in banks]
        tsems = [k.dsem("t") for _ in range(2)]
        tab_free = [[], []]
        bi = 0
        yi = 0
        yfT_v = T["yfT"].rearrange("(c p) s -> p c s", p=128)

        def load_tabs(sti):
            sl = sti % 2
            return [k.dma("sp", CTs[sl][:, :, :], T["dftc"][sti], deps=tab_free[sl], sem=tsems[sl]),
                    k.dma("sp", STs[sl][:, :, :], T["dfts"][sti], deps=tab_free[sl], sem=tsems[sl])]

        tabs = {0: load_tabs(0)}
        for sti, (w0, n) in enumerate(stiles):
            if sti + 1 < len(stiles):
                tabs[sti + 1] = load_tabs(sti + 1)
            tt = tabs.pop(sti)
            CT, ST = CTs[sti % 2], STs[sti % 2]
            tab_reads = []
            for g in range(NH):
                for h in range(2):
                    zs = zr.take()
                    Z = zb[zs]
                    tz = k.dma("sp", Z[:], T["z"][g, h], deps=zr.free[zs], sem=zr.sems[zs])
                    b = bi % 4
                    bi += 1
                    ps = banks[b]
                    t = None
                    for sc in range(SC):
                        k.op("pe", lambda e, ps=ps, Z=Z, sc=sc, n=n, CT=CT: e.matmul(ps[:, :n], Z[:, sc, 0:128], CT[:, sc, :n], start=(sc == 0), stop=False),
                             deps=[tz, tt, bank_free[b]] if sc == 0 else (), sig=False)
                        t = k.op("pe", lambda e, ps=ps, Z=Z, sc=sc, n=n, ST=ST: e.matmul(ps[:, :n], Z[:, sc, 128:256], ST[:, sc, :n], start=False,
                                                                                 stop=(sc == SC - 1)), sig=(sc == SC - 1))
                    zr.free[zs] = [t]
                    tab_reads.append(t)
                    ys = yi % 2
                    yi += 1
                    ta = k.op("act", lambda e, ps=ps, ys=ys, n=n: e.activation(ysts[ys][:, :n], ps[:, :n], AF.Copy, scale=float(S) ** -0.5),
                              deps=[t, yst_free[ys]])
                    bank_free[b] = [ta]
                    yst_free[ys] = [k.dma("pool", yfT_v[:, 2 * g + h, w0:w0 + n], ysts[ys][:, :n], deps=[ta], sem=ysems[ys])]
            tab_free[sti % 2] = tab_reads


def phase_gate(k, cfg, T, PK):
    D, KC, AQ, NLOC, S = cfg.D, cfg.KC, cfg.AQ, cfg.NLOC, cfg.S
    AK = AQ // 128
    DBW = 256
    NDB = D // DBW
    tiles = [(i * 512, min(512, NLOC - i * 512)) for i in range((NLOC + 511) // 512)]
    TW = 512
    if len(tiles) >= 2 and tiles[-1][1] <= 8:
        tl_ = tiles.pop()
        tiles[-1] = (tiles[-1][0], tiles[-1][1] + tl_[1])
        TW = 512 + tl_[1]
    with k.phase("p4a") as ph:
        hTt = ph.sbuf("hTt", [128, KC, TW], BF16)
        onTt = ph.sbuf("onTt", [128, AK, TW], BF16)
        yfTt = ph.sbuf("yfTt", [128, AK, TW], BF16)
        wbufs = [ph.sbuf("w", [128, KC, DBW], BF16) for _ in range(6)]
        wr = Ring(k, wbufs, "w")
        bg = ph.sbuf("bg", [128, 2 * D // 128], F32)
        sg = [[ph.sbuf("sg", [128, 512], F32) for _ in range(2)] for _ in range(2)]
        tt12 = [[ph.sbuf("t12", [128, 512], F32) for _ in range(2)] for _ in range(2)]
        zst = [ph.sbuf("zst", [128, 512], BF16) for _ in range(2)]
        banks = [ph.psum("ps", [128, 512], F32) for _ in range(8)]
        bank_free = [[] for _ in banks]
        cs = k.dsem("c")
        t_bg = k.dma("sp", bg[:], T["bgate_pm"], sem=cs)
        isem = k.dsem("i")
        zsems = [k.dsem("z") for _ in range(2)]
        slot_free = [[], []]
        hT_v = T["hT"].rearrange("(kc p) s -> p kc s", p=128)
        on_v = T["onT"].rearrange("(kc p) s -> p kc s", p=128)
        yf_v = T["yfT"].rearrange("(kc p) s -> p kc s", p=128)
        zT_v = T["zT"].rearrange("(kc p) s -> p kc s", p=128)
        wao_v = T["w_attn_o"].rearrange("(kc p) n -> p kc n", p=128)
        wf_v = T["w_fourier"].rearrange("(kc p) n -> p kc n", p=128)
        wg_v = T["w_gate"].rearrange("(kc p) n -> p kc n", p=128)
        tile_reads = []
        di = 0
        for (tt0_, n_) in tiles:
            tin = [k.dma("sp", hTt[:, :, :n_], hT_v[:, :, S + tt0_:S + tt0_ + n_], deps=tile_reads, sem=isem),
                   k.dma("sp", onTt[:, :, :n_], on_v[:, :, tt0_:tt0_ + n_], deps=tile_reads, sem=isem),
                   k.dma("sp", yfTt[:, :, :n_], yf_v[:, :, tt0_:tt0_ + n_], deps=tile_reads, sem=isem)]
            tile_reads = []
            subs = [(o, min(512, n_ - o)) for o in range(0, n_, 512)]
            for db in range(NDB):
                c0 = db * DBW
                sA, sG1, sG2 = wr.take(), wr.take(), wr.take()
                wA, wG1, wG2 = wbufs[sA], wbufs[sG1], wbufs[sG2]
                tA = [k.dma("pool", wA[:, 0:AK, :], T["wao_pk"][db], deps=[wr.free[sA], PK["A"]], sem=wr.sems[sA]),
                      k.dma("pool", wA[:, AK:2 * AK, :], T["wf_pk"][db], deps=[wr.free[sA], PK["A"]], sem=wr.sems[sA])]
                tG1 = k.dma("pool", wG1[:], T["wg_pk"][db], deps=[wr.free[sG1], PK["A"]], sem=wr.sems[sG1])
                tG2 = k.dma("pool", wG2[:], T["wg_pk"][NDB + db], deps=[wr.free[sG2], PK["A"]], sem=wr.sems[sG2])
                rdA, rdG1, rdG2 = [], [], []
                for j, (off, n) in [(j, sb) for j in range(DBW // 128) for sb in subs]:
                    tt0 = tt0_ + off
                    dc = db * (DBW // 128) + j
                    sl = di % 2
                    di += 1
                    b0 = 4 * sl
                    pA, pF, pGa, pGf = banks[b0], banks[b0 + 1], banks[b0 + 2], banks[b0 + 3]
                    js = slice(j * 128, (j + 1) * 128)
                    gA = mm_group(k, pA[:, :n], lambda kc, wA=wA, js=js: wA[:, kc, js], lambda kc, n=n, off=off: onTt[:, kc, off:off + n], AK,
                                  [tA, tin, bank_free[b0]])
                    gF = mm_group(k, pF[:, :n], lambda kc, wA=wA, js=js: wA[:, AK + kc, js], lambda kc, n=n, off=off: yfTt[:, kc, off:off + n], AK,
                                  [tA, tin, bank_free[b0 + 1]])
                    gGa = mm_group(k, pGa[:, :n], lambda kc, wG1=wG1, js=js: wG1[:, kc, js], lambda kc, n=n, off=off: hTt[:, kc, off:off + n], KC,
                                   [tG1, tin, bank_free[b0 + 2]])
                    gGf = mm_group(k, pGf[:, :n], lambda kc, wG2=wG2, js=js: wG2[:, kc, js], lambda kc, n=n, off=off: hTt[:, kc, off:off + n], KC,
                                   [tG2, tin, bank_free[b0 + 3]])
                    rdA += [gA, gF]; rdG1.append(gGa); rdG2.append(gGf)
                    tile_reads += [gF, gGf]
                    sga, sgf = sg[sl]
                    t1, t2 = tt12[sl]
                    a1 = k.op("act", lambda e, pGa=pGa, sga=sga, dc=dc, n=n: e.activation(sga[:, :n], pGa[:, :n], AF.Sigmoid,
                                                                                         bias=bg[:, dc:dc + 1], scale=1.0),
                              deps=[gGa, t_bg, slot_free[sl]])
                    a2 = k.op("act", lambda e, pGf=pGf, sgf=sgf, dc=dc, n=n: e.activation(sgf[:, :n], pGf[:, :n], AF.Sigmoid,
                                                                                         bias=bg[:, D // 128 + dc:D // 128 + dc + 1], scale=1.0),
                              deps=[gGf])
                    d1 = k.op("dve", lambda e, pA=pA, sga=sga, t1=t1, n=n: e.tensor_tensor(t1[:, :n], pA[:, :n], sga[:, :n], ALU.mult),
                              deps=[gA, a1])
                    d2 = k.op("dve", lambda e, pF=pF, sgf=sgf, t2=t2, n=n: e.tensor_tensor(t2[:, :n], pF[:, :n], sgf[:, :n], ALU.mult),
                              deps=[gF, a2])
                    d3 = k.op("dve", lambda e, sl=sl, t1=t1, t2=t2, n=n: e.tensor_tensor(zst[sl][:, :n], t1[:, :n], t2[:, :n], ALU.add),
                              deps=[d1, d2])
                    bank_free[b0] = [d1]; bank_free[b0 + 1] = [d2]; bank_free[b0 + 2] = [a1]; bank_free[b0 + 3] = [a2]
                    tz = k.dma("sp", zT_v[:, dc, tt0:tt0 + n], zst[sl][:, :n], deps=[d3], sem=zsems[sl])
                    slot_free[sl] = [tz]
                wr.free[sA] = rdA; wr.free[sG1] = rdG1; wr.free[sG2] = rdG2


def phase_tm_matmul(k, cfg, T, name, aT_dram, nk, ntok, w_pk, out_dram, TB=512, nslots=3, CWB=256, wdep=None):
    D = cfg.D
    NCB = D // CWB
    blocks = [(i * TB, min(TB, ntok - i * TB)) for i in range((ntok + TB - 1) // TB)]
    TBW = TB
    if len(blocks) >= 2 and blocks[-1][1] <= 8:
        tl_ = blocks.pop()
        blocks[-1] = (blocks[-1][0], blocks[-1][1] + tl_[1])
        TBW = TB + tl_[1]
    with k.phase(name) as ph:
        abufs = [ph.sbuf("a", [128, nk, TBW], BF16) for _ in range(2 if nk * TB * 2 <= 40 * 1024 else 1)]
        ar = Ring(k, abufs, "a")
        wbufs = [ph.sbuf("w", [128, nk, CWB], BF16) for _ in range(nslots)]
        wr = Ring(k, wbufs, "w")
        msts = [ph.sbuf("mst", [128, CWB], F32) for _ in range(2)]
        mst_free = [[], []]
        msems = [k.dsem("m") for _ in range(2)]
        banks = [ph.psum("ps", [128, 512], F32) for _ in range(8)]
        bank_free = [[] for _ in banks]
        a_v = aT_dram.rearrange("(kc p) s -> p kc s", p=128)
        bi = mi = 0
        for (tb0, bn) in blocks:
            as_ = ar.take()
            At = abufs[as_]
            ta = k.dma("sp", At[:, :, :bn], a_v[:, :, tb0:tb0 + bn], deps=ar.free[as_], sem=ar.sems[as_])
            areads = []
            for cb in range(NCB):
                ws = wr.take()
                Wt = wbufs[ws]
                tw = k.dma("pool", Wt[:], w_pk[cb], deps=[wr.free[ws], wdep], sem=wr.sems[ws])
                wreads = []
                for sub in range((bn + 127) // 128):
                    m = min(128, bn - sub * 128)
                    b = bi % 8
                    bi += 1
                    ps = banks[b]
                    tg = mm_group(k, ps[:m, :CWB], lambda kc, At=At, sub=sub, m=m: At[:, kc, sub * 128:sub * 128 + m],
                                  lambda kc, Wt=Wt: Wt[:, kc, :], nk, [ta, tw, bank_free[b]])
                    wreads.append(tg)
                    ms = mi % 2
                    mi += 1
                    te = k.op("act", lambda e, ps=ps, ms=ms, m=m: e.activation(msts[ms][:m, :], ps[:m, :CWB], AF.Copy), deps=[tg, mst_free[ms]])
                    bank_free[b] = [te]
                    r0 = tb0 + sub * 128
                    mst_free[ms] = [k.dma("sp", out_dram[r0:r0 + m, cb * CWB:(cb + 1) * CWB], msts[ms][:m, :], deps=[te], sem=msems[ms])]
                wr.free[ws] = wreads
                areads += wreads
            ar.free[as_] = areads


def phase_ffn_up(k, cfg, T, PK):
    D, KC, TL, FC, DFF = cfg.D, cfg.KC, cfg.TL, cfg.FC, cfg.DFF
    NHALF = 2
    HL = TL // NHALF
    L = HL + 2
    atiles = [(i * 512, min(512, L - i * 512)) for i in range((L + 511) // 512)]
    with k.phase("p5") as ph:
        h1Th = ph.sbuf("h1Th", [128, KC, L], BF16)
        wbufs = [ph.sbuf("w", [128, KC, 256], BF16) for _ in range(4)]
        wr = Ring(k, wbufs, "w")
        cw = ph.sbuf("cw", [128, 3, 2 * FC], F32)
        cb = ph.sbuf("cb", [128, 2 * FC], F32)
        cbuf = [[ph.sbuf("c", [128, L], F32) for _ in range(2)] for _ in range(2)]
        sgt = [ph.sbuf("sgt", [128, HL], F32) for _ in range(2)]
        pst = [ph.sbuf("pst", [128, HL], BF16) for _ in range(2)]
        banks = [ph.psum("ps", [128, 512], F32) for _ in range(8)]
        bank_free = [[] for _ in banks]
        cs = k.dsem("c")
        tcw = [k.dma("sp", cw[:], T["convw_pm"], sem=cs), k.dma("sp", cb[:], T["convb_pm"], sem=cs)]
        hsem = k.dsem("h")
        psems = [k.dsem("p") for _ in range(2)]
        slot_free = [[], []]
        h1T_v = T["h1T"].rearrange("(kc p) s -> p kc s", p=128)
        w_v = T["w_up"].rearrange("(kc p) n -> p kc n", p=128)
        pT_v = T["pT"].rearrange("(c p) s -> p c s", p=128)
        half_reads = []
        bi = 0
        ji = 0
        sc_ = k.dsem("pkC", barrier=False)
        wd_v = T["w_down"].rearrange("(kc p) n -> p kc n", p=128)
        wd_blocks = list(range(D // 256))
        for hf in range(NHALF):
            if hf == 0:
                pieces = [(0, TL, 1), (1, 0, HL + 1)]
            else:
                pieces = [(0, HL * hf - 1, HL + 1), (HL + 1, TL + 1, 1)]
            if NHALF > 2:
                raise NotImplementedError
            th = [k.dma("sp", h1Th[:, :, d0:d0 + n], h1T_v[:, :, s0:s0 + n], deps=half_reads, sem=hsem, slow=(n == 1))
                  for (d0, s0, n) in pieces]
            half_reads = []
            for jp in range(FC // 2):
                sg_, sv_ = wr.take(), wr.take()
                Wg, Wv = wbufs[sg_], wbufs[sv_]
                tg_ = k.dma("pool", Wg[:], w_v[:, :, jp * 256:(jp + 1) * 256], deps=wr.free[sg_], sem=wr.sems[sg_])
                tv_ = k.dma("pool", Wv[:], w_v[:, :, DFF + jp * 256:DFF + (jp + 1) * 256], deps=wr.free[sv_], sem=wr.sems[sv_])
                if wd_blocks and (hf * (FC // 2) + jp) % 5 == 4:
                    wb = wd_blocks.pop(0)
                    k.dma("pool", T["wdn_pk"][wb], wd_v[:, :, wb * 256:(wb + 1) * 256], sem=sc_)
                rdg, rdv = [], []
                for j in range(2):
                    jj = jp * 2 + j
                    sl = ji % 2
                    ji += 1
                    js = slice(j * 128, (j + 1) * 128)
                    lastc = []
                    for ci, (Wt, tw, rd, cidx) in enumerate(((Wg, tg_, rdg, jj), (Wv, tv_, rdv, FC + jj))):
                        cbf = cbuf[sl][ci]
                        tl_ = []
                        for (a0, n) in atiles:
                            b = bi % 8
                            bi += 1
                            ps = banks[b]
                            tg = mm_group(k, ps[:, :n], lambda kc, Wt=Wt, js=js: Wt[:, kc, js], lambda kc, a0=a0, n=n: h1Th[:, kc, a0:a0 + n], KC,
                                          [tw, th, bank_free[b]])
                            rd.append(tg)
                            tl_.append((a0, n, b, ps, tg))
                        half_reads.append(tl_[-1][4])
                        inits = []
                        for (a0, n, b, ps, tg) in tl_:
                            lo1, hi1 = max(a0, 1), min(a0 + n, L - 1)
                            if hi1 > lo1:
                                inits.append(k.op("act", lambda e, ps=ps, cbf=cbf, a0=a0, lo1=lo1, hi1=hi1, cidx=cidx: e.activation(
                                    cbf[:, lo1:hi1], ps[:, lo1 - a0:hi1 - a0], AF.Identity, bias=cb[:, cidx:cidx + 1], scale=cw[:, 1, cidx:cidx + 1]),
                                    deps=[tg, tcw, slot_free[sl]]))
                            else:
                                inits.append(None)
                        for ti_, (a0, n, b, ps, tg) in enumerate(tl_):
                            last = inits[ti_]
                            e0 = min(a0 + n, L - 2)
                            if e0 > a0:
                                last = k.op("dve", lambda e, ps=ps, cbf=cbf, a0=a0, e0=e0, cidx=cidx: e.scalar_tensor_tensor(
                                    cbf[:, a0 + 1:e0 + 1], ps[:, 0:e0 - a0], cw[:, 0, cidx:cidx + 1], cbf[:, a0 + 1:e0 + 1], ALU.mult, ALU.add),
                                    deps=[tg, inits, last])
                            s0 = max(a0, 2)
                            if a0 + n > s0:
                                last = k.op("dve", lambda e, ps=ps, cbf=cbf, a0=a0, n=n, s0=s0, cidx=cidx: e.scalar_tensor_tensor(
                                    cbf[:, s0 - 1:a0 + n - 1], ps[:, s0 - a0:n], cw[:, 2, cidx:cidx + 1], cbf[:, s0 - 1:a0 + n - 1], ALU.mult, ALU.add),
                                    deps=[tg, inits, last])
                            bank_free[b] = [last, inits[ti_]]
                            lastc.append(last)
                    cg, cv = cbuf[sl]
                    t_s = k.op("act", lambda e, cg=cg, sl=sl: e.activation(sgt[sl][:, :], cg[:, 1:L - 1], AF.Silu), deps=[lastc])
                    t_p = k.op("dve", lambda e, cv=cv, sl=sl: e.tensor_tensor(pst[sl][:, :], sgt[sl][:, :], cv[:, 1:L - 1], ALU.mult), deps=[t_s, lastc])
                    slot_free[sl] = [k.dma("sp", pT_v[:, jj, hf * HL:(hf + 1) * HL], pst[sl][:, :], deps=[t_p], sem=psems[sl]), t_p]
                wr.free[sg_] = rdg
                wr.free[sv_] = rdv
        for wb in wd_blocks:
            k.dma("pool", T["wdn_pk"][wb], wd_v[:, :, wb * 256:(wb + 1) * 256], sem=sc_)
        PK["C"] = Tok(sc_, sc_.count)


def build(cfg):
    nc = bass.Bass("TRN2", target_bir_lowering=False)
    D, S, AQ, NLOC, NTOK, DFF, TL = cfg.D, cfg.S, cfg.AQ, cfg.NLOC, cfg.NTOK, cfg.DFF, cfg.TL
    T = {}

    def inp(name, shape, dt=F32):
        T[name] = nc.dram_tensor(name, list(shape), dt, kind="ExternalInput").ap()

    def scr(name, shape, dt):
        kind = "ExternalOutput" if name in cfg.debug else "Internal"
        T[name] = nc.dram_tensor(name, list(shape), dt, kind=kind).ap()

    inp("xcat", [NTOK, D])
    inp("lng_rep", [128, D]); inp("lnb_rep", [128, D])
    inp("ident", [128, 128], BF16)
    scr("hT", [D, NTOK], BF16)
    scr("hloc", [NLOC, D], F32)
    T["out"] = nc.dram_tensor("out", [TL, D], F32, kind="ExternalOutput").ap()

    NH = cfg.NH
    inp("w_in", [D, 4 * AQ])
    inp("perm32", [128, 128], BF16)
    inp("ropek_cos", [32, S]); inp("ropek_sin", [32, S])
    inp("ropeq_cos", [32, NLOC]); inp("ropeq_sin", [32, NLOC])
    inp("csg", [256, 512], BF16)
    scr("kT", [2 * NH, 128, S], BF16)
    scr("v", [S, AQ], BF16)
    scr("z", [NH, 2, 128, S // 128, 256], BF16)
    scr("qT", [2 * NH, 128, NLOC], BF16)
    CW = min(512, AQ)
    inp("subg_rep", [128, 256]); inp("lam_rep", [128, 4, 128])
    scr("onT", [AQ, NLOC], BF16)
    NST = TL // 256 + 1
    inp("dftc", [NST, 128, S // 128, 256], BF16); inp("dfts", [NST, 128, S // 128, 256], BF16)
    scr("yfT", [AQ, NLOC], BF16)
    inp("w_attn_o", [AQ, D]); inp("w_fourier", [AQ, D]); inp("w_gate", [D, 2 * D]); inp("w_mix", [D, D])
    inp("bgate_pm", [128, 2 * D // 128])
    inp("ln1g_rep", [128, D]); inp("ln1b_rep", [128, D]); inp("hmask", [128, 1])
    scr("zT", [D, NLOC], BF16)
    scr("msc", [NLOC, D], F32)
    scr("h1", [NLOC, D], F32)
    scr("h1T", [D, NLOC], BF16)
    inp("w_up", [D, 2 * DFF]); inp("w_down", [DFF, D])
    inp("convw_pm", [128, 3, 2 * DFF // 128]); inp("convb_pm", [128, 2 * DFF // 128])
    inp("ln2g_rep", [128, D]); inp("ln2b_rep", [128, D])
    scr("pT", [DFF, TL], BF16)
    scr("wao_pk", [D // 256, 128, AQ // 128, 256], BF16); scr("wf_pk", [D // 256, 128, AQ // 128, 256], BF16)
    scr("wg_pk", [2 * D // 256, 128, D // 128, 256], BF16)
    scr("wmix_pk", [D // min(512, D), 128, D // 128, min(512, D)], BF16)
    scr("wdn_pk", [D // 256, 128, DFF // 128, 256], BF16)
    scr("fsc", [TL, D], F32)

    with ExitStack() as es:
        k = KB(nc, es)
        phase_ln(k, cfg, T, "p0", T["xcat"], NTOK, T["lng_rep"], T["lnb_rep"], out_f=(T["hloc"], S, 0), out_T=T["hT"], group_break=S)
        if cfg.upto >= 1:
            jobs = [dict(kind="rope", w=T["w_in"], c0=AQ, nblk=AQ // CW, out=T["kT"]),
                    dict(kind="tm", w=T["w_in"], c0=2 * AQ, nblk=AQ // CW, out=T["v"]),
                    dict(kind="fmf", w=T["w_in"], c0=3 * AQ, nblk=AQ // CW, out=None)]
            sel = os.environ.get("P1SEL")
            if sel:
                jobs = [j for j in jobs if j["kind"] in sel.split(",")]
            phase_proj(k, cfg, T, "p1", 0, S, 2, jobs, T["ropek_cos"], T["ropek_sin"])
        if cfg.upto >= 2:
            jobs = [dict(kind="rope", w=T["w_in"], c0=0, nblk=AQ // CW, out=T["qT"])]
            phase_proj(k, cfg, T, "p2", S, NLOC, 2, jobs, T["ropeq_cos"], T["ropeq_sin"])
        PK = {}
        if cfg.upto >= 3:
            phase_attn(k, cfg, T, PK)
        if cfg.upto >= 4:
            phase_fourier(k, cfg, T)
        if cfg.upto >= 5:
            phase_gate(k, cfg, T, PK)
        if cfg.upto >= 6:
            phase_tm_matmul(k, cfg, T, "p4b", T["zT"], cfg.KC, NLOC, T["wmix_pk"], T["msc"], nslots=3, CWB=min(512, D), wdep=PK["B"])
            phase_ln(k, cfg, T, "p4c", T["msc"], NLOC, T["ln1g_rep"], T["ln1b_rep"], res=T["hloc"], out_f=(T["h1"], 0, 0),
                     out_T=T["h1T"], mask_in=T["hmask"])
        if cfg.upto >= 7:
            phase_ffn_up(k, cfg, T, PK)
        if cfg.upto >= 8:
            phase_tm_matmul(k, cfg, T, "p6", T["pT"], cfg.FC, TL, T["wdn_pk"], T["fsc"], TB=512, nslots=2, wdep=PK["C"])
        if cfg.upto >= 9:
            phase_ln(k, cfg, T, "p7", T["fsc"], TL, T["ln2g_rep"], T["ln2b_rep"], res=T["h1"], out_f=(T["out"], 0, 0))
        with k.phase("fin"):
            k.wait_only("sp", [])
    return nc


def host_inputs(cfg, inputs):
    D, S, TL, NLOC = cfg.D, cfg.S, cfg.TL, cfg.NLOC
    x = np.asarray(inputs["x"], dtype=np.float32)
    maps = []
    common = {
        "lng_rep": np.ascontiguousarray(np.broadcast_to(np.asarray(inputs["ln_emb_g"], np.float32)[None, :], (128, D))),
        "lnb_rep": np.ascontiguousarray(np.broadcast_to(np.asarray(inputs["ln_emb_b"], np.float32)[None, :], (128, D))),
        "ident": np.eye(128, dtype=np.float32).astype(ml_dtypes.bfloat16),
    }
    bf = ml_dtypes.bfloat16
    f32c = lambda a: np.ascontiguousarray(np.asarray(a, np.float32))
    common["w_in"] = f32c(inputs["w_in"][0])
    perm = np.zeros((128, 128), np.float32)
    for j in range(16):
        perm[j + 16, j] = 1.0
        perm[j, j + 16] = 1.0
    common["perm32"] = perm.astype(bf)
    inv_freq = (np.float32(ROPE_THETA) ** (-np.arange(0, 32, 2, dtype=np.float32) / np.float32(32))).astype(np.float32)

    def rope_tabs(pos):
        ang = (pos.astype(np.float32)[None, :] * inv_freq[:, None]).astype(np.float32).astype(np.float64)
        cos = np.concatenate([np.cos(ang), np.cos(ang)], 0)
        sin = np.concatenate([-np.sin(ang), np.sin(ang)], 0)
        return f32c(cos), f32c(sin)

    common["ropek_cos"], common["ropek_sin"] = rope_tabs(np.arange(S))
    cg = np.arange(256)[:, None] * np.arange(256)[None, :] % 256
    ang = 2.0 * np.pi * cg / 256.0
    common["csg"] = np.concatenate([np.cos(ang), -np.sin(ang)], 1).astype(np.float32).astype(bf)
    rep = lambda v: np.ascontiguousarray(np.broadcast_to(np.asarray(v, np.float32)[None], (128,) + np.asarray(v).shape))
    common["subg_rep"] = rep(inputs["subln_g"][0])
    common["lam_rep"] = rep(np.stack([inputs["lambda_q1"][0], inputs["lambda_k1"][0], inputs["lambda_q2"][0], inputs["lambda_k2"][0]], 0))
    pm = lambda v: np.ascontiguousarray(np.asarray(v, np.float32).reshape(-1, 128).T)
    common["w_attn_o"] = f32c(inputs["w_attn_o"][0]); common["w_fourier"] = f32c(inputs["w_fourier"][0])
    common["w_gate"] = f32c(inputs["w_gate"][0]); common["w_mix"] = f32c(inputs["w_mix_out"][0])
    common["bgate_pm"] = pm(inputs["b_gate"][0])
    common["ln1g_rep"] = rep(inputs["ln1_g"][0]); common["ln1b_rep"] = rep(inputs["ln1_b"][0])
    common["ln2g_rep"] = rep(inputs["ln2_g"][0]); common["ln2b_rep"] = rep(inputs["ln2_b"][0])
    common["w_up"] = f32c(inputs["w_up"][0]); common["w_down"] = f32c(inputs["w_down"][0])
    cwv = np.asarray(inputs["conv_w"][0], np.float32)
    common["convw_pm"] = np.ascontiguousarray(cwv.reshape(3, -1, 128).transpose(2, 0, 1))
    common["convb_pm"] = pm(inputs["conv_b"][0])
    for c in range(N_CORES):
        b, qi = c // 4, c % 4
        t0 = qi * TL
        iL = t0 - 1 if t0 - 1 >= 0 else t0
        iR = t0 + TL if t0 + TL < S else t0 + TL - 1
        xcat = np.concatenate([x[b], x[b, t0:t0 + TL], x[b, iL:iL + 1], x[b, iR:iR + 1]], axis=0)
        m = dict(common)
        m["xcat"] = np.ascontiguousarray(xcat)
        lpos = np.concatenate([np.arange(t0, t0 + TL), [iL, iR]])
        m["ropeq_cos"], m["ropeq_sin"] = rope_tabs(lpos)
        hm = np.zeros((128, 1), np.float32)
        hm[0, 0] = 1.0 if t0 - 1 >= 0 else 0.0
        hm[1, 0] = 1.0 if t0 + TL < S else 0.0
        m["hmask"] = hm
        NST = TL // 256 + 1
        lp = np.zeros(NST * 256, np.int64)
        lp[:NLOC] = lpos
        prod = (np.arange(S, dtype=np.int64)[:, None] * lp[None, :]) % S
        ang = prod.astype(np.float64) * (2.0 * np.pi / S)
        lay = lambda a: np.ascontiguousarray(a.astype(np.float32).astype(bf).reshape(S // 128, 128, NST, 256).transpose(2, 1, 0, 3))
        m["dftc"] = lay(np.cos(ang))
        m["dfts"] = lay(np.sin(ang))
        maps.append(m)
    return maps


def kernel(**inputs):
    cfg = Cfg()
    nc = build(cfg)
    maps = host_inputs(cfg, inputs)
    res = run_bass_kernel_spmd(nc, maps, core_ids=list(range(N_CORES)))
    out = np.empty((2, cfg.S, cfg.D), np.float32)
    for c in range(N_CORES):
        b, qi = c // 4, c % 4
        out[b, qi * cfg.TL:(qi + 1) * cfg.TL] = res.results[c]["out"]
    return out
```

```python
import math
import os
from contextlib import ExitStack, contextmanager

import numpy as np
import ml_dtypes

import concourse.bass as bass
import concourse.mybir as mybir
from concourse.bass_utils import run_bass_kernel_spmd

F32 = mybir.dt.float32
BF16 = mybir.dt.bfloat16
ALU = mybir.AluOpType
AF = mybir.ActivationFunctionType
AX = mybir.AxisListType

LN_EPS = 1e-5
ROPE_THETA = 500000.0
N_CORES = 8


class Cfg:
    def __init__(self, D=4096, S=8192, DFF=11008, upto=99, debug=()):
        self.D, self.S, self.DFF = D, S, DFF
        self.KC = D // 128
        self.NH = D // 512
        self.AQ = self.NH * 256
        self.TL = S // 4
        self.NLOC = self.TL + 2
        self.NTOK = S + self.NLOC
        self.FC = DFF // 128
        self.alpha = 2.0 ** 0.25
        self.lambda_init = 0.8 - 0.6 * math.exp(0.0)
        self.upto = upto
        self.debug = tuple(debug)


class Sem:
    def __init__(self, h, inc):
        self.h, self.inc, self.count = h, inc, 0


class Tok:
    __slots__ = ("sem", "val")

    def __init__(self, sem, val):
        self.sem, self.val = sem, val


ENGS = ("pe", "act", "dve", "pool", "sp")


def _flat(deps):
    out = []
    for d in deps:
        if d is None:
            continue
        if isinstance(d, (list, tuple)):
            out.extend(_flat(d))
        else:
            out.append(d)
    return out


class KB:
    def __init__(self, nc, es):
        self.nc, self.es = nc, es
        self.q = {e: [] for e in ENGS}
        self.waited = {e: {} for e in ENGS}
        self.esem = {}
        self.nsem = 0
        self.dsems = []
        self.bar = []
        self.last = {e: None for e in ENGS}
        self._new_esems()

    def _sem(self, name, inc):
        self.nsem += 1
        h = self.es.enter_context(self.nc.semaphore(f"{name}_{self.nsem}"))
        return Sem(h, inc)

    def _new_esems(self):
        for e in ("pe", "act", "dve", "pool"):
            self.esem[e] = self._sem("e" + e, 1)

    def dsem(self, name="d", barrier=True):
        s = self._sem(name, 16)
        if barrier:
            self.dsems.append(s)
        return s

    def _waits(self, eng, deps):
        waits = []
        w = self.waited[eng]
        best = {}
        for t in _flat(list(deps) + self.bar):
            if id(t.sem) not in best or best[id(t.sem)].val < t.val:
                best[id(t.sem)] = t
        for t in best.values():
            if w.get(id(t.sem), 0) >= t.val:
                continue
            w[id(t.sem)] = t.val
            waits.append(t)
        return waits

    def op(self, eng, fn, deps=(), sig=True):
        waits = self._waits(eng, deps)
        tok = None
        if sig:
            s = self.esem[eng]
            s.count += 1
            tok = Tok(s, s.count)
            self.last[eng] = tok
        self.q[eng].append((waits, fn, tok))
        return tok

    def dma(self, q, out, in_, deps=(), sem=None, slow=False):
        waits = self._waits(q, deps)
        sem.count += 16
        tok = Tok(sem, sem.count)
        if slow:
            fn = lambda e, o=out, i=in_: e.dma_start(out=o, in_=i, allow_slow_non_contiguous=True)
        else:
            fn = lambda e, o=out, i=in_: e.dma_start(out=o, in_=i)
        self.q[q].append((waits, fn, tok))
        return tok

    def wait_only(self, eng, deps):
        waits = self._waits(eng, deps)
        self.q[eng].append((waits, None, None))

    def barrier(self):
        toks = [t for t in self.last.values() if t is not None]
        toks += [Tok(s, s.count) for s in self.dsems if s.count > 0]
        self.bar = toks

    def flush(self):
        nc = self.nc

        def emit(eng, e):
            for waits, fn, tok in self.q[eng]:
                for w in waits:
                    e.wait_ge(w.sem.h, w.val)
                if fn is None:
                    continue
                ins = fn(e)
                if tok is not None:
                    ins.then_inc(tok.sem.h, tok.sem.inc)
            self.q[eng] = []

        with nc.Block() as blk:
            @blk.tensor
            def _(e):
                emit("pe", e)

            @blk.scalar
            def _(e):
                emit("act", e)

            @blk.vector
            def _(e):
                emit("dve", e)

            @blk.gpsimd
            def _(e):
                emit("pool", e)

            @blk.sync
            def _(e):
                emit("sp", e)

    @contextmanager
    def phase(self, name):
        ph = Phase(self, name)
        with ExitStack() as pes:
            ph.es = pes
            yield ph
            self.barrier()
            self.flush()


class Phase:
    def __init__(self, k, name):
        self.k, self.name, self.es, self.n = k, name, None, 0

    def sbuf(self, name, shape, dt):
        self.n += 1
        return self.es.enter_context(self.k.nc.sbuf_tensor(f"{self.name}_{name}_{self.n}", list(shape), dt))

    def psum(self, name, shape, dt):
        self.n += 1
        return self.es.enter_context(self.k.nc.psum_tensor(f"{self.name}_{name}_{self.n}", list(shape), dt))


class Ring:
    def __init__(self, k, bufs, name="r"):
        self.k, self.bufs = k, bufs
        self.n = len(bufs)
        self.sems = [k.dsem(name) for _ in bufs]
        self.free = [[] for _ in bufs]
        self.i = 0

    def take(self):
        s = self.i % self.n
        self.i += 1
        return s


def transpose_tile(k, cfg, hb, n, deps, ident, pst, pst_free, stg, col0, evac_eng_cycle):
    KC = cfg.KC
    toks = []
    ngrp = (KC + 7) // 8
    for g in range(ngrp):
        kcs = list(range(g * 8, min(KC, g * 8 + 8)))
        b = pst["i"] % len(pst["t"])
        pst["i"] += 1
        ps = pst["t"][b]
        tp = None
        for j, kc in enumerate(kcs):
            last = j == len(kcs) - 1
            tp = k.op("pe", lambda e, j=j, kc=kc, ps=ps: e.transpose(ps[:, j, :n], hb[:n, kc * 128:(kc + 1) * 128], ident[:n, :n]),
                      deps=(list(deps) + [pst_free[b]]) if j == 0 else (), sig=last)
        eng = evac_eng_cycle[g % len(evac_eng_cycle)]
        nk = len(kcs)
        if eng == "act":
            te = k.op("act", lambda e, ps=ps, k0=kcs[0], nk=nk: e.activation(stg[:, k0:k0 + nk, col0:col0 + n], ps[:, 0:nk, :n], AF.Copy),
                      deps=[tp])
        else:
            te = k.op("dve", lambda e, ps=ps, k0=kcs[0], nk=nk: e.tensor_copy(stg[:, k0:k0 + nk, col0:col0 + n], ps[:, 0:nk, :n]),
                      deps=[tp])
        pst_free[b] = te
        toks.append(te)
    return toks


def phase_ln(k, cfg, T, name, src, ntok, g_in, b_in, res=None, out_f=None, out_T=None, mask_in=None, group_break=None):
    D, KC = cfg.D, cfg.KC
    nch = D // 512
    with k.phase(name) as ph:
        g_rep = ph.sbuf("g", [128, D], F32)
        b_rep = ph.sbuf("b", [128, D], F32)
        xts = [ph.sbuf("xt", [128, D], F32) for _ in range(3)]
        stats = ph.sbuf("st", [128, 6 * nch], F32)
        mv = ph.sbuf("mv", [128, 2], F32)
        rstd = ph.sbuf("rs", [128, 1], F32)
        nmr = ph.sbuf("nm", [128, 1], F32)
        eps_t = ph.sbuf("eps", [128, 1], F32)
        t_eps = k.op("dve", lambda e: e.memset(eps_t[:], LN_EPS))
        cs = k.dsem("c")
        tc = [k.dma("sp", g_rep[:], g_in, sem=cs), k.dma("sp", b_rep[:], b_in, sem=cs)]
        if res is not None:
            rts = [ph.sbuf("rt", [128, D], F32) for _ in range(3)]
            rr = Ring(k, rts, "r")
        if out_T is not None:
            ident = ph.sbuf("id", [128, 128], BF16)
            tc.append(k.dma("sp", ident[:], T["ident"], sem=cs))
            hbs = [ph.sbuf("hb", [128, D], BF16) for _ in range(2)]
            hb_free = [[], []]
            nstg = 1 if res is not None else 2
            stgs = [ph.sbuf("stg", [128, KC, 512], BF16) for _ in range(nstg)]
            stg_free = [[] for _ in range(nstg)]
            sts = [k.dsem("st") for _ in range(nstg)]
            pst = {"t": [ph.psum("pt", [128, 8, 128], BF16) for _ in range(4)], "i": 0}
            pst_free = [None] * 4
            oT_v = out_T.rearrange("(kc p) s -> p kc s", p=128)
        if mask_in is not None:
            mask = ph.sbuf("mask", [128, 1], F32)
            tc.append(k.dma("sp", mask[:], mask_in, sem=cs))
        xr = Ring(k, xts, "x")
        fsems = [k.dsem("f") for _ in range(len(xts))]
        tiles = [(i * 128, 128) for i in range(ntok // 128)]
        if ntok % 128:
            tiles.append((ntok - ntok % 128, ntok % 128))
        gi = gcol = g0 = 0
        gtoks = []
        loads = {}

        def load(i):
            t0, n = tiles[i]
            s_ = xr.take()
            tl = [k.dma("sp", xts[s_][:n, :], src[t0:t0 + n, :], deps=xr.free[s_], sem=xr.sems[s_])]
            rs_ = None
            if res is not None:
                rs_ = rr.take()
                tl.append(k.dma("sp", rts[rs_][:n, :], res[t0:t0 + n, :], deps=rr.free[rs_], sem=rr.sems[rs_]))
            loads[i] = (s_, rs_, tl)

        R = len(xts)
        for i0 in range(min(R, len(tiles))):
            load(i0)
        sm2 = [(stats, mv, rstd, nmr),
               (ph.sbuf("st2", [128, 6 * nch], F32), ph.sbuf("mv2", [128, 2], F32), ph.sbuf("rs2", [128, 1], F32), ph.sbuf("nm2", [128, 1], F32))]
        stA = {}

        def stage_a(i):
            t0, n = tiles[i]
            s_, rs_, tl = loads.pop(i)
            xt = xts[s_]
            stats_, mv_, rstd_, nmr_ = sm2[i % 2]
            deps = [tl, t_eps, tc]
            if res is not None:
                rt = rts[rs_]
                t = k.op("dve", lambda e, xt=xt, rt=rt, n=n: e.scalar_tensor_tensor(xt[:n, :], rt[:n, :], cfg.alpha, xt[:n, :], ALU.mult, ALU.add),
                         deps=deps)
                rr.free[rs_] = [t]
                deps = [t]
            t = None
            for c in range(nch):
                t = k.op("dve", lambda e, c=c, xt=xt, n=n: e.bn_stats(stats_[:n, c * 6:(c + 1) * 6], xt[:n, c * 512:(c + 1) * 512]),
                         deps=deps if c == 0 else ())
            t = k.op("dve", lambda e, n=n: e.bn_aggr(mv_[:n, :], stats_[:n, :]), deps=[t])
            t = k.op("act", lambda e, n=n: e.activation(rstd_[:n, :], mv_[:n, 1:2], AF.Sqrt, bias=eps_t[:n, 0:1], scale=1.0), deps=[t])
            t = k.op("dve", lambda e, n=n: e.reciprocal(rstd_[:n, :], rstd_[:n, :]), deps=[t])
            t = k.op("dve", lambda e, n=n: e.tensor_scalar(nmr_[:n, :], mv_[:n, 0:1], rstd_[:n, 0:1], -1.0, ALU.mult, ALU.mult), deps=[t])
            t = k.op("act", lambda e, xt=xt, n=n: e.activation(xt[:n, :], xt[:n, :], AF.Identity, bias=nmr_[:n, 0:1], scale=rstd_[:n, 0:1]),
                     deps=[t])
            t = k.op("pool", lambda e, xt=xt, n=n: e.tensor_tensor(xt[:n, :], xt[:n, :], g_rep[:n, :], ALU.mult), deps=[t])
            stA[i] = (s_, xt, t)

        def stage_b(i):
            nonlocal gi, gcol, g0, gtoks
            t0, n = tiles[i]
            s_, xt, t = stA.pop(i)
            want_f = out_f is not None and t0 >= out_f[1]
            done = []
            if want_f or out_T is None:
                t_y = k.op("dve", lambda e, xt=xt, n=n: e.tensor_tensor(xt[:n, :], xt[:n, :], b_rep[:n, :], ALU.add), deps=[t])
                done.append(t_y)
            if want_f:
                r0 = t0 - out_f[1] + out_f[2]
                done.append(k.dma("pool", out_f[0][r0:r0 + n, :], xt[:n, :], deps=[t_y], sem=fsems[s_]))
            if out_T is not None:
                hb = hbs[i % 2]
                if want_f:
                    t_b = k.op("act", lambda e, xt=xt, hb=hb, n=n: e.activation(hb[:n, :], xt[:n, :], AF.Copy), deps=[t_y, hb_free[i % 2]])
                else:
                    t_b = k.op("dve", lambda e, xt=xt, hb=hb, n=n: e.tensor_tensor(hb[:n, :], xt[:n, :], b_rep[:n, :], ALU.add),
                               deps=[t, hb_free[i % 2]])
                if mask_in is not None and n < 128:
                    t_b = k.op("dve", lambda e, hb=hb, n=n: e.tensor_scalar(hb[:n, :], hb[:n, :], mask[:n, 0:1], None, ALU.mult), deps=[t_b])
                done.append(t_b)
                stg = stgs[gi % nstg]
                if gcol == 0:
                    g0 = t0
                    gtoks = []
                tt = transpose_tile(k, cfg, hb, n, [t_b] + (stg_free[gi % nstg] if gcol == 0 else []), ident, pst, pst_free,
                                    stg, gcol, ["act", "dve"])
                hb_free[i % 2] = tt
                gtoks += tt
                gcol += n
                nxt = tiles[i + 1] if i + 1 < len(tiles) else None
                if gcol == 512 or nxt is None or nxt[0] == group_break or nxt[1] != 128:
                    stg_free[gi % nstg] = [k.dma("pool", oT_v[:, :, g0:g0 + gcol], stg[:, :, 0:gcol], deps=gtoks, sem=sts[gi % nstg])]
                    gi += 1
                    gcol = 0
            xr.free[s_] = done

        stage_a(0)
        for i in range(len(tiles)):
            if i + 1 < len(tiles):
                stage_a(i + 1)
            stage_b(i)
            if i + R < len(tiles):
                load(i + R)


def mm_group(k, ps_ap, lhs_fn, rhs_fn, nk, deps):
    t = None
    for kc in range(nk):
        t = k.op("pe", lambda e, kc=kc: e.matmul(ps_ap, lhs_fn(kc), rhs_fn(kc), start=(kc == 0), stop=(kc == nk - 1)),
                 deps=deps if kc == 0 else (), sig=(kc == nk - 1))
    return t


def phase_proj(k, cfg, T, name, col0, ntok, TBN, jobs, rope_cos=None, rope_sin=None):
    D, KC, AQ = cfg.D, cfg.KC, cfg.AQ
    CW = min(512, AQ)
    NJ = CW // 128
    has_f = any(j["kind"] == "fmf" for j in jobs)
    has_rope = any(j["kind"] == "rope" for j in jobs)
    tiles = [(i * 512, min(512, ntok - i * 512)) for i in range((ntok + 511) // 512)]
    blocks = [tiles[i:i + TBN] for i in range(0, len(tiles), TBN)]
    BW = TBN * 512
    with k.phase(name) as ph:
        hTb = ph.sbuf("hTb", [128, KC, BW], BF16)
        wbufs = [ph.sbuf("w", [128, KC, CW], BF16) for _ in range(2)]
        wr = Ring(k, wbufs, "w")
        ksts = [ph.sbuf("kst", [128, 512], BF16) for _ in range(2)]
        kst_free = [[], []]
        kst_sems = [k.dsem("ks") for _ in range(2)]
        kst_i = 0
        banks = [ph.psum("ps", [128, 512], F32) for _ in range(6)]
        bank_free = [[] for _ in banks]
        bi = 0
        if has_rope:
            perm = ph.sbuf("perm", [128, 128], BF16)
            cs = k.dsem("c")
            t_perm = k.dma("sp", perm[:], T["perm32"], sem=cs)
            cosb = ph.sbuf("cos", [128, BW], F32)
            sinb = ph.sbuf("sin", [128, BW], F32)
            rsem = k.dsem("rt")
            t1f = [ph.sbuf("t1f", [128, 512], F32) for _ in range(2)]
            t2f = [ph.sbuf("t2f", [128, 512], F32) for _ in range(2)]
            tf_free = [[], []]
            rbanks = [ph.psum("pr", [128, 512], F32) for _ in range(2)]
            rbank_free = [[], []]
            ri = 0
        if has_f:
            NZ = 6
            zsts = [ph.sbuf("zst", [128, 512], BF16) for _ in range(NZ)]
            zst_free = [[] for _ in range(NZ)]
            zst_sems = [k.dsem("zs") for _ in range(NZ)]
            zi = 0
            ufT = ph.sbuf("ufT", [128, AQ // 128, BW], BF16)
            csg = ph.sbuf("csg", [128, 2, 512], BF16)
            cs2 = k.dsem("c2")
            t_csg = k.dma("sp", csg[:], T["csg"].rearrange("(cc p) n -> p cc n", p=128), sem=cs2)
        hsem = k.dsem("h")
        hT_v = T["hT"].rearrange("(kc p) s -> p kc s", p=128)
        blk_reads = []
        pending_tails = []
        tail_reads = []

        def run_tails():
            while pending_tails:
                pending_tails.pop(0)()

        def issue_block_loads(blk):
            b0 = blk[0][0]
            bn = sum(n for _, n in blk)
            t_h = k.dma("sp", hTb[:, :, 0:bn], hT_v[:, :, col0 + b0:col0 + b0 + bn], deps=blk_reads, sem=hsem)
            t_rt = []
            if has_rope:
                t_rt = [k.dma("sp", cosb[0:32, 0:bn], rope_cos[:, b0:b0 + bn], deps=blk_reads, sem=rsem),
                        k.dma("sp", sinb[0:32, 0:bn], rope_sin[:, b0:b0 + bn], deps=blk_reads, sem=rsem)]
            return t_h, t_rt

        pre = None
        for bidx, blk in enumerate(blocks):
            run_tails()
            blk_reads += tail_reads
            del tail_reads[:]
            b0 = blk[0][0]
            bn = sum(n for _, n in blk)
            if pre is None:
                t_h, t_rt = issue_block_loads(blk)
            else:
                t_h, t_rt = pre
                pre = None
            blk_reads = []
            uf_writes = []
            for job in jobs:
                w_v = job["w"].rearrange("(kc p) n -> p kc n", p=128)
                for cb in range(job["nblk"]):
                    c0 = job["c0"] + cb * CW
                    ws = wr.take()
                    wt = wbufs[ws]
                    t_w = k.dma("pool", wt[:], w_v[:, :, c0:c0 + CW], deps=wr.free[ws], sem=wr.sems[ws])
                    wreads = []
                    for (tt0, n) in blk:
                        lo = tt0 - b0
                        if job["kind"] in ("rope", "fmf"):
                            for j in range(NJ):
                                b = bi % len(banks); bi += 1
                                ps = banks[b]
                                tg = mm_group(k, ps[:, :n], lambda kc, j=j, wt=wt: wt[:, kc, j * 128:(j + 1) * 128],
                                              lambda kc, lo=lo, n=n: hTb[:, kc, lo:lo + n], KC, [t_w, t_h, bank_free[b]])
                                wreads.append(tg)
                                run_tails()
                                chunk = (c0 - job["c0"]) // 128 + j
                                if job["kind"] == "fmf":
                                    te = k.op("act", lambda e, ps=ps, chunk=chunk, lo=lo, n=n: e.activation(
                                        ufT[:, chunk, lo:lo + n], ps[:, :n], AF.Copy), deps=[tg, t_rt])
                                    bank_free[b] = [te]
                                    uf_writes.append(te)
                                else:
                                    si = kst_i % 2; kst_i += 1
                                    kst = ksts[si]
                                    ta = k.op("act", lambda e, ps=ps, kst=kst, n=n: e.activation(kst[:, :n], ps[:, :n], AF.Copy),
                                              deps=[tg, kst_free[si]])
                                    rb = ri % 2; ri += 1
                                    pr = rbanks[rb]
                                    f1, f2 = t1f[rb], t2f[rb]
                                    td1 = k.op("dve", lambda e, ps=ps, f1=f1, lo=lo, n=n: e.tensor_tensor(
                                        f1[0:32, :n], ps[0:32, :n], cosb[0:32, lo:lo + n], ALU.mult), deps=[tg, t_rt, tf_free[rb], ta])
                                    bank_free[b] = [ta, td1]
                                    blk_reads.append(td1)

                                    def tail(pr=pr, kst=kst, n=n, lo=lo, f1=f1, f2=f2, rb=rb, si=si, ta=ta, td1=td1, chunk=chunk, tt0=tt0, job=job):
                                        tr = k.op("pe", lambda e: e.matmul(pr[:, :n], perm[:, :], kst[:, :n], start=True, stop=True),
                                                  deps=[ta, t_perm, rbank_free[rb]])
                                        td2 = k.op("dve", lambda e: e.tensor_tensor(f2[0:32, :n], pr[0:32, :n], sinb[0:32, lo:lo + n], ALU.mult),
                                                   deps=[tr, t_rt])
                                        td3 = k.op("dve", lambda e: e.tensor_tensor(kst[0:32, :n], f1[0:32, :n], f2[0:32, :n], ALU.add),
                                                   deps=[td1, td2, ta])
                                        rbank_free[rb] = [td2]
                                        tf_free[rb] = [td3]
                                        tail_reads.append(td2)
                                        kst_free[si] = [k.dma("sp", job["out"][chunk, :, tt0:tt0 + n], kst[:, :n], deps=[td3], sem=kst_sems[si])]

                                    pending_tails.append(tail)
                        else:
                            nsub = (n + 127) // 128
                            for sub in range(nsub):
                                m = min(128, n - sub * 128)
                                b = bi % len(banks); bi += 1
                                ps = banks[b]
                                tg = mm_group(k, ps[:m, :CW], lambda kc, lo=lo, sub=sub, m=m: hTb[:, kc, lo + sub * 128:lo + sub * 128 + m],
                                              lambda kc, wt=wt: wt[:, kc, :], KC, [t_w, t_h, bank_free[b]])
                                wreads.append(tg)
                                run_tails()
                                si = kst_i % 2; kst_i += 1
                                kst = ksts[si]
                                ta = k.op("act", lambda e, ps=ps, kst=kst, m=m: e.activation(kst[:m, :CW], ps[:m, :CW], AF.Copy),
                                          deps=[tg, kst_free[si]])
                                bank_free[b] = [ta]
                                r0 = tt0 + sub * 128
                                tst = k.dma("sp", job["out"][r0:r0 + m, c0 - job["c0"]:c0 - job["c0"] + CW], kst[:m, :CW],
                                            deps=[ta], sem=kst_sems[si])
                                kst_free[si] = [tst]
                    wr.free[ws] = wreads
                    blk_reads += wreads
            run_tails()
            blk_reads += tail_reads
            del tail_reads[:]
            if has_f and bidx + 1 < len(blocks):
                pre = issue_block_loads(blocks[bidx + 1])
                blk_reads = []
            if has_f:
                NG = AQ // 256
                for (tt0, n) in blk:
                    lo = tt0 - b0
                    for sub in range((n + 127) // 128):
                        m = min(128, n - sub * 128)
                        for g in range(NG):
                            b = bi % len(banks); bi += 1
                            ps = banks[b]
                            tg = mm_group(k, ps[:m, :], lambda cc, g=g, lo=lo, sub=sub, m=m: ufT[:, 2 * g + cc, lo + sub * 128:lo + sub * 128 + m],
                                          lambda cc: csg[:, cc, :], 2, [uf_writes, t_csg, bank_free[b]])
                            blk_reads.append(tg)
                            si = zi % NZ; zi += 1
                            kst = zsts[si]
                            ta = k.op("act", lambda e, ps=ps, kst=kst, m=m: e.activation(kst[:m, :], ps[:m, :], AF.Copy, scale=1.0 / 16.0),
                                      deps=[tg, zst_free[si]])
                            bank_free[b] = [ta]
                            sc = (tt0 + sub * 128) // 128
                            kv = kst[:m, :].rearrange("p (ri h c) -> p ri h c", ri=2, h=2)
                            tst = [k.dma("sp", T["z"][g, h, 0:m, sc, :].rearrange("p (ri c) -> p ri c", ri=2), kv[:, :, h, :],
                                         deps=[ta], sem=zst_sems[si]) for h in range(2)]
                            zst_free[si] = tst


def issue_pack(k, cfg, T, PK):
    D = cfg.D

    def pack(sem, w_ap, pk_ap, CWB):
        w_v = w_ap.rearrange("(kc p) n -> p kc n", p=128)
        ncol = w_ap.shape[1]
        for b in range(ncol // CWB):
            k.dma("pool", pk_ap[b], w_v[:, :, b * CWB:(b + 1) * CWB], sem=sem)

    sa = k.dsem("pkA", barrier=False)
    pack(sa, T["w_attn_o"], T["wao_pk"], 256)
    pack(sa, T["w_fourier"], T["wf_pk"], 256)
    pack(sa, T["w_gate"], T["wg_pk"], 256)
    PK["A"] = Tok(sa, sa.count)
    sb = k.dsem("pkB", barrier=False)
    pack(sb, T["w_mix"], T["wmix_pk"], min(512, D))
    PK["B"] = Tok(sb, sb.count)


def phase_attn(k, cfg, T, PK):
    S, NH, NLOC, TL = cfg.S, cfg.NH, cfg.NLOC, cfg.TL
    SC = S // 128
    NKB = S // 512
    NG8 = SC // 8
    qtiles = [(i * 128, 128) for i in range(TL // 128)] + [(TL, NLOC - TL)]
    sm_scale = 128.0 ** -0.5
    with k.phase("p3") as ph:
        KT = ph.sbuf("KT", [128, 2, S], BF16)
        V = ph.sbuf("V", [128, SC, 256], BF16)
        QT = ph.sbuf("QT", [128, 2, NLOC], BF16)
        E1 = [ph.sbuf("e1", [128, S], BF16) for _ in range(2)]
        E2 = [ph.sbuf("e2", [128, S], BF16) for _ in range(2)]
        aT = [ph.sbuf("aT", [128, SC, 128], BF16) for _ in range(2)]
        rs = [ph.sbuf("rs", [128, 2, NKB], F32) for _ in range(2)]
        lsum = [ph.sbuf("l", [128, 2], F32) for _ in range(2)]
        rl = [ph.sbuf("rl", [128, 2], F32) for _ in range(2)]
        coef = [ph.sbuf("cf", [128, 1], F32) for _ in range(2)]
        o_sb = [ph.sbuf("o", [128, 256], F32) for _ in range(2)]
        junk = ph.sbuf("junk", [128, 256], F32)
        ss = [ph.sbuf("ss", [128, 1], F32) for _ in range(2)]
        rstd = [ph.sbuf("rstd", [128, 1], F32) for _ in range(2)]
        onb = [ph.sbuf("onb", [128, 256], BF16) for _ in range(2)]
        onst = [ph.sbuf("onst", [128, 2, 128], BF16) for _ in range(2)]
        ident = ph.sbuf("id", [128, 128], BF16)
        g_rep = ph.sbuf("g", [128, 256], F32)
        lamr = ph.sbuf("lamr", [128, 4, 128], F32)
        lprod = ph.sbuf("lprod", [128, 2, 128], F32)
        lsm = ph.sbuf("lsm", [128, 2], F32)
        lex = ph.sbuf("lex", [128, 2], F32)
        neg_lam = ph.sbuf("nlam", [128, 1], F32)
        eps_t = ph.sbuf("eps", [128, 1], F32)
        sbanks = [ph.psum("s", [128, 512], F32) for _ in range(4)]
        sbank_free = [[] for _ in sbanks]
        tbanks = [ph.psum("t", [128, 8, 128], BF16) for _ in range(2)]
        tbank_free = [[], []]
        po = ph.psum("po", [128, 256], F32)
        po_free = []
        pon = ph.psum("pon", [128, 2, 128], BF16)
        pon_free = []
        cs = k.dsem("c3")
        tc = [k.dma("sp", ident[:], T["ident"], sem=cs), k.dma("sp", g_rep[:], T["subg_rep"], sem=cs),
              k.dma("sp", lamr[:], T["lam_rep"], sem=cs)]
        t_eps = k.op("dve", lambda e: e.memset(eps_t[:], LN_EPS))
        t = k.op("dve", lambda e: e.tensor_tensor(lprod[:, 0, :], lamr[:, 0, :], lamr[:, 1, :], ALU.mult), deps=tc)
        t = k.op("dve", lambda e: e.tensor_tensor(lprod[:, 1, :], lamr[:, 2, :], lamr[:, 3, :], ALU.mult), deps=[t])
        t = k.op("dve", lambda e: e.tensor_reduce(lsm[:, :], lprod[:, :, :], AX.X, ALU.add), deps=[t])
        t = k.op("act", lambda e: e.activation(lex[:, :], lsm[:, :], AF.Exp), deps=[t])
        t = k.op("dve", lambda e: e.tensor_tensor(neg_lam[:, :], lex[:, 1:2], lex[:, 0:1], ALU.subtract), deps=[t])
        t_lam = k.op("dve", lambda e: e.tensor_scalar(neg_lam[:, :], neg_lam[:, :], -cfg.lambda_init, None, ALU.add), deps=[t])
        issue_pack(k, cfg, T, PK)
        hsem = k.dsem("kv")
        osems = [k.dsem("on") for _ in range(2)]
        kT_d, qT_d = T["kT"], T["qT"]
        v_v = T["v"].rearrange("(sc p) c -> p sc c", p=128)
        onT_v = T["onT"].rearrange("(c p) s -> p c s", p=128)
        head_reads = []
        E_free = [[], []]
        aT_free = [[], []]
        small_free = [[], []]
        rs_free = [[], []]
        onst_free = [[], []]
        st = {"bi": 0, "ti": 0, "po_free": [], "pon_free": []}
        nsteps = 2 * NKB

        for hd in range(NH):
            tl = [k.dma("sp", KT[:, 0, :], kT_d[2 * hd, :, :], deps=head_reads, sem=hsem),
                  k.dma("sp", KT[:, 1, :], kT_d[2 * hd + 1, :, :], deps=head_reads, sem=hsem),
                  k.dma("sp", QT[:, 0, :], qT_d[2 * hd, :, :], deps=head_reads, sem=hsem),
                  k.dma("sp", QT[:, 1, :], qT_d[2 * hd + 1, :, :], deps=head_reads, sem=hsem),
                  k.dma("sp", V[:, :, :], v_v[:, :, hd * 256:(hd + 1) * 256], deps=head_reads, sem=hsem)]
            head_reads = []
            acts = {}

            def qk_steps(ti_, lo, hi, tl=tl, acts=acts):
                q0, nq = qtiles[ti_]
                sl = ti_ % 2
                for stp in range(lo, hi):
                    c, kb = stp // NKB, stp % NKB
                    ec = E1[sl] if c == 0 else E2[sl]
                    b = st["bi"] % 4
                    st["bi"] += 1
                    ps = sbanks[b]
                    tm = k.op("pe", lambda e, ps=ps, c=c, kb=kb, q0=q0, nq=nq: e.matmul(
                        ps[:nq, :], QT[:, c, q0:q0 + nq], KT[:, c, kb * 512:(kb + 1) * 512], start=True, stop=True),
                        deps=[tl, sbank_free[b]])
                    ta = k.op("act", lambda e, ps=ps, ec=ec, kb=kb, c=c, nq=nq, sl=sl: e.activation(
                        ec[:nq, kb * 512:(kb + 1) * 512], ps[:nq, :], AF.Exp, scale=sm_scale,
                        accum_out=rs[sl][:nq, c, kb:kb + 1]), deps=[tm, E_free[sl], rs_free[sl]])
                    sbank_free[b] = [ta]
                    acts.setdefault(ti_, []).append(ta)
                    head_reads.append(tm)

            combs = {}

            def combine(tj, acts=acts, combs=combs):
                q0, nq = qtiles[tj]
                sl = tj % 2
                e1, e2 = E1[sl], E2[sl]
                t = k.op("dve", lambda e: e.tensor_reduce(lsum[sl][:nq, :], rs[sl][:nq, :, :], AX.X, ALU.add),
                         deps=[acts[tj], small_free[sl]])
                rs_free[sl] = [t]
                t = k.op("dve", lambda e: e.reciprocal(rl[sl][:nq, :], lsum[sl][:nq, :]), deps=[t])
                t = k.op("dve", lambda e: e.tensor_tensor(coef[sl][:nq, :], lsum[sl][:nq, 0:1], rl[sl][:nq, 1:2], ALU.mult), deps=[t])
                t = k.op("dve", lambda e: e.tensor_tensor(coef[sl][:nq, :], coef[sl][:nq, :], neg_lam[:nq, :], ALU.mult), deps=[t, t_lam])
                combs[tj] = k.op("dve", lambda e: e.scalar_tensor_tensor(
                    e1[:nq, :], e2[:nq, :], coef[sl][:nq, 0:1], e1[:nq, :], ALU.mult, ALU.add), deps=[t])

            nqt = len(qtiles)
            half = nsteps // 2
            qk_steps(0, 0, nsteps)
            if nqt > 1:
                qk_steps(1, 0, half)
            for ti_ in range(nqt):
                q0, nq = qtiles[ti_]
                sl = ti_ % 2
                e1, e2 = E1[sl], E2[sl]
                nxt = ti_ + 1 < nqt
                nxt2 = ti_ + 2 < nqt
                if ti_ == 0:
                    combine(0)
                t_comb = combs[ti_]
                evs = []
                tp = None
                for g in range(NG8):
                    if nxt:
                        lo = half + (nsteps - half) * g // NG8
                        hi = half + (nsteps - half) * (g + 1) // NG8
                        qk_steps(ti_ + 1, lo, hi)
                    tb = st["ti"] % 2
                    st["ti"] += 1
                    pt = tbanks[tb]
                    for j in range(8):
                        sc = g * 8 + j
                        tp = k.op("pe", lambda e, pt=pt, j=j, sc=sc, e1=e1, nq=nq: e.transpose(
                            pt[:, j, :nq], e1[:nq, sc * 128:(sc + 1) * 128], ident[:nq, :nq]),
                            deps=[t_comb, tbank_free[tb], aT_free[sl]] if j == 0 else (), sig=(j == 7))
                    te = k.op("dve", lambda e, pt=pt, g=g, sl=sl, nq=nq: e.tensor_copy(aT[sl][:, g * 8:(g + 1) * 8, :nq], pt[:, :, :nq]),
                              deps=[tp])
                    tbank_free[tb] = [te]
                    evs.append(te)
                E_free[sl] = [tp]
                tpv = None
                for stp in range(half):
                    if nxt2:
                        qk_steps(ti_ + 2, stp, stp + 1)
                    for sc in range(SC * stp // half, SC * (stp + 1) // half):
                        tpv = k.op("pe", lambda e, sc=sc, sl=sl, nq=nq: e.matmul(po[:nq, :], aT[sl][:, sc, :nq], V[:, sc, :],
                                                                               start=(sc == 0), stop=(sc == SC - 1)),
                                   deps=[evs, st["po_free"]] if sc == 0 else (), sig=(sc == SC - 1))
                aT_free[sl] = [tpv]
                head_reads.append(tpv)
                if nxt:
                    combine(ti_ + 1)
                t_o = k.op("dve", lambda e, sl=sl, nq=nq: e.tensor_scalar(o_sb[sl][:nq, :], po[:nq, :], rl[sl][:nq, 0:1], None, ALU.mult),
                           deps=[tpv])
                st["po_free"] = [t_o]
                t = k.op("dve", lambda e, sl=sl, nq=nq: e.scalar_tensor_tensor(junk[:nq, :], o_sb[sl][:nq, :], 1.0, o_sb[sl][:nq, :],
                                                                               ALU.mult, ALU.mult, accum_out=ss[sl][:nq, 0:1]), deps=[t_o])
                t = k.op("act", lambda e, sl=sl, nq=nq: e.activation(rstd[sl][:nq, :], ss[sl][:nq, :], AF.Ln, bias=eps_t[:nq, 0:1],
                                                                     scale=1.0 / 256.0), deps=[t, t_eps])
                t = k.op("act", lambda e, sl=sl, nq=nq: e.activation(rstd[sl][:nq, :], rstd[sl][:nq, :], AF.Exp, scale=-0.5), deps=[t])
                t = k.op("dve", lambda e, sl=sl, nq=nq: e.tensor_scalar(rstd[sl][:nq, :], rstd[sl][:nq, :], 1.0 - cfg.lambda_init, None, ALU.mult),
                         deps=[t])
                t_on = k.op("dve", lambda e, sl=sl, nq=nq: e.scalar_tensor_tensor(
                    onb[sl][:nq, :], o_sb[sl][:nq, :], rstd[sl][:nq, 0:1], g_rep[:nq, :], ALU.mult, ALU.mult), deps=[t, tc])
                tp2 = None
                for h in range(2):
                    tp2 = k.op("pe", lambda e, h=h, sl=sl, nq=nq: e.transpose(pon[:, h, :nq], onb[sl][:nq, h * 128:(h + 1) * 128], ident[:nq, :nq]),
                               deps=[t_on, st["pon_free"]] if h == 0 else (), sig=(h == 1))
                te = k.op("dve", lambda e, sl=sl, nq=nq: e.tensor_copy(onst[sl][:, :, :nq], pon[:, :, :nq]), deps=[tp2, onst_free[sl]])
                st["pon_free"] = [te]
                small_free[sl] = [te]
                onst_free[sl] = [k.dma("sp", onT_v[:, 2 * hd:2 * hd + 2, q0:q0 + nq], onst[sl][:, :, :nq], deps=[te], sem=osems[sl])]


def phase_fourier(k, cfg, T):
    S, NH, NLOC, TL = cfg.S, cfg.NH, cfg.NLOC, cfg.TL
    SC = S // 128
    W = 256
    stiles = [(i * W, W) for i in range(TL // W)] + [(TL, NLOC - TL)]
    with k.phase("p3b") as ph:
        CTs = [ph.sbuf("CT", [128, SC, W], BF16) for _ in range(2)]
        STs = [ph.sbuf("ST", [128, SC, W], BF16) for _ in range(2)]
        zb = [ph.sbuf("zs", [128, SC, 256], BF16) for _ in range(2)]
        zr = Ring(k, zb, "z")
        ysts = [ph.sbuf("yst", [128, W], BF16) for _ in range(2)]
        yst_free = [[], []]
        ysems = [k.dsem("y") for _ in range(2)]
        banks = [ph.psum("ps", [128, 512], F32) for _ in range(4)]
        bank_free = [[] for _ in banks]
        tsems = [k.dsem("t") for _ in range(2)]
        tab_free = [[], []]
        bi = 0
        yi = 0
        yfT_v = T["yfT"].rearrange("(c p) s -> p c s", p=128)

        def load_tabs(sti):
            sl = sti % 2
            return [k.dma("sp", CTs[sl][:, :, :], T["dftc"][sti], deps=tab_free[sl], sem=tsems[sl]),
                    k.dma("sp", STs[sl][:, :, :], T["dfts"][sti], deps=tab_free[sl], sem=tsems[sl])]

        tabs = {0: load_tabs(0)}
        for sti, (w0, n) in enumerate(stiles):
            if sti + 1 < len(stiles):
                tabs[sti + 1] = load_tabs(sti + 1)
            tt = tabs.pop(sti)
            CT, ST = CTs[sti % 2], STs[sti % 2]
            tab_reads = []
            for g in range(NH):
                for h in range(2):
                    zs = zr.take()
                    Z = zb[zs]
                    tz = k.dma("sp", Z[:], T["z"][g, h], deps=zr.free[zs], sem=zr.sems[zs])
                    b = bi % 4
                    bi += 1
                    ps = banks[b]
                    t = None
                    for sc in range(SC):
                        k.op("pe", lambda e, ps=ps, Z=Z, sc=sc, n=n, CT=CT: e.matmul(ps[:, :n], Z[:, sc, 0:128], CT[:, sc, :n], start=(sc == 0), stop=False),
                             deps=[tz, tt, bank_free[b]] if sc == 0 else (), sig=False)
                        t = k.op("pe", lambda e, ps=ps, Z=Z, sc=sc, n=n, ST=ST: e.matmul(ps[:, :n], Z[:, sc, 128:256], ST[:, sc, :n], start=False,
                                                                                 stop=(sc == SC - 1)), sig=(sc == SC - 1))
                    zr.free[zs] = [t]
                    tab_reads.append(t)
                    ys = yi % 2
                    yi += 1
                    ta = k.op("act", lambda e, ps=ps, ys=ys, n=n: e.activation(ysts[ys][:, :n], ps[:, :n], AF.Copy, scale=float(S) ** -0.5),
                              deps=[t, yst_free[ys]])
                    bank_free[b] = [ta]
                    yst_free[ys] = [k.dma("pool", yfT_v[:, 2 * g + h, w0:w0 + n], ysts[ys][:, :n], deps=[ta], sem=ysems[ys])]
            tab_free[sti % 2] = tab_reads


def phase_gate(k, cfg, T, PK):
    D, KC, AQ, NLOC, S = cfg.D, cfg.KC, cfg.AQ, cfg.NLOC, cfg.S
    AK = AQ // 128
    DBW = 256
    NDB = D // DBW
    tiles = [(i * 512, min(512, NLOC - i * 512)) for i in range((NLOC + 511) // 512)]
    TW = 512
    if len(tiles) >= 2 and tiles[-1][1] <= 8:
        tl_ = tiles.pop()
        tiles[-1] = (tiles[-1][0], tiles[-1][1] + tl_[1])
        TW = 512 + tl_[1]
    with k.phase("p4a") as ph:
        hTt = ph.sbuf("hTt", [128, KC, TW], BF16)
        onTt = ph.sbuf("onTt", [128, AK, TW], BF16)
        yfTt = ph.sbuf("yfTt", [128, AK, TW], BF16)
        wbufs = [ph.sbuf("w", [128, KC, DBW], BF16) for _ in range(6)]
        wr = Ring(k, wbufs, "w")
        bg = ph.sbuf("bg", [128, 2 * D // 128], F32)
        sg = [[ph.sbuf("sg", [128, 512], F32) for _ in range(2)] for _ in range(2)]
        tt12 = [[ph.sbuf("t12", [128, 512], F32) for _ in range(2)] for _ in range(2)]
        zst = [ph.sbuf("zst", [128, 512], BF16) for _ in range(2)]
        banks = [ph.psum("ps", [128, 512], F32) for _ in range(8)]
        bank_free = [[] for _ in banks]
        cs = k.dsem("c")
        t_bg = k.dma("sp", bg[:], T["bgate_pm"], sem=cs)
        isem = k.dsem("i")
        zsems = [k.dsem("z") for _ in range(2)]
        slot_free = [[], []]
        hT_v = T["hT"].rearrange("(kc p) s -> p kc s", p=128)
        on_v = T["onT"].rearrange("(kc p) s -> p kc s", p=128)
        yf_v = T["yfT"].rearrange("(kc p) s -> p kc s", p=128)
        zT_v = T["zT"].rearrange("(kc p) s -> p kc s", p=128)
        wao_v = T["w_attn_o"].rearrange("(kc p) n -> p kc n", p=128)
        wf_v = T["w_fourier"].rearrange("(kc p) n -> p kc n", p=128)
        wg_v = T["w_gate"].rearrange("(kc p) n -> p kc n", p=128)
        tile_reads = []
        di = 0
        for (tt0_, n_) in tiles:
            tin = [k.dma("sp", hTt[:, :, :n_], hT_v[:, :, S + tt0_:S + tt0_ + n_], deps=tile_reads, sem=isem),
                   k.dma("sp", onTt[:, :, :n_], on_v[:, :, tt0_:tt0_ + n_], deps=tile_reads, sem=isem),
                   k.dma("sp", yfTt[:, :, :n_], yf_v[:, :, tt0_:tt0_ + n_], deps=tile_reads, sem=isem)]
            tile_reads = []
            subs = [(o, min(512, n_ - o)) for o in range(0, n_, 512)]
            for db in range(NDB):
                c0 = db * DBW
                sA, sG1, sG2 = wr.take(), wr.take(), wr.take()
                wA, wG1, wG2 = wbufs[sA], wbufs[sG1], wbufs[sG2]
                tA = [k.dma("pool", wA[:, 0:AK, :], T["wao_pk"][db], deps=[wr.free[sA], PK["A"]], sem=wr.sems[sA]),
                      k.dma("pool", wA[:, AK:2 * AK, :], T["wf_pk"][db], deps=[wr.free[sA], PK["A"]], sem=wr.sems[sA])]
                tG1 = k.dma("pool", wG1[:], T["wg_pk"][db], deps=[wr.free[sG1], PK["A"]], sem=wr.sems[sG1])
                tG2 = k.dma("pool", wG2[:], T["wg_pk"][NDB + db], deps=[wr.free[sG2], PK["A"]], sem=wr.sems[sG2])
                rdA, rdG1, rdG2 = [], [], []
                for j, (off, n) in [(j, sb) for j in range(DBW // 128) for sb in subs]:
                    tt0 = tt0_ + off
                    dc = db * (DBW // 128) + j
                    sl = di % 2
                    di += 1
                    b0 = 4 * sl
                    pA, pF, pGa, pGf = banks[b0], banks[b0 + 1], banks[b0 + 2], banks[b0 + 3]
                    js = slice(j * 128, (j + 1) * 128)
                    gA = mm_group(k, pA[:, :n], lambda kc, wA=wA, js=js: wA[:, kc, js], lambda kc, n=n, off=off: onTt[:, kc, off:off + n], AK,
                                  [tA, tin, bank_free[b0]])
                    gF = mm_group(k, pF[:, :n], lambda kc, wA=wA, js=js: wA[:, AK + kc, js], lambda kc, n=n, off=off: yfTt[:, kc, off:off + n], AK,
                                  [tA, tin, bank_free[b0 + 1]])
                    gGa = mm_group(k, pGa[:, :n], lambda kc, wG1=wG1, js=js: wG1[:, kc, js], lambda kc, n=n, off=off: hTt[:, kc, off:off + n], KC,
                                   [tG1, tin, bank_free[b0 + 2]])
                    gGf = mm_group(k, pGf[:, :n], lambda kc, wG2=wG2, js=js: wG2[:, kc, js], lambda kc, n=n, off=off: hTt[:, kc, off:off + n], KC,
                                   [tG2, tin, bank_free[b0 + 3]])
                    rdA += [gA, gF]; rdG1.append(gGa); rdG2.append(gGf)
                    tile_reads += [gF, gGf]
                    sga, sgf = sg[sl]
                    t1, t2 = tt12[sl]
                    a1 = k.op("act", lambda e, pGa=pGa, sga=sga, dc=dc, n=n: e.activation(sga[:, :n], pGa[:, :n], AF.Sigmoid,
                                                                                         bias=bg[:, dc:dc + 1], scale=1.0),
                              deps=[gGa, t_bg, slot_free[sl]])
                    a2 = k.op("act", lambda e, pGf=pGf, sgf=sgf, dc=dc, n=n: e.activation(sgf[:, :n], pGf[:, :n], AF.Sigmoid,
                                                                                         bias=bg[:, D // 128 + dc:D // 128 + dc + 1], scale=1.0),
                              deps=[gGf])
                    d1 = k.op("dve", lambda e, pA=pA, sga=sga, t1=t1, n=n: e.tensor_tensor(t1[:, :n], pA[:, :n], sga[:, :n], ALU.mult),
                              deps=[gA, a1])
                    d2 = k.op("dve", lambda e, pF=pF, sgf=sgf, t2=t2, n=n: e.tensor_tensor(t2[:, :n], pF[:, :n], sgf[:, :n], ALU.mult),
                              deps=[gF, a2])
                    d3 = k.op("dve", lambda e, sl=sl, t1=t1, t2=t2, n=n: e.tensor_tensor(zst[sl][:, :n], t1[:, :n], t2[:, :n], ALU.add),
                              deps=[d1, d2])
                    bank_free[b0] = [d1]; bank_free[b0 + 1] = [d2]; bank_free[b0 + 2] = [a1]; bank_free[b0 + 3] = [a2]
                    tz = k.dma("sp", zT_v[:, dc, tt0:tt0 + n], zst[sl][:, :n], deps=[d3], sem=zsems[sl])
                    slot_free[sl] = [tz]
                wr.free[sA] = rdA; wr.free[sG1] = rdG1; wr.free[sG2] = rdG2


def phase_tm_matmul(k, cfg, T, name, aT_dram, nk, ntok, w_pk, out_dram, TB=512, nslots=3, CWB=256, wdep=None):
    D = cfg.D
    NCB = D // CWB
    blocks = [(i * TB, min(TB, ntok - i * TB)) for i in range((ntok + TB - 1) // TB)]
    TBW = TB
    if len(blocks) >= 2 and blocks[-1][1] <= 8:
        tl_ = blocks.pop()
        blocks[-1] = (blocks[-1][0], blocks[-1][1] + tl_[1])
        TBW = TB + tl_[1]
    with k.phase(name) as ph:
        abufs = [ph.sbuf("a", [128, nk, TBW], BF16) for _ in range(2 if nk * TB * 2 <= 40 * 1024 else 1)]
        ar = Ring(k, abufs, "a")
        wbufs = [ph.sbuf("w", [128, nk, CWB], BF16) for _ in range(nslots)]
        wr = Ring(k, wbufs, "w")
        msts = [ph.sbuf("mst", [128, CWB], F32) for _ in range(2)]
        mst_free = [[], []]
        msems = [k.dsem("m") for _ in range(2)]
        banks = [ph.psum("ps", [128, 512], F32) for _ in range(8)]
        bank_free = [[] for _ in banks]
        a_v = aT_dram.rearrange("(kc p) s -> p kc s", p=128)
        bi = mi = 0
        for (tb0, bn) in blocks:
            as_ = ar.take()
            At = abufs[as_]
            ta = k.dma("sp", At[:, :, :bn], a_v[:, :, tb0:tb0 + bn], deps=ar.free[as_], sem=ar.sems[as_])
            areads = []
            for cb in range(NCB):
                ws = wr.take()
                Wt = wbufs[ws]
                tw = k.dma("pool", Wt[:], w_pk[cb], deps=[wr.free[ws], wdep], sem=wr.sems[ws])
                wreads = []
                for sub in range((bn + 127) // 128):
                    m = min(128, bn - sub * 128)
                    b = bi % 8
                    bi += 1
                    ps = banks[b]
                    tg = mm_group(k, ps[:m, :CWB], lambda kc, At=At, sub=sub, m=m: At[:, kc, sub * 128:sub * 128 + m],
                                  lambda kc, Wt=Wt: Wt[:, kc, :], nk, [ta, tw, bank_free[b]])
                    wreads.append(tg)
                    ms = mi % 2
                    mi += 1
                    te = k.op("act", lambda e, ps=ps, ms=ms, m=m: e.activation(msts[ms][:m, :], ps[:m, :CWB], AF.Copy), deps=[tg, mst_free[ms]])
                    bank_free[b] = [te]
                    r0 = tb0 + sub * 128
                    mst_free[ms] = [k.dma("sp", out_dram[r0:r0 + m, cb * CWB:(cb + 1) * CWB], msts[ms][:m, :], deps=[te], sem=msems[ms])]
                wr.free[ws] = wreads
                areads += wreads
            ar.free[as_] = areads


def phase_ffn_up(k, cfg, T, PK):
    D, KC, TL, FC, DFF = cfg.D, cfg.KC, cfg.TL, cfg.FC, cfg.DFF
    NHALF = 2
    HL = TL // NHALF
    L = HL + 2
    atiles = [(i * 512, min(512, L - i * 512)) for i in range((L + 511) // 512)]
    with k.phase("p5") as ph:
        h1Th = ph.sbuf("h1Th", [128, KC, L], BF16)
        wbufs = [ph.sbuf("w", [128, KC, 256], BF16) for _ in range(4)]
        wr = Ring(k, wbufs, "w")
        cw = ph.sbuf("cw", [128, 3, 2 * FC], F32)
        cb = ph.sbuf("cb", [128, 2 * FC], F32)
        cbuf = [[ph.sbuf("c", [128, L], F32) for _ in range(2)] for _ in range(2)]
        sgt = [ph.sbuf("sgt", [128, HL], F32) for _ in range(2)]
        pst = [ph.sbuf("pst", [128, HL], BF16) for _ in range(2)]
        banks = [ph.psum("ps", [128, 512], F32) for _ in range(8)]
        bank_free = [[] for _ in banks]
        cs = k.dsem("c")
        tcw = [k.dma("sp", cw[:], T["convw_pm"], sem=cs), k.dma("sp", cb[:], T["convb_pm"], sem=cs)]
        hsem = k.dsem("h")
        psems = [k.dsem("p") for _ in range(2)]
        slot_free = [[], []]
        h1T_v = T["h1T"].rearrange("(kc p) s -> p kc s", p=128)
        w_v = T["w_up"].rearrange("(kc p) n -> p kc n", p=128)
        pT_v = T["pT"].rearrange("(c p) s -> p c s", p=128)
        half_reads = []
        bi = 0
        ji = 0
        sc_ = k.dsem("pkC", barrier=False)
        wd_v = T["w_down"].rearrange("(kc p) n -> p kc n", p=128)
        wd_blocks = list(range(D // 256))
        for hf in range(NHALF):
            if hf == 0:
                pieces = [(0, TL, 1), (1, 0, HL + 1)]
            else:
                pieces = [(0, HL * hf - 1, HL + 1), (HL + 1, TL + 1, 1)]
            if NHALF > 2:
                raise NotImplementedError
            th = [k.dma("sp", h1Th[:, :, d0:d0 + n], h1T_v[:, :, s0:s0 + n], deps=half_reads, sem=hsem, slow=(n == 1))
                  for (d0, s0, n) in pieces]
            half_reads = []
            for jp in range(FC // 2):
                sg_, sv_ = wr.take(), wr.take()
                Wg, Wv = wbufs[sg_], wbufs[sv_]
                tg_ = k.dma("pool", Wg[:], w_v[:, :, jp * 256:(jp + 1) * 256], deps=wr.free[sg_], sem=wr.sems[sg_])
                tv_ = k.dma("pool", Wv[:], w_v[:, :, DFF + jp * 256:DFF + (jp + 1) * 256], deps=wr.free[sv_], sem=wr.sems[sv_])
                if wd_blocks and (hf * (FC // 2) + jp) % 5 == 4:
                    wb = wd_blocks.pop(0)
                    k.dma("pool", T["wdn_pk"][wb], wd_v[:, :, wb * 256:(wb + 1) * 256], sem=sc_)
                rdg, rdv = [], []
                for j in range(2):
                    jj = jp * 2 + j
                    sl = ji % 2
                    ji += 1
                    js = slice(j * 128, (j + 1) * 128)
                    lastc = []
                    for ci, (Wt, tw, rd, cidx) in enumerate(((Wg, tg_, rdg, jj), (Wv, tv_, rdv, FC + jj))):
                        cbf = cbuf[sl][ci]
                        tl_ = []
                        for (a0, n) in atiles:
                            b = bi % 8
                            bi += 1
                            ps = banks[b]
                            tg = mm_group(k, ps[:, :n], lambda kc, Wt=Wt, js=js: Wt[:, kc, js], lambda kc, a0=a0, n=n: h1Th[:, kc, a0:a0 + n], KC,
                                          [tw, th, bank_free[b]])
                            rd.append(tg)
                            tl_.append((a0, n, b, ps, tg))
                        half_reads.append(tl_[-1][4])
                        inits = []
                        for (a0, n, b, ps, tg) in tl_:
                            lo1, hi1 = max(a0, 1), min(a0 + n, L - 1)
                            if hi1 > lo1:
                                inits.append(k.op("act", lambda e, ps=ps, cbf=cbf, a0=a0, lo1=lo1, hi1=hi1, cidx=cidx: e.activation(
                                    cbf[:, lo1:hi1], ps[:, lo1 - a0:hi1 - a0], AF.Identity, bias=cb[:, cidx:cidx + 1], scale=cw[:, 1, cidx:cidx + 1]),
                                    deps=[tg, tcw, slot_free[sl]]))
                            else:
                                inits.append(None)
                        for ti_, (a0, n, b, ps, tg) in enumerate(tl_):
                            last = inits[ti_]
                            e0 = min(a0 + n, L - 2)
                            if e0 > a0:
                                last = k.op("dve", lambda e, ps=ps, cbf=cbf, a0=a0, e0=e0, cidx=cidx: e.scalar_tensor_tensor(
                                    cbf[:, a0 + 1:e0 + 1], ps[:, 0:e0 - a0], cw[:, 0, cidx:cidx + 1], cbf[:, a0 + 1:e0 + 1], ALU.mult, ALU.add),
                                    deps=[tg, inits, last])
                            s0 = max(a0, 2)
                            if a0 + n > s0:
                                last = k.op("dve", lambda e, ps=ps, cbf=cbf, a0=a0, n=n, s0=s0, cidx=cidx: e.scalar_tensor_tensor(
                                    cbf[:, s0 - 1:a0 + n - 1], ps[:, s0 - a0:n], cw[:, 2, cidx:cidx + 1], cbf[:, s0 - 1:a0 + n - 1], ALU.mult, ALU.add),
                                    deps=[tg, inits, last])
                            bank_free[b] = [last, inits[ti_]]
                            lastc.append(last)
                    cg, cv = cbuf[sl]
                    t_s = k.op("act", lambda e, cg=cg, sl=sl: e.activation(sgt[sl][:, :], cg[:, 1:L - 1], AF.Silu), deps=[lastc])
                    t_p = k.op("dve", lambda e, cv=cv, sl=sl: e.tensor_tensor(pst[sl][:, :], sgt[sl][:, :], cv[:, 1:L - 1], ALU.mult), deps=[t_s, lastc])
                    slot_free[sl] = [k.dma("sp", pT_v[:, jj, hf * HL:(hf + 1) * HL], pst[sl][:, :], deps=[t_p], sem=psems[sl]), t_p]
                wr.free[sg_] = rdg
                wr.free[sv_] = rdv
        for wb in wd_blocks:
            k.dma("pool", T["wdn_pk"][wb], wd_v[:, :, wb * 256:(wb + 1) * 256], sem=sc_)
        PK["C"] = Tok(sc_, sc_.count)


def build(cfg):
    nc = bass.Bass("TRN2", target_bir_lowering=False)
    D, S, AQ, NLOC, NTOK, DFF, TL = cfg.D, cfg.S, cfg.AQ, cfg.NLOC, cfg.NTOK, cfg.DFF, cfg.TL
    T = {}

    def inp(name, shape, dt=F32):
        T[name] = nc.dram_tensor(name, list(shape), dt, kind="ExternalInput").ap()

    def scr(name, shape, dt):
        kind = "ExternalOutput" if name in cfg.debug else "Internal"
        T[name] = nc.dram_tensor(name, list(shape), dt, kind=kind).ap()

    inp("xcat", [NTOK, D])
    inp("lng_rep", [128, D]); inp("lnb_rep", [128, D])
    inp("ident", [128, 128], BF16)
    scr("hT", [D, NTOK], BF16)
    scr("hloc", [NLOC, D], F32)
    T["out"] = nc.dram_tensor("out", [TL, D], F32, kind="ExternalOutput").ap()

    NH = cfg.NH
    inp("w_in", [D, 4 * AQ])
    inp("perm32", [128, 128], BF16)
    inp("ropek_cos", [32, S]); inp("ropek_sin", [32, S])
    inp("ropeq_cos", [32, NLOC]); inp("ropeq_sin", [32, NLOC])
    inp("csg", [256, 512], BF16)
    scr("kT", [2 * NH, 128, S], BF16)
    scr("v", [S, AQ], BF16)
    scr("z", [NH, 2, 128, S // 128, 256], BF16)
    scr("qT", [2 * NH, 128, NLOC], BF16)
    CW = min(512, AQ)
    inp("subg_rep", [128, 256]); inp("lam_rep", [128, 4, 128])
    scr("onT", [AQ, NLOC], BF16)
    NST = TL // 256 + 1
    inp("dftc", [NST, 128, S // 128, 256], BF16); inp("dfts", [NST, 128, S // 128, 256], BF16)
    scr("yfT", [AQ, NLOC], BF16)
    inp("w_attn_o", [AQ, D]); inp("w_fourier", [AQ, D]); inp("w_gate", [D, 2 * D]); inp("w_mix", [D, D])
    inp("bgate_pm", [128, 2 * D // 128])
    inp("ln1g_rep", [128, D]); inp("ln1b_rep", [128, D]); inp("hmask", [128, 1])
    scr("zT", [D, NLOC], BF16)
    scr("msc", [NLOC, D], F32)
    scr("h1", [NLOC, D], F32)
    scr("h1T", [D, NLOC], BF16)
    inp("w_up", [D, 2 * DFF]); inp("w_down", [DFF, D])
    inp("convw_pm", [128, 3, 2 * DFF // 128]); inp("convb_pm", [128, 2 * DFF // 128])
    inp("ln2g_rep", [128, D]); inp("ln2b_rep", [128, D])
    scr("pT", [DFF, TL], BF16)
    scr("wao_pk", [D // 256, 128, AQ // 128, 256], BF16); scr("wf_pk", [D // 256, 128, AQ // 128, 256], BF16)
    scr("wg_pk", [2 * D // 256, 128, D // 128, 256], BF16)
    scr("wmix_pk", [D // min(512, D), 128, D // 128, min(512, D)], BF16)
    scr("wdn_pk", [D // 256, 128, DFF // 128, 256], BF16)
    scr("fsc", [TL, D], F32)

    with ExitStack() as es:
        k = KB(nc, es)
        phase_ln(k, cfg, T, "p0", T["xcat"], NTOK, T["lng_rep"], T["lnb_rep"], out_f=(T["hloc"], S, 0), out_T=T["hT"], group_break=S)
        if cfg.upto >= 1:
            jobs = [dict(kind="rope", w=T["w_in"], c0=AQ, nblk=AQ // CW, out=T["kT"]),
                    dict(kind="tm", w=T["w_in"], c0=2 * AQ, nblk=AQ // CW, out=T["v"]),
                    dict(kind="fmf", w=T["w_in"], c0=3 * AQ, nblk=AQ // CW, out=None)]
            sel = os.environ.get("P1SEL")
            if sel:
                jobs = [j for j in jobs if j["kind"] in sel.split(",")]
            phase_proj(k, cfg, T, "p1", 0, S, 2, jobs, T["ropek_cos"], T["ropek_sin"])
        if cfg.upto >= 2:
            jobs = [dict(kind="rope", w=T["w_in"], c0=0, nblk=AQ // CW, out=T["qT"])]
            phase_proj(k, cfg, T, "p2", S, NLOC, 2, jobs, T["ropeq_cos"], T["ropeq_sin"])
        PK = {}
        if cfg.upto >= 3:
            phase_attn(k, cfg, T, PK)
        if cfg.upto >= 4:
            phase_fourier(k, cfg, T)
        if cfg.upto >= 5:
            phase_gate(k, cfg, T, PK)
        if cfg.upto >= 6:
            phase_tm_matmul(k, cfg, T, "p4b", T["zT"], cfg.KC, NLOC, T["wmix_pk"], T["msc"], nslots=3, CWB=min(512, D), wdep=PK["B"])
            phase_ln(k, cfg, T, "p4c", T["msc"], NLOC, T["ln1g_rep"], T["ln1b_rep"], res=T["hloc"], out_f=(T["h1"], 0, 0),
                     out_T=T["h1T"], mask_in=T["hmask"])
        if cfg.upto >= 7:
            phase_ffn_up(k, cfg, T, PK)
        if cfg.upto >= 8:
            phase_tm_matmul(k, cfg, T, "p6", T["pT"], cfg.FC, TL, T["wdn_pk"], T["fsc"], TB=512, nslots=2, wdep=PK["C"])
        if cfg.upto >= 9:
            phase_ln(k, cfg, T, "p7", T["fsc"], TL, T["ln2g_rep"], T["ln2b_rep"], res=T["h1"], out_f=(T["out"], 0, 0))
        with k.phase("fin"):
            k.wait_only("sp", [])
    return nc


def host_inputs(cfg, inputs):
    D, S, TL, NLOC = cfg.D, cfg.S, cfg.TL, cfg.NLOC
    x = np.asarray(inputs["x"], dtype=np.float32)
    maps = []
    common = {
        "lng_rep": np.ascontiguousarray(np.broadcast_to(np.asarray(inputs["ln_emb_g"], np.float32)[None, :], (128, D))),
        "lnb_rep": np.ascontiguousarray(np.broadcast_to(np.asarray(inputs["ln_emb_b"], np.float32)[None, :], (128, D))),
        "ident": np.eye(128, dtype=np.float32).astype(ml_dtypes.bfloat16),
    }
    bf = ml_dtypes.bfloat16
    f32c = lambda a: np.ascontiguousarray(np.asarray(a, np.float32))
    common["w_in"] = f32c(inputs["w_in"][0])
    perm = np.zeros((128, 128), np.float32)
    for j in range(16):
        perm[j + 16, j] = 1.0
        perm[j, j + 16] = 1.0
    common["perm32"] = perm.astype(bf)
    inv_freq = (np.float32(ROPE_THETA) ** (-np.arange(0, 32, 2, dtype=np.float32) / np.float32(32))).astype(np.float32)

    def rope_tabs(pos):
        ang = (pos.astype(np.float32)[None, :] * inv_freq[:, None]).astype(np.float32).astype(np.float64)
        cos = np.concatenate([np.cos(ang), np.cos(ang)], 0)
        sin = np.concatenate([-np.sin(ang), np.sin(ang)], 0)
        return f32c(cos), f32c(sin)

    common["ropek_cos"], common["ropek_sin"] = rope_tabs(np.arange(S))
    cg = np.arange(256)[:, None] * np.arange(256)[None, :] % 256
    ang = 2.0 * np.pi * cg / 256.0
    common["csg"] = np.concatenate([np.cos(ang), -np.sin(ang)], 1).astype(np.float32).astype(bf)
    rep = lambda v: np.ascontiguousarray(np.broadcast_to(np.asarray(v, np.float32)[None], (128,) + np.asarray(v).shape))
    common["subg_rep"] = rep(inputs["subln_g"][0])
    common["lam_rep"] = rep(np.stack([inputs["lambda_q1"][0], inputs["lambda_k1"][0], inputs["lambda_q2"][0], inputs["lambda_k2"][0]], 0))
    pm = lambda v: np.ascontiguousarray(np.asarray(v, np.float32).reshape(-1, 128).T)
    common["w_attn_o"] = f32c(inputs["w_attn_o"][0]); common["w_fourier"] = f32c(inputs["w_fourier"][0])
    common["w_gate"] = f32c(inputs["w_gate"][0]); common["w_mix"] = f32c(inputs["w_mix_out"][0])
    common["bgate_pm"] = pm(inputs["b_gate"][0])
    common["ln1g_rep"] = rep(inputs["ln1_g"][0]); common["ln1b_rep"] = rep(inputs["ln1_b"][0])
    common["ln2g_rep"] = rep(inputs["ln2_g"][0]); common["ln2b_rep"] = rep(inputs["ln2_b"][0])
    common["w_up"] = f32c(inputs["w_up"][0]); common["w_down"] = f32c(inputs["w_down"][0])
    cwv = np.asarray(inputs["conv_w"][0], np.float32)
    common["convw_pm"] = np.ascontiguousarray(cwv.reshape(3, -1, 128).transpose(2, 0, 1))
    common["convb_pm"] = pm(inputs["conv_b"][0])
    for c in range(N_CORES):
        b, qi = c // 4, c % 4
        t0 = qi * TL
        iL = t0 - 1 if t0 - 1 >= 0 else t0
        iR = t0 + TL if t0 + TL < S else t0 + TL - 1
        xcat = np.concatenate([x[b], x[b, t0:t0 + TL], x[b, iL:iL + 1], x[b, iR:iR + 1]], axis=0)
        m = dict(common)
        m["xcat"] = np.ascontiguousarray(xcat)
        lpos = np.concatenate([np.arange(t0, t0 + TL), [iL, iR]])
        m["ropeq_cos"], m["ropeq_sin"] = rope_tabs(lpos)
        hm = np.zeros((128, 1), np.float32)
        hm[0, 0] = 1.0 if t0 - 1 >= 0 else 0.0
        hm[1, 0] = 1.0 if t0 + TL < S else 0.0
        m["hmask"] = hm
        NST = TL // 256 + 1
        lp = np.zeros(NST * 256, np.int64)
        lp[:NLOC] = lpos
        prod = (np.arange(S, dtype=np.int64)[:, None] * lp[None, :]) % S
        ang = prod.astype(np.float64) * (2.0 * np.pi / S)
        lay = lambda a: np.ascontiguousarray(a.astype(np.float32).astype(bf).reshape(S // 128, 128, NST, 256).transpose(2, 1, 0, 3))
        m["dftc"] = lay(np.cos(ang))
        m["dfts"] = lay(np.sin(ang))
        maps.append(m)
    return maps


def kernel(**inputs):
    cfg = Cfg()
    nc = build(cfg)
    maps = host_inputs(cfg, inputs)
    res = run_bass_kernel_spmd(nc, maps, core_ids=list(range(N_CORES)))
    out = np.empty((2, cfg.S, cfg.D), np.float32)
    for c in range(N_CORES):
        b, qi = c // 4, c % 4
        out[b, qi * cfg.TL:(qi + 1) * cfg.TL] = res.results[c]["out"]
    return out
```
